# Optimizing a Trainium2 kernel written in Bass

```python
import math
import jax, jax.numpy as jnp
from jax import lax
import numpy as np

D_MODEL = 1024
BATCH = 2
SEQ = 8192
DEPTH = 4
DEC_BATCH = 32
DEC_SEQ = 32
PAST_LEN = 2048

CHUNK = 64
N_META = 16
Q_BLOCK = 128
HEAD_DIM = 64
DIFF_HEADS = 4
FOX_HEADS = 8
DIFF_QK = DIFF_HEADS * 2 * HEAD_DIM
DIFF_V = DIFF_HEADS * 2 * HEAD_DIM
FOX_W = FOX_HEADS * HEAD_DIM
MIX = DIFF_V + FOX_W
SPLITS = (DIFF_QK, DIFF_QK, DIFF_V, FOX_W, FOX_W, FOX_W, FOX_HEADS, MIX)
IN_COLS = sum(SPLITS)
NORM_EPS = 1e-6
SUBLN_EPS = 1e-5
NEG = -1e30

kernel_name = "hymba_diff_fox_streaming_encoder"


def _rmsnorm(x, g, eps=NORM_EPS):
    xf = x.astype(jnp.float32)
    y = xf * lax.rsqrt(jnp.mean(xf * xf, axis=-1, keepdims=True) + eps)
    return (y * g.astype(jnp.float32)).astype(x.dtype)


def _chunk_id(idx):
    return jnp.where(idx < N_META, -1, (idx - N_META) // CHUNK)


def _project(h, w_in_l, b_f_l):
    B, T, _ = h.shape
    z = h @ w_in_l
    cuts = np.cumsum(SPLITS)[:-1].tolist()
    dq, dk, dv, fq, fk, fv, fl, gate = jnp.split(z, cuts, axis=-1)
    dq = dq.reshape(B, T, DIFF_HEADS, 2, HEAD_DIM)
    dk = dk.reshape(B, T, DIFF_HEADS, 2, HEAD_DIM)
    dv = dv.reshape(B, T, DIFF_HEADS, 2 * HEAD_DIM)
    fq = fq.reshape(B, T, FOX_HEADS, HEAD_DIM)
    fk = fk.reshape(B, T, FOX_HEADS, HEAD_DIM)
    fv = fv.reshape(B, T, FOX_HEADS, HEAD_DIM)
    logf = jax.nn.log_sigmoid((fl + b_f_l).astype(jnp.float32))
    return dq, dk, dv, fq, fk, fv, logf, gate


def _attend(qd, qf, qidx, Fq, Kd, Vd, Kf, Vf, kidx, Fk, lam, lam_init, subln_g):
    f32 = jnp.float32
    B, Tq = qd.shape[0], qd.shape[1]
    scale = HEAD_DIM ** -0.5
    slopes = 2.0 ** (-8.0 * jnp.arange(1, DIFF_HEADS + 1, dtype=f32) / DIFF_HEADS)
    dist = jnp.abs(qidx[:, None] - kidx[None, :]).astype(f32)
    sd = jnp.einsum('bqhme,bkhme->bhmqk', qd, Kd).astype(f32) * scale
    sd = sd - slopes[:, None, None, None] * dist
    chunk_ok = _chunk_id(kidx)[None, :] <= _chunk_id(qidx)[:, None]
    sd = jnp.where(chunk_ok, sd, NEG)
    pd = jax.nn.softmax(sd, axis=-1)
    ad = pd[:, :, 0] - lam * pd[:, :, 1]
    od = jnp.einsum('bhqk,bkhe->bqhe', ad.astype(Vd.dtype), Vd)
    od = _rmsnorm(od, subln_g, SUBLN_EPS) * (1.0 - lam_init)
    Fq_t = jnp.swapaxes(Fq, 1, 2)
    Fk_t = jnp.swapaxes(Fk, 1, 2)
    sf = jnp.einsum('bqhe,bkhe->bhqk', qf, Kf).astype(f32) * scale
    sf = sf + (Fq_t[:, :, :, None] - Fk_t[:, :, None, :])
    sf = jnp.where(kidx[None, :] <= qidx[:, None], sf, NEG)
    pf = jax.nn.softmax(sf, axis=-1)
    of = jnp.einsum('bhqk,bkhe->bqhe', pf.astype(Vf.dtype), Vf)
    return jnp.concatenate([od.reshape(B, Tq, DIFF_V), of.reshape(B, Tq, FOX_W)], axis=-1)


def _layer(x, prefix, n_lead, w_in_l, b_f_l, g_l, w_out_l, lam, lam_init, subln_g_l):
    B, T, _ = x.shape
    h = _rmsnorm(x, g_l)
    dq, dk, dv, fq, fk, fv, logf, gate = _project(h, w_in_l, b_f_l)
    if prefix is None:
        t_prev = 0
        Kd, Vd, Kf, Vf, LF = dk, dv, fk, fv, logf
    else:
        pdk, pdv, pfk, pfv, plf = prefix
        t_prev = pdk.shape[1]
        Kd = jnp.concatenate([pdk.astype(dk.dtype), dk], axis=1)
        Vd = jnp.concatenate([pdv.astype(dv.dtype), dv], axis=1)
        Kf = jnp.concatenate([pfk.astype(fk.dtype), fk], axis=1)
        Vf = jnp.concatenate([pfv.astype(fv.dtype), fv], axis=1)
        LF = jnp.concatenate([plf.astype(jnp.float32), logf], axis=1)
    F = jnp.cumsum(LF, axis=1)
    kidx = jnp.arange(t_prev + T)
    qidx = kidx[t_prev:]
    Fq = F[:, t_prev:]

    def attend_rows(a):
        q_d, q_f, q_i, F_q = a
        return _attend(q_d, q_f, q_i, F_q, Kd, Vd, Kf, Vf, kidx, F, lam, lam_init, subln_g_l)

    if n_lead >= T:
        out = attend_rows((dq, fq, qidx, Fq))
    else:
        lead = attend_rows((dq[:, :n_lead], fq[:, :n_lead], qidx[:n_lead], Fq[:, :n_lead]))
        nb = (T - n_lead) // Q_BLOCK

        def blk(a):
            return jnp.moveaxis(a.reshape(a.shape[0], nb, Q_BLOCK, *a.shape[2:]), 1, 0)

        rest = lax.map(attend_rows, (blk(dq[:, n_lead:]), blk(fq[:, n_lead:]),
                                     qidx[n_lead:].reshape(nb, Q_BLOCK), blk(Fq[:, n_lead:])))
        rest = jnp.moveaxis(rest, 0, 1).reshape(B, nb * Q_BLOCK, MIX)
        out = jnp.concatenate([lead, rest], axis=1)
    y = x + (out.astype(x.dtype) * jax.nn.silu(gate)) @ w_out_l
    return y, (dk, dv, fk, fv, logf)


def setup_inputs(seed: int = 0) -> dict:
    key = jax.random.key(seed)
    ks = jax.random.split(key, 20)
    L = N_META + PAST_LEN
    f32 = jnp.float32
    n = lambda k, s: jax.random.normal(k, s, f32)
    return {
        "x_prompt": n(ks[0], (BATCH, SEQ, D_MODEL)),
        "x_sample": n(ks[1], (DEC_BATCH, DEC_SEQ, D_MODEL)),
        "cache_diff_k": n(ks[2], (DEPTH, DEC_BATCH, L, DIFF_HEADS, 2, HEAD_DIM)),
        "cache_diff_v": n(ks[3], (DEPTH, DEC_BATCH, L, DIFF_HEADS, 2 * HEAD_DIM)),
        "cache_fox_k": n(ks[4], (DEPTH, DEC_BATCH, L, FOX_HEADS, HEAD_DIM)),
        "cache_fox_v": n(ks[5], (DEPTH, DEC_BATCH, L, FOX_HEADS, HEAD_DIM)),
        "cache_fox_logf": jax.nn.log_sigmoid(2.0 + 0.5 * n(ks[6], (DEPTH, DEC_BATCH, L, FOX_HEADS))),
        "meta_tokens": n(ks[7], (N_META, D_MODEL)),
        "w_in": n(ks[8], (DEPTH, D_MODEL, IN_COLS)) * D_MODEL ** -0.5,
        "b_forget": 2.0 + 0.1 * n(ks[9], (DEPTH, FOX_HEADS)),
        "norm_g": 1.0 + 0.01 * n(ks[10], (DEPTH, D_MODEL)),
        "w_out": n(ks[11], (DEPTH, MIX, D_MODEL)) * MIX ** -0.5,
        "lambda_q1": 0.1 * n(ks[12], (DEPTH, HEAD_DIM)),
        "lambda_k1": 0.1 * n(ks[13], (DEPTH, HEAD_DIM)),
        "lambda_q2": 0.1 * n(ks[14], (DEPTH, HEAD_DIM)),
        "lambda_k2": 0.1 * n(ks[15], (DEPTH, HEAD_DIM)),
        "subln_g": 1.0 + 0.01 * n(ks[16], (DEPTH, 2 * HEAD_DIM)),
        "final_norm_g": 1.0 + 0.01 * n(ks[17], (D_MODEL,)),
    }


def reference(x_prompt, x_sample, cache_diff_k, cache_diff_v, cache_fox_k, cache_fox_v, cache_fox_logf,
              meta_tokens, w_in, b_forget, norm_g, w_out, lambda_q1, lambda_k1, lambda_q2, lambda_k2,
              subln_g, final_norm_g):
    f32 = jnp.float32
    B = x_prompt.shape[0]
    meta = jnp.broadcast_to(meta_tokens[None].astype(x_prompt.dtype), (B, N_META, D_MODEL))
    hp = jnp.concatenate([meta, x_prompt], axis=1)
    hs = x_sample
    rows_p, rows_s = [], []
    for l in range(DEPTH):
        lam_init = 0.8 - 0.6 * math.exp(-0.3 * l)
        lam = (jnp.exp(jnp.sum(lambda_q1[l].astype(f32) * lambda_k1[l].astype(f32)))
               - jnp.exp(jnp.sum(lambda_q2[l].astype(f32) * lambda_k2[l].astype(f32))) + lam_init)
        hp, rp = _layer(hp, None, N_META, w_in[l], b_forget[l], norm_g[l], w_out[l], lam, lam_init, subln_g[l])
        prefix = (cache_diff_k[l], cache_diff_v[l], cache_fox_k[l], cache_fox_v[l], cache_fox_logf[l])
        hs, rs = _layer(hs, prefix, hs.shape[1], w_in[l], b_forget[l], norm_g[l], w_out[l], lam, lam_init, subln_g[l])
        rows_p.append(rp)
        rows_s.append(rs)
    y_prompt = _rmsnorm(hp[:, N_META:], final_norm_g)
    y_sample = _rmsnorm(hs, final_norm_g)
    p_diff_k = jnp.stack([r[0] for r in rows_p])
    p_diff_v = jnp.stack([r[1] for r in rows_p])
    p_fox_k = jnp.stack([r[2] for r in rows_p])
    p_fox_v = jnp.stack([r[3] for r in rows_p])
    p_fox_logf = jnp.stack([r[4] for r in rows_p])
    s_diff_k = jnp.stack([r[0] for r in rows_s])
    s_diff_v = jnp.stack([r[1] for r in rows_s])
    s_fox_k = jnp.stack([r[2] for r in rows_s])
    s_fox_v = jnp.stack([r[3] for r in rows_s])
    s_fox_logf = jnp.stack([r[4] for r in rows_s])
    return (y_prompt, y_sample, p_diff_k, p_diff_v, p_fox_k, p_fox_v, p_fox_logf,
            s_diff_k, s_diff_v, s_fox_k, s_fox_v, s_fox_logf)
```

```python
import math
import numpy as np
import ml_dtypes
import concourse.bass as bass
import concourse.mybir as mybir
from concourse.bass_utils import run_bass_kernel_spmd

F32 = mybir.dt.float32
BF16 = mybir.dt.bfloat16
AF = mybir.ActivationFunctionType
ALU = mybir.AluOpType
AX = mybir.AxisListType

D = 1024
NMETA = 16
HD = 64
NCOLS = 4104
NEGM = -30000.0
SLOPES = [2.0 ** (-8.0 * (h + 1) / 4) for h in range(4)]
ENGS = ("pe", "act", "dve", "pool", "sp")


class Prog:
    def __init__(self):
        self.ops = {e: [] for e in ENGS}
        self.cnt = {}
        self.rs = {}
        self.seen = {e: {} for e in ENGS}
        self.epoch = 0
        self.arena = {}
        self.ast = {}
        self.ncc = 0

    def region(self, name, arena=None, phase=None):
        if arena is not None:
            self.arena[name] = (arena, phase)

    def add(self, eng, fn, R=(), W=(), dma=None, cc=False):
        deps = {}

        def need(d):
            for k, i in d.items():
                if k[0] in ("dma", "cc"):
                    i = self.cnt[k]
                if deps.get(k, 0) < i:
                    deps[k] = i

        for r in R:
            st = self.rs.get(r)
            if st:
                need(st[0])
            ar = self.arena.get(r)
            if ar:
                for (a, p), d in self.ast.items():
                    if a == ar[0] and p != ar[1]:
                        need(d)
        for w in W:
            st = self.rs.get(w)
            if st:
                need(st[0])
                need(st[1])
            ar = self.arena.get(w)
            if ar:
                for (a, p), d in self.ast.items():
                    if a == ar[0] and p != ar[1]:
                        need(d)
        if cc:
            key = ("cc", self.epoch)
        elif dma:
            key = ("dma", dma)
        else:
            key = (eng, self.epoch)
        waits = []
        seen = self.seen[eng]
        for k, i in deps.items():
            if k == key and eng == "pe":
                continue
            if seen.get(k, 0) >= i:
                continue
            seen[k] = i
            waits.append((k, i))
        idx = self.cnt[key] = self.cnt.get(key, 0) + 1
        self.ops[eng].append((fn, waits, key))
        for r in R:
            st = self.rs.setdefault(r, ({}, {}))
            st[1][key] = idx
        for w in W:
            st = self.rs.setdefault(w, ({}, {}))
            st[0].clear()
            st[1].clear()
            st[0][key] = idx
        for x in tuple(R) + tuple(W):
            ar = self.arena.get(x)
            if ar:
                self.ast.setdefault(ar, {})[key] = idx

    def finish(self):
        waits = [(k, i) for k, i in self.cnt.items()]
        self.ops["sp"].append((None, waits, None))


def _cfg_from_shapes(seq, past_len, depth):
    assert seq % 512 == 0 and past_len % 64 == 0
    nblk = seq // 128
    nslot = nblk // 4
    import os
    cfg = dict(SEQ=seq, PAST=past_len, DEPTH=depth, NSLOT=nslot, LC=NMETA + past_len, GS=int(os.environ.get("K_GS", "4")))
    return cfg


def build_nc(cfg):
    NSLOT = cfg["NSLOT"]
    DEPTH = cfg["DEPTH"]
    LC = cfg["LC"]
    NTOKP = NSLOT * 128
    NLOC = NTOKP + NMETA
    NTOT = NLOC + 128
    LCN = LC + 32
    NCT = (LC + 127) // 128
    GS = min(int(cfg.get("GS", 4)), NSLOT)
    NQG = NSLOT // GS
    NB = 4 * NSLOT + 1
    FW = max(NLOC, LCN)
    KW = max(4 * NLOC, 8 * LCN // 2 + 8)
    VU = max(NB, (NCT + 1) * 4 // 2 + 2)

    nc = bass.Bass("TRN2", target_bir_lowering=False)
    P = Prog()

    def din(name, shape, dt=F32):
        return nc.dram_tensor(name, list(shape), dt, kind="ExternalInput").ap()

    def dout(name, shape, dt=F32):
        return nc.dram_tensor(name, list(shape), dt, kind="ExternalOutput").ap()

    def dscr(name, shape, dt):
        return nc.dram_tensor(name, list(shape), dt).ap()

    xp = din("xp", [NTOKP, D]); xm = din("xm", [NMETA, D]); xs = din("xs", [128, D])
    cdk = din("cdk", [DEPTH, 4, LC, 512]); cdv = din("cdv", [DEPTH, 4, LC, 512])
    cfk = din("cfk", [DEPTH, 4, LC, 512]); cfv = din("cfv", [DEPTH, 4, LC, 512])
    clf = din("clf", [DEPTH, 4, LC, 8])
    w_in = din("w_in", [DEPTH, D, NCOLS]); w_out = din("w_out", [DEPTH, D, D])
    b_forget = din("b_forget", [DEPTH, 8]); norm_g = din("norm_g", [DEPTH, D])
    lq1 = din("lq1", [DEPTH, 64]); lk1 = din("lk1", [DEPTH, 64]); lq2 = din("lq2", [DEPTH, 64]); lk2 = din("lk2", [DEPTH, 64])
    subln_g = din("subln_g", [DEPTH, 128]); fng = din("fng", [1, D])
    c_ident = din("c_ident", [128, 128], BF16)
    c_f32 = din("c_f32", [128, 3 * 128 + 8])
    c_mkfox = din("c_mkfox", [128, 4 * 128], BF16)
    c_mkdiff = din("c_mkdiff", [128, 16 * 128], BF16)
    c_mksm = din("c_mksm", [32, 5 * 16 + 5 * 32], BF16)
    c_qauxd = din("c_qauxd", [4, 6, NTOT], BF16)
    c_kauxd = din("c_kauxd", [4, 6, 4 * NLOC], BF16)
    c_kauxds = din("c_kauxds", [4, 6, LCN], BF16)
    onesb = din("c_ones", [8, 3, FW], BF16)

    yp = dout("yp", [NTOKP, D]); ys = dout("ys", [128, D])
    o_dk = dout("o_dk", [DEPTH, NTOT, 512]); o_dv = dout("o_dv", [DEPTH, NTOT, 512])
    o_fk = dout("o_fk", [DEPTH, NTOT, 512]); o_fv = dout("o_fv", [DEPTH, NTOT, 512])
    o_lf = dout("o_lf", [DEPTH, NTOT, 8])

    xscr = dscr("xscr", [NTOT, D], F32)
    qT = dscr("qT", [1024, NTOT], BF16)
    kTs = dscr("kTs", [1024, 128], BF16)
    vsn = dscr("vsn", [128, 8 * 129], BF16)
    agin_k = [dscr(f"agin_k{i}", [128, NLOC], BF16) for i in range(8)]
    agin_v = [dscr(f"agin_v{i}", [NLOC, 129], BF16) for i in range(8)]
    agin_lf = dscr("agin_lf", [8, NLOC], F32)
    agout_k = [[dscr(f"agout_k{i}_{j}", [4 * 128, NLOC], BF16) for j in range(8)] for i in range(2)]
    agout_v = [[dscr(f"agout_v{i}_{j}", [4 * NLOC, 129], BF16) for j in range(8)] for i in range(2)]
    agout_lf = [dscr(f"agout_lf{i}", [32, NLOC], F32) for i in range(2)]
    kauxf = dscr("kauxf", [8, 6, 4 * NLOC], BF16)
    qauxf = dscr("qauxf", [8, 6, NTOT], BF16)
    kauxfs = dscr("kauxfs", [4, 8, 6, LCN], BF16)

    def _sz(shape_free_elems, dt):
        nb = shape_free_elems * (2 if dt == BF16 else 4)
        return ((nb + 3) // 4 + 7) // 8 * 8
    sz_proj = 3 * _sz(8 * 512, BF16) + _sz(8 * 1024, BF16) + 2 * _sz(1024, F32) + 3 * _sz(512, F32) + 2 * _sz(512, BF16) + 2 * _sz(4 * 129, BF16) + 2 * _sz(1024, BF16)
    sz_F = 3 * _sz(FW, F32) + _sz(3 * FW, BF16) + 2 * _sz(NTOT, F32) + _sz(3 * NTOT, BF16) + _sz(NCT * 32, F32)
    sz_att = _sz(2 * KW, BF16) + _sz(2 * VU * 129, BF16) + 2 * _sz(NLOC, BF16)
    BIG_F32 = max(sz_proj, sz_F, sz_att) + 8
    from contextlib import ExitStack
    es = ExitStack()

    def sb(name, shape, dt):
        return es.enter_context(nc.sbuf_tensor(name, list(shape), dt))

    def ps(name, shape, dt=F32):
        return es.enter_context(nc.psum_tensor(name, list(shape), dt))

    arH = sb("arH", [128, max(_sz(8 * NTOT, BF16), 10 * _sz(512, F32) + 2 * _sz(512, BF16)) + 8], F32)
    arB = sb("arB", [128, BIG_F32], F32)
    Gt = sb("Gt", [128, 8, NTOT], BF16)
    Pt = [sb(f"Pt{i}", [128, 512], BF16) for i in range(4)]
    ident = sb("ident", [128, 128], BF16)
    cf32 = sb("cf32", [128, 3 * 128 + 8], F32)
    identf = cf32[:, 0:128]; sel63 = cf32[:, 128:256]; onesdiv = cf32[:, 256:384]; selmat = cf32[:, 384:392]
    ones_f = sb("ones_f", [128, 128], F32)
    zer_b = sb("zer_b", [1, 640], BF16)
    mkfox = sb("mkfox", [128, 4, 128], BF16)
    mkdiff = sb("mkdiff", [128, 16, 128], BF16)
    mksm = sb("mksm", [32, 240], BF16)
    gbc = sb("gbc", [128, D], F32)
    bfbc = sb("bfbc", [128, 8], F32)
    nbcol = sb("nbcol", [8, 1], F32)
    lamt = sb("lamt", [128, 4 * 64], F32)
    lamw = sb("lamw", [128, 16], F32)
    gsub = sb("gsub", [128, 1], F32)
    sml = sb("sml", [128, 16], F32)
    lfT = sb("lfT", [8, NTOT], F32)

    PS = [ps(f"psb{i}", [128, 512], F32) for i in range(8)]

    class Carver:
        def __init__(self, arena):
            self.a = arena
            self.off = 0

        def take(self, shape, dt):
            n = 1
            for s in shape[1:]:
                n *= s
            nb = n * (2 if dt == BF16 else 4)
            nf = (nb + 3) // 4
            nf = (nf + 7) // 8 * 8
            v = self.a[0:shape[0], self.off:self.off + nf]
            self.off += nf
            if dt == BF16:
                v = v.bitcast(BF16)[:, 0:n]
            else:
                v = v[:, 0:n]
            assert self.off <= self.a.shape[1], (self.off, self.a.shape)
            return v

    def v3(ap, a, b):
        return ap.rearrange("p (a b) -> p a b", a=a, b=b)

    cH0 = Carver(arH)
    hT = v3(cH0.take([128, 8 * NTOT], BF16), 8, NTOT)
    cH1 = Carver(arH)
    fin = {n: cH1.take([128, 512], F32) for n in ("lrow", "rb", "tmp", "od1", "od2", "od", "sq", "t")}
    cstf = [cH1.take([128, 512], F32) for _ in range(2)]
    cstb = [cH1.take([128, 512], BF16) for _ in range(2)]
    cB0 = Carver(arB)
    wbuf = [v3(cB0.take([128, 8 * 512], BF16), 8, 512) for _ in range(3)]
    wout = v3(cB0.take([128, 8 * 1024], BF16), 8, 1024)
    xt = [cB0.take([128, 1024], F32) for _ in range(2)]
    stf = [cB0.take([128, 512], F32) for _ in range(3)]
    stb = [cB0.take([128, 512], BF16) for _ in range(2)]
    vst = [cB0.take([128, 4 * 129], BF16) for _ in range(2)]
    hrow = cB0.take([128, 1024], BF16)
    junk = cB0.take([128, 1024], BF16)
    cB1 = Carver(arB)
    Fa = cB1.take([128, FW], F32); Fb = cB1.take([128, FW], F32); Fc = cB1.take([128, FW], F32)
    Fp = v3(cB1.take([128, 3 * FW], BF16), 3, FW)
    Fq = cB1.take([8, NTOT], F32); Fqr = cB1.take([8, NTOT], F32)
    Fqp = v3(cB1.take([8, 3 * NTOT], BF16), 3, NTOT)
    clfs = cB1.take([128, NCT * 32], F32)
    cB2 = Carver(arB)
    Kall = cB2.take([70, 2 * KW], BF16)
    Kt = [Kall[:, i * KW:(i + 1) * KW] for i in range(2)]
    Vall = cB2.take([128, 2 * VU * 129], BF16)
    Vt = [Vall[:, i * VU * 129:(i + 1) * VU * 129] for i in range(2)]
    Qt = [cB2.take([70, NLOC], BF16) for _ in range(2)]

    for i in range(3):
        P.region(f"wbuf{i}", "B", "proj"); P.region(f"stf{i}", "B", "proj")
    for i in range(2):
        P.region(f"xt{i}", "B", "proj"); P.region(f"stb{i}", "B", "proj"); P.region(f"vst{i}", "B", "proj")
        P.region(f"Kt{i}", "B", "att"); P.region(f"Vt{i}", "B", "att"); P.region(f"Qt{i}", "B", "att")
        P.region(f"cstf{i}", "H", "att"); P.region(f"cstb{i}", "H", "att")
    for n in ("wout", "hrow", "junk"):
        P.region(n, "B", "proj")
    for n in ("Fa", "Fb", "Fc", "Fp", "Fq", "Fqr", "Fqp", "clfs"):
        P.region(n, "B", "F")
    for n in fin:
        P.region("fin_" + n, "H", "att")
    for tg in range(NQG + 1):
        P.region(f"hT{tg}", "H", "proj")

    ctr = {"st": 0, "stb": 0, "vst": 0, "xt": 0, "w": 0, "ps": 0, "pt": 0, "cst": 0}

    def nxt(name, n):
        v = ctr[name] % n
        ctr[name] += 1
        return v

    def dma(eng, out, in_, R, W, ch):
        P.add(eng, lambda e, o=out, i=in_: e.dma_start(out=o, in_=i), R=R, W=W, dma=ch)

    def tgs():
        r = []
        for g in range(NQG):
            r.append((g * GS * 128, GS * 128, g))
        r.append((NTOKP, NMETA + 128, NQG))
        return r

    def tiles():
        r = [(j * 128, 128, j // GS) for j in range(NSLOT)]
        r.append((NTOKP, NMETA, NQG))
        r.append((NLOC, 128, NQG))
        return r

    def x_src(l, row0, n):
        if l == 0:
            if row0 < NTOKP:
                return xp[row0:row0 + n, :]
            if row0 < NLOC:
                return xm[0:n, :]
            return xs[0:n, :]
        return xscr[row0:row0 + n, :]

    def norm_tile(xtile, n, xreg, row0, hreg, gtile, greg):
        P.add("dve", lambda e: e.scalar_tensor_tensor(out=junk[0:n, :], in0=xtile[0:n, :], scalar=1.0, in1=xtile[0:n, :],
                                                      op0=ALU.mult, op1=ALU.mult, accum_out=sml[0:n, 0:1]),
              R=[xreg], W=["junk", "sml0"])
        P.add("act", lambda e: e.activation(out=sml[0:n, 1:2], in_=sml[0:n, 0:1], func=AF.Ln, scale=1.0 / D, bias=1e-6),
              R=["sml0"], W=["sml1"])
        P.add("act", lambda e: e.activation(out=sml[0:n, 2:3], in_=sml[0:n, 1:2], func=AF.Exp, scale=-0.5),
              R=["sml1"], W=["sml2"])
        P.add("dve", lambda e: e.scalar_tensor_tensor(out=hrow[0:n, :], in0=xtile[0:n, :], scalar=sml[0:n, 2:3], in1=gtile[0:n, :],
                                                      op0=ALU.mult, op1=ALU.mult),
              R=[xreg, "sml2", greg], W=["hrow"])
        pb = PS[nxt("ps", 8)]
        pbn = f"ps{(ctr['ps'] - 1) % 8}"
        tp = pb[:, :].bitcast(BF16)
        for kc in range(8):
            P.add("pe", lambda e, kc=kc: e.transpose(out=tp[:, kc * 128:kc * 128 + n], in_=hrow[0:n, kc * 128:(kc + 1) * 128],
                                                     identity=ident[0:n, 0:n]),
                  R=["hrow", "ident"], W=[pbn])
        tpv = tp.rearrange("p (a b) -> p a b", a=8, b=128)
        P.add("act", lambda e: e.activation(out=hT[:, :, row0:row0 + n], in_=tpv[:, :, 0:n], func=AF.Copy),
              R=[pbn], W=[hreg])

    dma("sp", ident[:, :], c_ident, [], ["ident"], "c0")
    dma("sp", cf32[:, :], c_f32, [], ["cf32"], "c0")
    dma("sp", mkfox[:, :, :], c_mkfox.rearrange("p (a b) -> p a b", a=4, b=128), [], ["mk"], "c0")
    dma("sp", mkdiff[:, :, :], c_mkdiff.rearrange("p (a b) -> p a b", a=16, b=128), [], ["mk"], "c0")
    dma("sp", mksm[:, :], c_mksm, [], ["mk"], "c0")
    P.add("pool", lambda e: e.memset(ones_f[:, :], 1.0), W=["ones_f"])
    P.add("pool", lambda e: e.memset(zer_b[:, :], 0.0), W=["zer_b"])
    P.add("pool", lambda e: e.memset(lfT[:, :], 0.0), W=["lfT"])
    for i in range(2):
        vv = Vt[i].rearrange("p (u c) -> p u c", u=VU, c=129)
        P.add("pool", lambda e, vv=vv: e.memset(vv[:, :, 64:65], 1.0), W=[f"Vt{i}"])
    for r_ in range(4):
        dma("sp", kauxf[:, 0:3, r_ * NLOC:(r_ + 1) * NLOC], onesb[:, :, 0:NLOC], [], ["kauxf"], "c0")
    dma("sp", qauxf[:, 3:6, 0:NLOC], onesb[:, :, 0:NLOC], [], ["qauxf"], "c0")
    dma("sp", qauxf[:, 3:6, NLOC:NTOT], onesb[:, :, 0:128], [], ["qauxf"], "c0")
    for s in range(4):
        dma("sp", kauxfs[s, :, 0:3, :], onesb[:, :, 0:LCN], [], ["kauxfs"], "c0")

    def load_layer_consts(l):
        dma("sp", gbc[:, :], norm_g[l:l + 1, :].partition_broadcast(128) if l < DEPTH else fng[0:1, :].partition_broadcast(128),
            [], ["gbc"], "c1")

    def vst_memset():
        for i in range(2):
            vv = vst[i].rearrange("p (u c) -> p u c", u=4, c=129)
            P.add("pool", lambda e, vv=vv: e.memset(vv[:, :, 64:65], 1.0), W=[f"vst{i}"])

    load_layer_consts(0)
    for (row0, n, hr) in tiles():
        xi = nxt("xt", 2)
        dma("sp", xt[xi][0:n, :], x_src(0, row0, n), [], [f"xt{xi}"], f"xt{xi}")
        norm_tile(xt[xi], n, f"xt{xi}", row0, f"hT{hr}", gbc, "gbc")

    CH = [("dq", 0, 512), ("dk", 512, 512), ("dv", 1024, 512), ("fq", 1536, 512), ("fk", 2048, 512),
          ("fv", 2560, 512), ("fl", 3072, 8), ("g0", 3080, 512), ("g1", 3592, 512)]

    def evac_copy(eng_name, out, in_, R, W, scale=None):
        if eng_name == "act":
            if scale is None:
                P.add("act", lambda e: e.activation(out=out, in_=in_, func=AF.Copy), R=R, W=W)
            else:
                P.add("act", lambda e: e.activation(out=out, in_=in_, func=AF.Copy, scale=scale), R=R, W=W)
        else:
            if scale is None:
                P.add("dve", lambda e: e.tensor_copy(out=out, in_=in_), R=R, W=W)
            else:
                P.add("dve", lambda e: e.tensor_scalar(out=out, in0=in_, scalar1=scale, scalar2=None, op0=ALU.mult), R=R, W=W)

    def projections(l):
        vst_memset()
        dma("sp", bfbc[:, :], b_forget[l:l + 1, :].partition_broadcast(128), [], ["bfbc"], "c1")
        dma("sp", nbcol[:, :], b_forget[l:l + 1, :].rearrange("a b -> b a"), [], ["nbcol"], "c1")
        P.add("dve", lambda e: e.tensor_scalar(out=nbcol[:, :], in0=nbcol[:, :], scalar1=-1.0, scalar2=None, op0=ALU.mult),
              R=["nbcol"], W=["nbcol"])
        for ci, (cname, c0, cw) in enumerate(CH):
            wi = nxt("w", 3)
            wreg = f"wbuf{wi}"
            wsrc = w_in[l].rearrange("(kc p) n -> p kc n", p=128)[:, :, c0:c0 + cw]
            dma("pool", wbuf[wi][:, :, 0:cw], wsrc, [], [wreg], wreg)
            wb = wbuf[wi]
            if cname in ("dq", "fq", "dk", "fk", "g0", "g1"):
                for sc in range(4):
                    for (col0, ncol, hr) in tgs():
                        bi = nxt("ps", 8)
                        pa = PS[bi]; pan = f"ps{bi}"
                        for kc in range(8):
                            P.add("pe", lambda e, kc=kc, pa=pa, sc=sc, col0=col0, ncol=ncol, wb=wb: e.matmul(
                                pa[:, 0:ncol], lhsT=wb[:, kc, sc * 128:(sc + 1) * 128], rhs=hT[:, kc, col0:col0 + ncol],
                                start=(kc == 0), stop=(kc == 7)), R=[wreg, f"hT{hr}"], W=[pan])
                        if cname in ("dq", "fq", "dk", "fk"):
                            si = nxt("stb", 2)
                            sreg = f"stb{si}"
                            evac_copy("act" if (sc % 2 == 0) else "dve", stb[si][:, 0:ncol], pa[:, 0:ncol], [pan], [sreg],
                                      scale=(0.125 if cname in ("dq", "fq") else None))
                            rb0 = (0 if cname[0] == "d" else 512) + sc * 128
                            if cname in ("dq", "fq"):
                                dma("sp", qT[rb0:rb0 + 128, col0:col0 + ncol], stb[si][:, 0:ncol], [sreg], ["qT"], sreg)
                            else:
                                kch = rb0 // 128
                                if col0 < NTOKP:
                                    dma("sp", agin_k[kch][:, col0:col0 + ncol], stb[si][:, 0:ncol], [sreg], [f"agin_k{kch}"], sreg)
                                else:
                                    dma("sp", agin_k[kch][:, NTOKP:NLOC], stb[si][:, 0:NMETA], [sreg], [f"agin_k{kch}"], sreg)
                                    dma("sp", kTs[rb0:rb0 + 128, :], stb[si][:, NMETA:NMETA + 128], [sreg], ["kTs"], sreg)
                        else:
                            chunk = (0 if cname == "g0" else 4) + sc
                            si = nxt("st", 3)
                            sreg = f"stf{si}"
                            greg = f"G{chunk}_{hr}"
                            P.add("act", lambda e, pa=pa, si=si, ncol=ncol: e.activation(out=stf[si][:, 0:ncol], in_=pa[:, 0:ncol],
                                                                                         func=AF.Exp, scale=-1.0), R=[pan], W=[sreg])
                            P.add("dve", lambda e, si=si, ncol=ncol: e.tensor_scalar(out=stf[si][:, 0:ncol], in0=stf[si][:, 0:ncol],
                                                                                     scalar1=1.0, scalar2=None, op0=ALU.add), R=[sreg], W=[sreg])
                            P.add("dve", lambda e, si=si, ncol=ncol: e.reciprocal(out=stf[si][:, 0:ncol], in_=stf[si][:, 0:ncol]),
                                  R=[sreg], W=[sreg])
                            P.add("dve", lambda e, pa=pa, si=si, ncol=ncol, chunk=chunk, col0=col0: e.tensor_tensor(
                                out=Gt[:, chunk, col0:col0 + ncol], in0=pa[:, 0:ncol], in1=stf[si][:, 0:ncol], op=ALU.mult),
                                R=[pan, sreg], W=[greg])
            if cname in ("dk", "dv", "fk", "fv"):
                odst = {"dk": o_dk, "dv": o_dv, "fk": o_fk, "fv": o_fv}[cname]
                for (row0, n, hr) in tiles():
                    bi = nxt("ps", 8)
                    pb = PS[bi]; pbn = f"ps{bi}"
                    for kc in range(8):
                        P.add("pe", lambda e, kc=kc, pb=pb, row0=row0, n=n, wb=wb: e.matmul(
                            pb[0:n, 0:512], lhsT=hT[:, kc, row0:row0 + n], rhs=wb[:, kc, 0:512],
                            start=(kc == 0), stop=(kc == 7)), R=[wreg, f"hT{hr}"], W=[pbn])
                    si = nxt("st", 3)
                    sreg = f"stf{si}"
                    evac_copy("act", stf[si][0:n, :], pb[0:n, 0:512], [pbn], [sreg])
                    dma("sp", odst[l, row0:row0 + n, :], stf[si][0:n, :], [sreg], ["out_" + cname], sreg)
                    if cname in ("dv", "fv"):
                        vi = nxt("vst", 2)
                        vreg = f"vst{vi}"
                        vv = vst[vi].rearrange("p (u c) -> p u c", u=4, c=129)
                        pv = pb[:, 0:512].rearrange("p (u t c) -> p u t c", u=4, t=2, c=64)
                        P.add("dve", lambda e, vv=vv, pv=pv, n=n: e.tensor_copy(out=vv[0:n, :, 0:64], in_=pv[0:n, :, 0, :]), R=[pbn], W=[vreg])
                        P.add("dve", lambda e, vv=vv, pv=pv, n=n: e.tensor_copy(out=vv[0:n, :, 65:129], in_=pv[0:n, :, 1, :]), R=[pbn], W=[vreg])
                        pb0 = 0 if cname == "dv" else 4
                        if row0 < NLOC:
                            for u_ in range(4):
                                dma("sp", agin_v[pb0 + u_][row0:row0 + n, :], vst[vi][0:n, u_ * 129:(u_ + 1) * 129], [vreg], [f"agin_v{pb0 + u_}"], vreg)
                        else:
                            dma("sp", vsn[0:n, pb0 * 129:(pb0 + 4) * 129], vst[vi][0:n, :], [vreg], ["vsn"], vreg)
            if cname == "fl":
                for (row0, n, hr) in tiles():
                    bi = nxt("ps", 8)
                    pb = PS[bi]; pbn = f"ps{bi}"
                    for kc in range(8):
                        P.add("pe", lambda e, kc=kc, pb=pb, row0=row0, n=n, wb=wb: e.matmul(
                            pb[0:n, 0:8], lhsT=hT[:, kc, row0:row0 + n], rhs=wb[:, kc, 0:8],
                            start=(kc == 0), stop=(kc == 7)), R=[wreg, f"hT{hr}"], W=[pbn])
                    si = nxt("st", 3)
                    sreg = f"stf{si}"
                    s8 = stf[si][0:n, 0:8]
                    P.add("dve", lambda e, s8=s8, pb=pb, n=n: e.tensor_tensor(out=s8, in0=pb[0:n, 0:8], in1=bfbc[0:n, :], op=ALU.add),
                          R=[pbn, "bfbc"], W=[sreg])
                    P.add("act", lambda e, s8=s8: e.activation(out=s8, in_=s8, func=AF.Exp, scale=-1.0), R=[sreg], W=[sreg])
                    P.add("act", lambda e, s8=s8: e.activation(out=s8, in_=s8, func=AF.Ln, bias=1.0), R=[sreg], W=[sreg])
                    P.add("dve", lambda e, s8=s8: e.tensor_scalar(out=s8, in0=s8, scalar1=-1.0, scalar2=None, op0=ALU.mult), R=[sreg], W=[sreg])
                    dma("sp", o_lf[l, row0:row0 + n, :], s8, [sreg], ["out_lf"], sreg)
                for (col0, ncol, hr) in tgs():
                    bi = nxt("ps", 8)
                    pa = PS[bi]; pan = f"ps{bi}"
                    for kc in range(8):
                        P.add("pe", lambda e, kc=kc, pa=pa, col0=col0, ncol=ncol, wb=wb: e.matmul(
                            pa[0:8, 0:ncol], lhsT=wb[:, kc, 0:8], rhs=hT[:, kc, col0:col0 + ncol],
                            start=(kc == 0), stop=(kc == 7)), R=[wreg, f"hT{hr}"], W=[pan])
                    lv = lfT[0:8, col0:col0 + ncol]
                    P.add("act", lambda e, lv=lv, pa=pa, ncol=ncol: e.activation(out=lv, in_=pa[0:8, 0:ncol], func=AF.Exp, scale=-1.0, bias=nbcol[:, 0:1]),
                          R=[pan, "nbcol"], W=["lfT"])
                    P.add("act", lambda e, lv=lv: e.activation(out=lv, in_=lv, func=AF.Ln, bias=1.0), R=["lfT"], W=["lfT"])
                    P.add("dve", lambda e, lv=lv: e.tensor_scalar(out=lv, in0=lv, scalar1=-1.0, scalar2=None, op0=ALU.mult), R=["lfT"], W=["lfT"])
        dma("sp", agin_lf[:, :], lfT[0:8, 0:NLOC], ["lfT"], ["agin_lf"], "c1")

    def allgather(l):
        par = l % 2
        rg = [[0, 1, 2, 3], [4, 5, 6, 7]]
        lst = [(agin_lf, agout_lf[par], "agin_lf", f"agout_lf{par}")]
        for i in range(8):
            lst.append((agin_k[i], agout_k[par][i], f"agin_k{i}", f"agout_k{par}_{i}"))
        for i in range(8):
            lst.append((agin_v[i], agout_v[par][i], f"agin_v{i}", f"agout_v{par}_{i}"))
        for (src, dst, rn, wn) in lst:
            P.add("pool", lambda e, src=src, dst=dst: e.collective_compute("AllGather", ALU.bypass, replica_groups=rg,
                                                                           ins=[src.opt()], outs=[dst.opt()]),
                  R=[rn], W=[wn], cc=True)

    def lam_consts(l):
        lam_init = 0.8 - 0.6 * math.exp(-0.3 * l)
        for i, t in enumerate((lq1, lk1, lq2, lk2)):
            dma("sp", lamt[:, i * 64:(i + 1) * 64], t[l:l + 1, :].partition_broadcast(128), [], ["lamt"], "c1")
        dma("sp", gsub[:, :], subln_g[l:l + 1, :].rearrange("a b -> b a"), [], ["gsub"], "c1")
        P.add("dve", lambda e: e.scalar_tensor_tensor(out=lamt[:, 0:64], in0=lamt[:, 0:64], scalar=1.0, in1=lamt[:, 64:128],
                                                      op0=ALU.mult, op1=ALU.mult, accum_out=lamw[:, 0:1]), R=["lamt"], W=["lamt", "lamw"])
        P.add("dve", lambda e: e.scalar_tensor_tensor(out=lamt[:, 128:192], in0=lamt[:, 128:192], scalar=1.0, in1=lamt[:, 192:256],
                                                      op0=ALU.mult, op1=ALU.mult, accum_out=lamw[:, 1:2]), R=["lamt", "lamw"], W=["lamt", "lamw"])
        P.add("act", lambda e: e.activation(out=lamw[:, 2:4], in_=lamw[:, 0:2], func=AF.Exp), R=["lamw"], W=["lamw"])
        P.add("dve", lambda e: e.tensor_tensor(out=lamw[:, 4:5], in0=lamw[:, 3:4], in1=lamw[:, 2:3], op=ALU.subtract), R=["lamw"], W=["lamw"])
        P.add("dve", lambda e: e.tensor_scalar(out=lamw[:, 5:6], in0=lamw[:, 4:5], scalar1=-lam_init, scalar2=None, op0=ALU.add), R=["lamw"], W=["lamw"])
        P.add("dve", lambda e: e.tensor_scalar(out=gsub[:, :], in0=gsub[:, :], scalar1=(1.0 - lam_init), scalar2=None, op0=ALU.mult), R=["gsub"], W=["gsub"])

    def pieces(src, n_p, width, dstp, sreg, preg, neg):
        r = Fc[0:n_p, 0:width]
        P.add("dve", lambda e: e.tensor_scalar(out=r, in0=src, scalar1=(-1.0 if neg else 1.0), scalar2=None, op0=ALU.mult), R=[sreg], W=["Fc"])
        for k in range(3):
            P.add("dve", lambda e, k=k: e.tensor_copy(out=dstp[0:n_p, k, 0:width], in_=r), R=["Fc"], W=[preg])
            if k < 2:
                P.add("dve", lambda e, k=k: e.tensor_tensor(out=r, in0=r, in1=dstp[0:n_p, k, 0:width], op=ALU.subtract), R=["Fc", preg], W=["Fc"])

    def fox_F(l):
        par = l % 2
        P.add("pool", lambda e: e.memset(Fa[:, :], 0.0), W=["Fa"])
        P.add("pool", lambda e: e.memset(Fb[:, :], 0.0), W=["Fb"])
        for r_ in range(4):
            dma("sp", Fa[32 * r_:32 * r_ + 8, 0:NLOC], agout_lf[par][8 * r_:8 * r_ + 8, :], [f"agout_lf{par}"], ["Fa"], "f0")
        P.add("dve", lambda e: e.tensor_tensor_scan(out=Fb[0:8, NTOKP:NLOC], data0=ones_f[0:8, 0:NMETA], data1=Fa[0:8, NTOKP:NLOC],
                                                    initial=0.0, op0=ALU.mult, op1=ALU.add), R=["Fa", "ones_f"], W=["Fb"])
        for r_ in range(1, 4):
            P.add("dve", lambda e, r_=r_: e.tensor_copy(out=Fb[32 * r_:32 * r_ + 8, NTOKP:NLOC], in_=Fb[0:8, NTOKP:NLOC]), R=["Fb"], W=["Fb"])
        prev = (0, NLOC - 1)
        for B in range(4 * NSLOT):
            r_, j = B % 4, B // 4
            pr, pc = prev
            if pr != r_:
                P.add("dve", lambda e, r_=r_, pr=pr, pc=pc: e.tensor_copy(out=sml[32 * r_:32 * r_ + 8, 8:9], in_=Fb[32 * pr:32 * pr + 8, pc:pc + 1]),
                      R=["Fb"], W=["sml8"])
                init = sml[32 * r_:32 * r_ + 8, 8:9]
            else:
                init = Fb[32 * pr:32 * pr + 8, pc:pc + 1]
            P.add("dve", lambda e, r_=r_, j=j, init=init: e.tensor_tensor_scan(
                out=Fb[32 * r_:32 * r_ + 8, j * 128:(j + 1) * 128], data0=ones_f[32 * r_:32 * r_ + 8, 0:128],
                data1=Fa[32 * r_:32 * r_ + 8, j * 128:(j + 1) * 128], initial=init, op0=ALU.mult, op1=ALU.add),
                R=["Fa", "Fb", "sml8", "ones_f"], W=["Fb"])
            prev = (r_, j * 128 + 127)
        c = 0
        while c < NLOC:
            w = min(512, NLOC - c)
            bi = nxt("ps", 8)
            pm = PS[bi]; pmn = f"ps{bi}"
            P.add("pe", lambda e, pm=pm, c=c, w=w: e.matmul(pm[0:8, 0:w], lhsT=selmat[:, 0:8], rhs=Fb[:, c:c + w], start=True, stop=True),
                  R=["Fb", "cf32"], W=[pmn])
            P.add("dve", lambda e, pm=pm, c=c, w=w: e.tensor_copy(out=Fq[0:8, c:c + w], in_=pm[0:8, 0:w]), R=[pmn], W=["Fq"])
            c += w
        pieces(Fb[:, 0:NLOC], 128, NLOC, Fp, "Fb", "Fp", True)
        for r_ in range(4):
            dma("sp", kauxf[:, 3:6, r_ * NLOC:(r_ + 1) * NLOC], Fp[32 * r_:32 * r_ + 8, :, 0:NLOC], ["Fp"], ["kauxf"], "f1")
        cl4w = clfs.rearrange("p (t s h) -> p t s h", t=NCT, s=4, h=8)
        nfull = LC // 128
        for s_ in range(4):
            dma("sp", cl4w[:, 0:nfull, s_, :], clf[l, s_, 0:nfull * 128, :].rearrange("(t p) h -> p t h", p=128), [], ["clfs"], "f0")
            if LC % 128:
                n = LC % 128
                dma("sp", cl4w[0:n, nfull, s_, :], clf[l, s_, nfull * 128:LC, :], [], ["clfs"], "f0")
        cl4 = clfs.rearrange("p (t s h) -> p t s h", t=NCT, s=4, h=8)
        for s in range(4):
            t = 0
            while t < NCT:
                nt = min(4, NCT - t)
                bi = nxt("ps", 8)
                pm = PS[bi]; pmn = f"ps{bi}"
                tot = 0
                for tt in range(t, t + nt):
                    n = min(128, LC - tt * 128)
                    P.add("pe", lambda e, pm=pm, tt=tt, t=t, n=n, s=s: e.transpose(out=pm[0:8, (tt - t) * 128:(tt - t) * 128 + n],
                                                                                   in_=cl4[0:n, tt, s, :], identity=identf[0:n, 0:n]),
                          R=["clfs", "cf32"], W=[pmn])
                    tot = (tt - t) * 128 + n
                P.add("dve", lambda e, pm=pm, t=t, tot=tot: e.tensor_copy(out=Fa[0:8, t * 128:t * 128 + tot], in_=pm[0:8, 0:tot]), R=[pmn, "Fp"], W=["Fa"])
                t += nt
            P.add("dve", lambda e, s=s: e.tensor_copy(out=Fa[0:8, LC:LCN], in_=lfT[0:8, NLOC + 32 * s:NLOC + 32 * s + 32]), R=["lfT"], W=["Fa"])
            c0_ = 0
            while c0_ < LCN:
                w_ = min(128, LCN - c0_)
                init = 0.0 if c0_ == 0 else Fb[0:8, c0_ - 1:c0_]
                P.add("dve", lambda e, c0_=c0_, w_=w_, init=init: e.tensor_tensor_scan(
                    out=Fb[0:8, c0_:c0_ + w_], data0=ones_f[0:8, 0:w_], data1=Fa[0:8, c0_:c0_ + w_], initial=init,
                    op0=ALU.mult, op1=ALU.add), R=["Fa", "Fb", "ones_f"], W=["Fb"])
                c0_ += w_
            P.add("dve", lambda e, s=s: e.tensor_copy(out=Fq[0:8, NLOC + 32 * s:NLOC + 32 * s + 32], in_=Fb[0:8, LC:LCN]), R=["Fb"], W=["Fq"])
            pieces(Fb[0:8, 0:LCN], 8, LCN, Fp, "Fb", "Fp", True)
            dma("sp", kauxfs[s, :, 3:6, :], Fp[0:8, :, 0:LCN], ["Fp"], ["kauxfs"], "f1")
        P.add("dve", lambda e: e.tensor_copy(out=Fqr[0:8, :], in_=Fq[0:8, :]), R=["Fq"], W=["Fqr"])
        for k in range(3):
            P.add("dve", lambda e, k=k: e.tensor_copy(out=Fqp[0:8, k, :], in_=Fqr[0:8, :]), R=["Fqr"], W=["Fqp"])
            if k < 2:
                P.add("dve", lambda e, k=k: e.tensor_tensor(out=Fqr[0:8, :], in0=Fqr[0:8, :], in1=Fqp[0:8, k, :], op=ALU.subtract), R=["Fqr", "Fqp"], W=["Fqr"])
        dma("sp", qauxf[:, 0:3, :], Fqp[0:8, :, :], ["Fqp"], ["qauxf"], "f1")


    ACC = [(PS[3], PS[4], "ps3", "ps4"), (PS[5], PS[6], "ps5", "ps6")]
    SB = [(PS[0], "ps0"), (PS[1], "ps1"), (PS[2], "ps2")]
    PM = (PS[7], "ps7")
    actr = {"acc": 0, "s": 0}

    def zero_bank(pbank, pname, width):
        P.add("pe", lambda e: e.matmul(pbank[:, 0:width], lhsT=zer_b[0:1, 0:128], rhs=zer_b[0:1, 128:128 + width], start=True, stop=True,
                                       skip_group_check=True), R=["zer_b"], W=[pname])

    def finalize_fox(X, Xn, co, N, odd, chunk, tc0, greg):
        pm, pmn = PM
        lrow, rb, tmp = fin["lrow"], fin["rb"], fin["tmp"]
        if not odd:
            P.add("dve", lambda e: e.tensor_copy(out=lrow[64:65, 0:N], in_=X[64:65, co:co + N]), R=[Xn], W=["fin_lrow"])
            P.add("pe", lambda e: e.matmul(pm[0:64, 0:N], lhsT=ones_f[64:65, 0:64], rhs=lrow[64:65, 0:N], start=True, stop=True),
                  R=["fin_lrow", "ones_f"], W=[pmn])
            lo, hi = 0, 64
        else:
            P.add("dve", lambda e: e.tensor_copy(out=lrow[32:64, 0:N], in_=X[32:64, co:co + N]), R=[Xn], W=["fin_lrow"])
            P.add("pe", lambda e: e.matmul(pm[0:128, 0:N], lhsT=sel63[32:64, 0:128], rhs=lrow[32:64, 0:N], start=True, stop=True),
                  R=["fin_lrow", "cf32"], W=[pmn])
            lo, hi = 64, 128
        P.add("dve", lambda e: e.reciprocal(out=rb[lo:hi, 0:N], in_=pm[lo:hi, 0:N]), R=[pmn], W=["fin_rb"])
        P.add("dve", lambda e: e.tensor_tensor(out=tmp[lo:hi, 0:N], in0=X[lo:hi, co:co + N], in1=rb[lo:hi, 0:N], op=ALU.mult),
              R=[Xn, "fin_rb"], W=["fin_tmp"])
        P.add("dve", lambda e: e.tensor_tensor(out=Gt[lo:hi, chunk, tc0:tc0 + N], in0=tmp[lo:hi, 0:N], in1=Gt[lo:hi, chunk, tc0:tc0 + N], op=ALU.mult),
              R=["fin_tmp", greg], W=[greg])

    def finalize_diff_comp(X, Y, Xn, Yn, co, N, odn):
        pm, pmn = PM
        lrow, rb = fin["lrow"], fin["rb"]
        od = fin[odn]
        P.add("dve", lambda e: e.tensor_copy(out=lrow[64:65, 0:N], in_=X[64:65, co:co + N]), R=[Xn], W=["fin_lrow"])
        P.add("pe", lambda e: e.matmul(pm[0:128, 0:N], lhsT=ones_f[64:65, 0:128], rhs=lrow[64:65, 0:N], start=True, stop=True),
              R=["fin_lrow", "ones_f"], W=[pmn])
        P.add("dve", lambda e: e.reciprocal(out=rb[:, 0:N], in_=pm[:, 0:N]), R=[pmn], W=["fin_rb"])
        P.add("dve", lambda e: e.tensor_tensor(out=od[0:64, 0:N], in0=X[0:64, co:co + N], in1=rb[0:64, 0:N], op=ALU.mult),
              R=[Xn, "fin_rb"], W=["fin_" + odn])
        P.add("dve", lambda e: e.tensor_tensor(out=od[64:128, 0:N], in0=Y[64:128, co:co + N], in1=rb[64:128, 0:N], op=ALU.mult),
              R=[Yn, "fin_rb"], W=["fin_" + odn])

    def finalize_diff_head(N, chunk, tc0, greg):
        pm, pmn = PM
        od1, od2, od, sq, t = fin["od1"], fin["od2"], fin["od"], fin["sq"], fin["t"]
        P.add("dve", lambda e: e.scalar_tensor_tensor(out=od[:, 0:N], in0=od2[:, 0:N], scalar=lamw[:, 5:6], in1=od1[:, 0:N],
                                                      op0=ALU.mult, op1=ALU.add), R=["fin_od1", "fin_od2", "lamw"], W=["fin_od"])
        P.add("pool", lambda e: e.tensor_tensor(out=sq[:, 0:N], in0=od[:, 0:N], in1=od[:, 0:N], op=ALU.mult), R=["fin_od"], W=["fin_sq"])
        P.add("pe", lambda e: e.matmul(pm[:, 0:N], lhsT=onesdiv[:, 0:128], rhs=sq[:, 0:N], start=True, stop=True), R=["fin_sq", "cf32"], W=[pmn])
        P.add("act", lambda e: e.activation(out=t[:, 0:N], in_=pm[:, 0:N], func=AF.Ln, bias=1e-5), R=[pmn], W=["fin_t"])
        P.add("act", lambda e: e.activation(out=t[:, 0:N], in_=t[:, 0:N], func=AF.Exp, scale=-0.5), R=["fin_t"], W=["fin_t"])
        P.add("dve", lambda e: e.tensor_tensor(out=t[:, 0:N], in0=t[:, 0:N], in1=od[:, 0:N], op=ALU.mult), R=["fin_t", "fin_od"], W=["fin_t"])
        P.add("dve", lambda e: e.scalar_tensor_tensor(out=Gt[:, chunk, tc0:tc0 + N], in0=t[:, 0:N], scalar=gsub[:, 0:1], in1=Gt[:, chunk, tc0:tc0 + N],
                                                      op0=ALU.mult, op1=ALU.mult), R=["fin_t", "gsub", greg], W=[greg])

    def attn_qk(kT, kreg, nk, qlist, masks):
        sbank, sname = SB[actr["s"] % 3]
        actr["s"] += 1
        first = True
        wtot = 0
        for (qa, qreg, sc0, N) in qlist:
            P.add("pe", lambda e, qa=qa, sc0=sc0, N=N, first=first: e.matmul(sbank[0:nk, sc0:sc0 + N], lhsT=kT, rhs=qa, start=first, stop=False,
                                                                             skip_group_check=True), R=[kreg, qreg], W=[sname])
            first = False
            wtot = max(wtot, sc0 + N)
        for (ma, sc0) in masks:
            w = ma.shape[-1]
            P.add("pe", lambda e, ma=ma, sc0=sc0, w=w: e.matmul(sbank[0:nk, sc0:sc0 + w], lhsT=ident[0:nk, 0:nk], rhs=ma, start=False, stop=True,
                                                                skip_group_check=True), R=["mk", "ident"], W=[sname])
        pi = nxt("pt", 4)
        pt = Pt[pi]; ptn = f"Pt{pi}"
        P.add("act", lambda e: e.activation(out=pt[0:nk, 0:wtot], in_=sbank[0:nk, 0:wtot], func=AF.Exp), R=[sname], W=[ptn])
        return (pt, ptn)

    def attn_pv(vlist, st, nk):
        pt, ptn = st
        for (va, vreg, acc, accn, sc0, N) in vlist:
            P.add("pe", lambda e, va=va, acc=acc, sc0=sc0, N=N: e.matmul(acc, lhsT=va, rhs=pt[0:nk, sc0:sc0 + N], start=False, stop=True,
                                                                         skip_group_check=True), R=[vreg, ptn], W=[accn])

    def run_pipeline(items, LA=2):
        pend = []
        for it in items:
            for f in it.get("pre", ()):
                f()
            st = it["qk"]()
            pend.append((it, st))
            if len(pend) > LA:
                it0, st0 = pend.pop(0)
                for f in it0.get("prepv", ()):
                    f()
                it0["pv"](st0)
                for f in it0.get("post", ()):
                    f()
        while pend:
            it0, st0 = pend.pop(0)
            for f in it0.get("prepv", ()):
                f()
            it0["pv"](st0)
            for f in it0.get("post", ()):
                f()

    def prompt_attention(l):
        par = l % 2
        kcount = 0
        items = []

        def mk_loads_v(vp, vi):
            vreg = f"Vt{vi}"
            Vv = Vt[vi].rearrange("p (u c) -> p u c", u=VU, c=129)
            av = agout_v[par][vp]

            def f():
                for r_ in range(4):
                    src = av[r_ * NLOC:r_ * NLOC + NTOKP, :].rearrange("(j p) c -> p j c", p=128)
                    dma("sp", Vv[:, r_ * NSLOT:(r_ + 1) * NSLOT, :], src, [f"agout_v{par}_{vp}"], [vreg], vreg)
                dma("sp", Vv[0:NMETA, 4 * NSLOT, :], av[NTOKP:NLOC, :], [f"agout_v{par}_{vp}"], [vreg], vreg)
            return f

        def mk_loads_kq(vp, comp, ki, isdiff):
            kreg = f"Kt{ki}"; qreg = f"Qt{ki}"
            K3 = Kt[ki][:, 0:4 * NLOC].rearrange("p (r n) -> p r n", r=4, n=NLOC)

            def f():
                akc = agout_k[par][comp // 2].rearrange("(r k) n -> k r n", r=4, k=128)
                dma("sp", K3[0:64, :, :], akc[(comp % 2) * 64:(comp % 2) * 64 + 64, :, :], [f"agout_k{par}_{comp // 2}"], [kreg], kreg)
                if isdiff:
                    dma("sp", Kt[ki][64:70, 0:4 * NLOC], c_kauxd[vp, :, :], [], [kreg], kreg)
                    dma("sp", Qt[ki][64:70, 0:NLOC], c_qauxd[vp, :, 0:NLOC], [], [qreg], qreg)
                else:
                    h = comp - 8
                    dma("sp", Kt[ki][64:70, 0:4 * NLOC], kauxf[h, :, :], ["kauxf"], [kreg], kreg)
                    dma("sp", Qt[ki][64:70, 0:NLOC], qauxf[h, :, 0:NLOC], ["qauxf"], [qreg], qreg)
                dma("sp", Qt[ki][0:64, 0:NLOC], qT[comp * 64:(comp + 1) * 64, 0:NLOC], ["qT"], [qreg], qreg)
            return f

        def mk_block(kT, kreg, nk, qa, qreg, qn, masks, vl):
            return dict(qk=lambda: attn_qk(kT, kreg, nk, [(qa, qreg, 0, qn)], masks),
                        pv=lambda st: attn_pv(vl, st, nk))

        def mk_final(isdiff, ci, odd, X, Y, Xn, Yn, QN, chunk, q0, greg, g):
            def f():
                if isdiff:
                    finalize_diff_comp(X, Y, Xn, Yn, 0, QN, "od1" if ci == 0 else "od2")
                    if ci == 0:
                        P.add("pool", lambda e: e.tensor_copy(out=od1g[g][:, 0:QN], in_=fin["od1"][:, 0:QN]), R=["fin_od1"], W=[f"od1g{g}"])
                    else:
                        P.add("pool", lambda e: e.tensor_copy(out=fin["od1"][:, 0:QN], in_=od1g[g][:, 0:QN]), R=[f"od1g{g}"], W=["fin_od1"])
                        finalize_diff_head(QN, chunk, q0, greg)
                else:
                    finalize_fox(X, Xn, 0, QN, odd, chunk, q0, greg)
            return f

        for vp in range(8):
            vi = vp % 2
            vreg = f"Vt{vi}"
            Vv = Vt[vi].rearrange("p (u c) -> p u c", u=VU, c=129)
            isdiff = vp < 4
            comps = (2 * vp, 2 * vp + 1) if isdiff else (8 + 2 * (vp - 4), 8 + 2 * (vp - 4) + 1)
            first_of_vp = True
            for ci, comp in enumerate(comps):
                ki = kcount % 2
                kcount += 1
                kreg = f"Kt{ki}"; qreg = f"Qt{ki}"
                K3 = Kt[ki][:, 0:4 * NLOC].rearrange("p (r n) -> p r n", r=4, n=NLOC)
                h = vp if isdiff else comp - 8
                odd = (not isdiff) and (ci == 1)
                chunk = vp
                first_of_comp = True
                for g in range(NQG + 1):
                    X, Y, Xn, Yn = ACC[actr["acc"] % 2]
                    actr["acc"] += 1
                    meta_q = (g == NQG)
                    if meta_q:
                        q0, QN = NTOKP, NMETA
                    else:
                        q0, QN = g * GS * 128, GS * 128
                    pre = []
                    if first_of_vp:
                        pre.append(mk_loads_v(vp, vi))
                        first_of_vp = False
                    if first_of_comp:
                        pre.append(mk_loads_kq(vp, comp, ki, isdiff))
                        first_of_comp = False
                    prepv = [lambda X=X, Xn=Xn, QN=QN: zero_bank(X, Xn, QN)]
                    if isdiff:
                        prepv.append(lambda Y=Y, Yn=Yn, QN=QN: zero_bank(Y, Yn, QN))
                    blocks = [("meta", 0, NSLOT)]
                    if not meta_q:
                        for j in range((g + 1) * GS):
                            for r_ in range(4):
                                blocks.append(("blk", r_, j))
                    grp_items = []
                    for (kind, r_, j) in blocks:
                        if kind == "meta":
                            nk = NMETA
                            kT = K3[0:70, 0, NTOKP:NLOC]
                            vb = 4 * NSLOT
                            qs, qn = 0, QN
                            masks = []
                            if meta_q:
                                mo = 0 if not isdiff else 16 * (1 + h)
                                masks = [(mksm[0:NMETA, mo:mo + NMETA], 0)]
                        else:
                            nk = 128
                            kT = K3[0:70, r_, j * 128:(j + 1) * 128]
                            vb = r_ * NSLOT + j
                            i = j - g * GS
                            masks = []
                            if i < 0:
                                qs, qn = 0, QN
                            else:
                                qs, qn = i * 128, QN - i * 128
                                masks = [((mkfox[:, r_, :] if not isdiff else mkdiff[:, h * 4 + r_, :]), 0)]
                        qa = Qt[ki][0:70, q0 + qs:q0 + qs + qn]
                        if isdiff:
                            vl = [(Vv[0:nk, vb, 0:65], vreg, X[0:65, qs:qs + qn], Xn, 0, qn),
                                  (Vv[0:nk, vb, 1:129], vreg, Y[0:128, qs:qs + qn], Yn, 0, qn)]
                        elif not odd:
                            vl = [(Vv[0:nk, vb, 0:65], vreg, X[0:65, qs:qs + qn], Xn, 0, qn)]
                        else:
                            vl = [(Vv[0:nk, vb, 1:129], vreg, X[0:128, qs:qs + qn], Xn, 0, qn)]
                        grp_items.append(mk_block(kT, kreg, nk, qa, qreg, qn, masks, vl))
                    grp_items[0]["pre"] = pre
                    grp_items[0]["prepv"] = prepv
                    hr = NQG if meta_q else g
                    greg = f"G{chunk}_{hr}"
                    grp_items[-1]["post"] = [mk_final(isdiff, ci, odd, X, Y, Xn, Yn, QN, chunk, q0, greg, g)]
                    items.extend(grp_items)
        run_pipeline(items)

    diff_stash = {}
    od1g = [sb(f"od1g{g}", [128, 512 if g < NQG else NMETA], F32) for g in range(NQG + 1)]

    def sample_attention(l):
        K8 = Kall[:, 0:8 * LCN].rearrange("p (c n) -> p c n", c=8, n=LCN)
        NBs = NCT + 1
        Vs = Vall[:, 0:NBs * 4 * 129].rearrange("p (t u c) -> p t u c", t=NBs, u=4, c=129)
        Vs1 = Vall[:, 0:NBs * 4 * 129].rearrange("p (t u c) -> p t u c", t=NBs, u=4, c=129)
        kregs = ["Kt0", "Kt1"]; vregs = ["Vt0", "Vt1"]
        P.add("pool", lambda e: e.memset(Vs1[:, :, :, 64:65], 1.0), W=vregs)
        Q8 = Qt[0][:, 0:8 * 32].rearrange("p (c n) -> p c n", c=8, n=32)
        for s in range(4):
            for hf in range(2):
                ck = cdk if hf == 0 else cfk
                cv = cdv if hf == 0 else cfv
                for t in range(NCT):
                    n = min(128, LC - t * 128)
                    ci_ = nxt("cst", 2)
                    dma("sp", cstf[ci_][0:n, :], ck[l, s, t * 128:t * 128 + n, :], [], [f"cstf{ci_}"], f"cstf{ci_}")
                    P.add("pool", lambda e, ci_=ci_, n=n: e.tensor_copy(out=cstb[ci_][0:n, :], in_=cstf[ci_][0:n, :]), R=[f"cstf{ci_}"], W=[f"cstb{ci_}"])
                    pm, pmn = PM
                    tp = pm[:, :].bitcast(BF16)
                    for c in range(8):
                        P.add("pe", lambda e, c=c, ci_=ci_, n=n: e.transpose(out=tp[0:64, c * 128:c * 128 + n], in_=cstb[ci_][0:n, c * 64:(c + 1) * 64],
                                                                             identity=ident[0:n, 0:n]), R=[f"cstb{ci_}", "ident"], W=[pmn])
                    tpv = tp.rearrange("p (c n) -> p c n", c=8, n=128)
                    P.add("dve", lambda e, t=t, n=n, tpv=tpv: e.tensor_copy(out=K8[0:64, :, t * 128:t * 128 + n], in_=tpv[0:64, :, 0:n]), R=[pmn], W=kregs)
                for c in range(8):
                    comp = hf * 8 + c
                    dma("sp", K8[0:64, c, LC:LCN], kTs[comp * 64:(comp + 1) * 64, 32 * s:32 * s + 32], ["kTs"], kregs, "Kt0")
                    if hf == 0:
                        dma("sp", K8[64:70, c, :], c_kauxds[c // 2, :, :], [], kregs, "Kt0")
                        dma("sp", Q8[64:70, c, :], c_qauxd[c // 2, :, NLOC + 32 * s:NLOC + 32 * s + 32], [], ["Qt0"], "Qt0")
                    else:
                        dma("sp", K8[64:70, c, :], kauxfs[s, c, :, :], ["kauxfs"], kregs, "Kt0")
                        dma("sp", Q8[64:70, c, :], qauxf[c, :, NLOC + 32 * s:NLOC + 32 * s + 32], ["qauxf"], ["Qt0"], "Qt0")
                    dma("sp", Q8[0:64, c, :], qT[comp * 64:(comp + 1) * 64, NLOC + 32 * s:NLOC + 32 * s + 32], ["qT"], ["Qt0"], "Qt0")
                for t in range(NCT):
                    n = min(128, LC - t * 128)
                    ci_ = nxt("cst", 2)
                    dma("sp", cstf[ci_][0:n, :], cv[l, s, t * 128:t * 128 + n, :], [], [f"cstf{ci_}"], f"cstf{ci_}")
                    cvw = cstf[ci_].rearrange("p (u t c) -> p u t c", u=4, t=2, c=64)
                    P.add("pool", lambda e, t=t, n=n, cvw=cvw: e.tensor_copy(out=Vs[0:n, t, :, 0:64], in_=cvw[0:n, :, 0, :]), R=[f"cstf{ci_}"], W=vregs)
                    P.add("pool", lambda e, t=t, n=n, cvw=cvw: e.tensor_copy(out=Vs[0:n, t, :, 65:129], in_=cvw[0:n, :, 1, :]), R=[f"cstf{ci_}"], W=vregs)
                dma("sp", Vs[0:32, NCT, :, :], vsn[32 * s:32 * s + 32, hf * 4 * 129:(hf + 1) * 4 * 129].rearrange("p (u c) -> p u c", u=4, c=129),
                    ["vsn"], vregs, "Vt0")
                X, Y, Xn, Yn = ACC[actr["acc"] % 2]
                actr["acc"] += 1
                zero_bank(X, Xn, 256)
                if hf == 0:
                    zero_bank(Y, Yn, 256)
                for t in range(NCT + 1):
                    if t < NCT:
                        nk = min(128, LC - t * 128)
                        k0 = t * 128
                    else:
                        nk = 32
                        k0 = LC
                    qlist = []; vlist = []; masks = []
                    for c in range(8):
                        qlist.append((Q8[0:70, c, :], "Qt0", c * 32, 32))
                        if t == NCT:
                            mo = 80 + (0 if hf == 1 else 32 * (1 + c // 2))
                            masks.append((mksm[0:32, mo:mo + 32], c * 32))
                    sbk, sn = SB[actr["s"] % 3]
                    actr["s"] += 1
                    first = True
                    for c in range(8):
                        P.add("pe", lambda e, c=c, nk=nk, k0=k0, sbk=sbk, first=first: e.matmul(
                            sbk[0:nk, c * 32:(c + 1) * 32], lhsT=K8[0:70, c, k0:k0 + nk], rhs=Q8[0:70, c, :], start=first, stop=False,
                            skip_group_check=True), R=kregs + ["Qt0"], W=[sn])
                        first = False
                    for (ma, sc0) in masks:
                        P.add("pe", lambda e, ma=ma, sc0=sc0, nk=nk, sbk=sbk: e.matmul(sbk[0:nk, sc0:sc0 + 32], lhsT=ident[0:nk, 0:nk], rhs=ma,
                                                                                       start=False, stop=True, skip_group_check=True),
                              R=["mk", "ident"], W=[sn])
                    pi = nxt("pt", 4)
                    pt = Pt[pi]; ptn = f"Pt{pi}"
                    P.add("act", lambda e, pt=pt, sbk=sbk, nk=nk: e.activation(out=pt[0:nk, 0:256], in_=sbk[0:nk, 0:256], func=AF.Exp), R=[sn], W=[ptn])
                    for c in range(8):
                        if hf == 0:
                            u = c // 2
                            P.add("pe", lambda e, c=c, u=u, t=t, nk=nk, pt=pt, X=X: e.matmul(X[0:65, c * 32:(c + 1) * 32], lhsT=Vs[0:nk, t, u, 0:65],
                                                                                             rhs=pt[0:nk, c * 32:(c + 1) * 32], start=False, stop=True,
                                                                                             skip_group_check=True), R=vregs + [ptn], W=[Xn])
                            P.add("pe", lambda e, c=c, u=u, t=t, nk=nk, pt=pt, Y=Y: e.matmul(Y[0:128, c * 32:(c + 1) * 32], lhsT=Vs[0:nk, t, u, 1:129],
                                                                                             rhs=pt[0:nk, c * 32:(c + 1) * 32], start=False, stop=True,
                                                                                             skip_group_check=True), R=vregs + [ptn], W=[Yn])
                        else:
                            u = c // 2
                            if c % 2 == 0:
                                P.add("pe", lambda e, c=c, u=u, t=t, nk=nk, pt=pt, X=X: e.matmul(X[0:65, c * 32:(c + 1) * 32], lhsT=Vs[0:nk, t, u, 0:65],
                                                                                                 rhs=pt[0:nk, c * 32:(c + 1) * 32], start=False, stop=True,
                                                                                                 skip_group_check=True), R=vregs + [ptn], W=[Xn])
                            else:
                                P.add("pe", lambda e, c=c, u=u, t=t, nk=nk, pt=pt, X=X: e.matmul(X[0:128, c * 32:(c + 1) * 32], lhsT=Vs[0:nk, t, u, 1:129],
                                                                                                 rhs=pt[0:nk, c * 32:(c + 1) * 32], start=False, stop=True,
                                                                                                 skip_group_check=True), R=vregs + [ptn], W=[Xn])
                tc0 = NLOC + 32 * s
                if hf == 0:
                    for hh in range(4):
                        finalize_diff_comp(X, Y, Xn, Yn, (2 * hh) * 32, 32, "od1")
                        finalize_diff_comp(X, Y, Xn, Yn, (2 * hh + 1) * 32, 32, "od2")
                        finalize_diff_head(32, hh, tc0, f"G{hh}_{NQG}")
                else:
                    for c in range(8):
                        finalize_fox(X, Xn, c * 32, 32, c % 2 == 1, 4 + c // 2, tc0, f"G{4 + c // 2}_{NQG}")
        for i in range(2):
            vv = Vt[i].rearrange("p (u c) -> p u c", u=VU, c=129)
            P.add("pool", lambda e, vv=vv: e.memset(vv[:, :, 64:65], 1.0), W=[f"Vt{i}"])

    def out_proj(l):
        dma("pool", wout[:, :, :], w_out[l].rearrange("(kc p) n -> p kc n", p=128), [], ["wout"], "wout")
        last = (l == DEPTH - 1)
        load_layer_consts(l + 1)
        for (row0, n, hr) in tiles():
            xi = nxt("xt", 2)
            xreg = f"xt{xi}"
            dma("sp", xt[xi][0:n, :], x_src(l, row0, n), ["xscr"], [xreg], xreg)
            for half in range(2):
                bi = nxt("ps", 8)
                pb = PS[bi]; pbn = f"ps{bi}"
                for kc in range(8):
                    P.add("pe", lambda e, kc=kc, pb=pb, row0=row0, n=n, half=half: e.matmul(
                        pb[0:n, 0:512], lhsT=Gt[:, kc, row0:row0 + n], rhs=wout[:, kc, half * 512:(half + 1) * 512],
                        start=(kc == 0), stop=(kc == 7)), R=["wout", f"G{kc}_{hr}"], W=[pbn])
                P.add("dve", lambda e, pb=pb, xi=xi, n=n, half=half: e.tensor_tensor(out=xt[xi][0:n, half * 512:(half + 1) * 512],
                                                                                    in0=pb[0:n, 0:512], in1=xt[xi][0:n, half * 512:(half + 1) * 512], op=ALU.add),
                      R=[pbn, xreg], W=[xreg])
            if not last:
                dma("sp", xscr[row0:row0 + n, :], xt[xi][0:n, :], [xreg], ["xscr"], xreg)
                norm_tile(xt[xi], n, xreg, row0, f"hT{hr}", gbc, "gbc")
            else:
                if row0 >= NTOKP and row0 < NLOC:
                    continue
                P.add("dve", lambda e, xi=xi, n=n: e.scalar_tensor_tensor(out=junk[0:n, :], in0=xt[xi][0:n, :], scalar=1.0, in1=xt[xi][0:n, :],
                                                                          op0=ALU.mult, op1=ALU.mult, accum_out=sml[0:n, 0:1]), R=[xreg], W=["junk", "sml0"])
                P.add("act", lambda e, n=n: e.activation(out=sml[0:n, 1:2], in_=sml[0:n, 0:1], func=AF.Ln, scale=1.0 / D, bias=1e-6), R=["sml0"], W=["sml1"])
                P.add("act", lambda e, n=n: e.activation(out=sml[0:n, 2:3], in_=sml[0:n, 1:2], func=AF.Exp, scale=-0.5), R=["sml1"], W=["sml2"])
                P.add("dve", lambda e, xi=xi, n=n: e.scalar_tensor_tensor(out=xt[xi][0:n, :], in0=xt[xi][0:n, :], scalar=sml[0:n, 2:3], in1=gbc[0:n, :],
                                                                          op0=ALU.mult, op1=ALU.mult), R=[xreg, "sml2", "gbc"], W=[xreg])
                dst = yp[row0:row0 + n, :] if row0 < NTOKP else ys[0:n, :]
                dma("sp", dst, xt[xi][0:n, :], [xreg], ["yout"], xreg)

    import os as _os
    _stop = _os.environ.get("K_STOP", "")
    for l in range(DEPTH):
        P.epoch = l
        projections(l)
        if _stop == "proj":
            break
        allgather(l)
        if _stop == "ag":
            break
        lam_consts(l)
        fox_F(l)
        if _stop == "F":
            break
        prompt_attention(l)
        if _stop == "patt":
            break
        sample_attention(l)
        if _stop == "satt":
            break
        out_proj(l)
    P.finish()

    sems = {}
    for k in P.cnt:
        sems[k] = es.enter_context(nc.semaphore("s_" + "_".join(str(x) for x in k)))

    def emit(ename, e):
        for (fn, waits, key) in P.ops[ename]:
            for (k, i) in waits:
                e.wait_ge(sems[k], i * (16 if k[0] == "dma" else 1))
            if fn is None:
                continue
            ins = fn(e)
            if key[0] == "dma":
                ins.then_inc(sems[key], 16)
            elif key[0] == "cc":
                ins.then_inc(sems[key])
            else:
                ins.then_inc(sems[key], 1)

    with nc.Block() as block:
        @block.tensor
        def _(e):
            emit("pe", e)

        @block.scalar
        def _(e):
            emit("act", e)

        @block.vector
        def _(e):
            emit("dve", e)

        @block.gpsimd
        def _(e):
            emit("pool", e)

        @block.sync
        def _(e):
            emit("sp", e)
    es.close()
    return nc


def _bf(a):
    return np.asarray(a, dtype=np.float32).astype(ml_dtypes.bfloat16)


def _host_tables(cfg, r):
    NSLOT = cfg["NSLOT"]; LC = cfg["LC"]; PAST = cfg["PAST"]
    NTOKP = NSLOT * 128; NLOC = NTOKP + NMETA; NTOT = NLOC + 128; LCN = LC + 32

    def pos_local(rank):
        pa = np.zeros(NLOC); pb = np.zeros(NLOC)
        for j in range(NSLOT):
            B = 4 * j + rank
            pa[j * 128:(j + 1) * 128] = 128 * B
            pb[j * 128:(j + 1) * 128] = 16 + np.arange(128)
        pa[NTOKP:] = 0
        pb[NTOKP:] = np.arange(NMETA)
        return pa, pb

    pa_q = np.zeros(NTOT); pb_q = np.zeros(NTOT)
    pa_q[:NLOC], pb_q[:NLOC] = pos_local(r)
    for s in range(4):
        pa_q[NLOC + 32 * s:NLOC + 32 * s + 32] = PAST
        pb_q[NLOC + 32 * s:NLOC + 32 * s + 32] = 16 + np.arange(32)
    pa_k = np.concatenate([pos_local(rr)[0] for rr in range(4)])
    pb_k = np.concatenate([pos_local(rr)[1] for rr in range(4)])
    idx = np.arange(LCN)
    pa_s = np.where(idx < LC, 128 * (idx // 128), PAST).astype(np.float64)
    pb_s = np.where(idx < LC, idx % 128, 16 + (idx - LC)).astype(np.float64)
    qauxd = np.zeros((4, 6, NTOT), np.float32); kauxd = np.zeros((4, 6, 4 * NLOC), np.float32); kauxds = np.zeros((4, 6, LCN), np.float32)
    for h in range(4):
        s_ = SLOPES[h]
        qauxd[h, 0] = -s_ * pa_q; qauxd[h, 1] = -s_ * pb_q; qauxd[h, 2] = 1; qauxd[h, 3] = 1
        kauxd[h, 0] = 1; kauxd[h, 1] = 1; kauxd[h, 2] = s_ * pa_k; kauxd[h, 3] = s_ * pb_k
        kauxds[h, 0] = 1; kauxds[h, 1] = 1; kauxds[h, 2] = s_ * pa_s; kauxds[h, 3] = s_ * pb_s
    k = np.arange(128)[:, None]; q = np.arange(128)[None, :]
    mkfox = np.zeros((128, 4, 128), np.float32); mkdiff = np.zeros((128, 16, 128), np.float32)
    for rr in range(4):
        if rr > r:
            mkfox[:, rr, :] = NEGM
        elif rr == r:
            mkfox[:, rr, :] = np.where(k > q, NEGM, 0.0)
        for h in range(4):
            if rr > r:
                mkdiff[:, h * 4 + rr, :] = NEGM
            elif rr == r:
                m = np.where((k // 64) > (q // 64), NEGM, np.where(k > q, -2.0 * SLOPES[h] * (k - q), 0.0))
                mkdiff[:, h * 4 + rr, :] = m
    mksm = np.zeros((32, 240), np.float32)
    k16 = np.arange(16)[:, None]; q16 = np.arange(16)[None, :]
    mksm[0:16, 0:16] = np.where(k16 > q16, NEGM, 0.0)
    k32 = np.arange(32)[:, None]; q32 = np.arange(32)[None, :]
    mksm[0:32, 80:112] = np.where(k32 > q32, NEGM, 0.0)
    for h in range(4):
        mksm[0:16, 16 * (1 + h):16 * (2 + h)] = np.where(k16 > q16, -2.0 * SLOPES[h] * (k16 - q16), 0.0)
        mksm[0:32, 80 + 32 * (1 + h):80 + 32 * (2 + h)] = np.where(k32 > q32, -2.0 * SLOPES[h] * (k32 - q32), 0.0)
    cf = np.zeros((128, 392), np.float32)
    cf[:, 0:128] = np.eye(128)
    cf[63, 128 + 64:128 + 128] = 1.0
    cf[:, 256:384] = 1.0 / 128.0
    for h in range(8):
        cf[32 * r + h, 384 + h] = 1.0
    return dict(c_ident=_bf(np.eye(128)), c_f32=cf, c_mkfox=_bf(mkfox.reshape(128, -1)), c_mkdiff=_bf(mkdiff.reshape(128, -1)),
                c_mksm=_bf(mksm), c_qauxd=_bf(qauxd), c_kauxd=_bf(kauxd), c_kauxds=_bf(kauxds),
                c_ones=_bf(np.ones((8, 3, max(NLOC, LCN)), np.float32)))


_NC_CACHE = {}


def kernel(x_prompt, x_sample, cache_diff_k, cache_diff_v, cache_fox_k, cache_fox_v, cache_fox_logf,
           meta_tokens, w_in, b_forget, norm_g, w_out, lambda_q1, lambda_k1, lambda_q2, lambda_k2,
           subln_g, final_norm_g):
    f32 = np.float32
    x_prompt = np.asarray(x_prompt, f32); x_sample = np.asarray(x_sample, f32)
    B, SEQ, _ = x_prompt.shape
    DEPTH = w_in.shape[0]
    LC = cache_diff_k.shape[2]
    assert B == 2 and x_sample.shape[0] == 32 and x_sample.shape[1] == 32
    cfg = _cfg_from_shapes(SEQ, LC - NMETA, DEPTH)
    NSLOT = cfg["NSLOT"]
    NTOKP = NSLOT * 128; NLOC = NTOKP + NMETA; NTOT = NLOC + 128
    key = (SEQ, LC, DEPTH)
    if key not in _NC_CACHE:
        _NC_CACHE[key] = build_nc(cfg)
    nc = _NC_CACHE[key]
    in_maps = []
    xpb = x_prompt.reshape(B, NSLOT, 4, 128, D)
    for c in range(8):
        b, r = c // 4, c % 4
        m = dict(
            xp=np.ascontiguousarray(xpb[b, :, r]).reshape(NTOKP, D),
            xm=np.ascontiguousarray(np.asarray(meta_tokens, f32)),
            xs=np.ascontiguousarray(x_sample[4 * c:4 * c + 4]).reshape(128, D),
            cdk=np.ascontiguousarray(np.asarray(cache_diff_k, f32)[:, 4 * c:4 * c + 4]).reshape(DEPTH, 4, LC, 512),
            cdv=np.ascontiguousarray(np.asarray(cache_diff_v, f32)[:, 4 * c:4 * c + 4]).reshape(DEPTH, 4, LC, 512),
            cfk=np.ascontiguousarray(np.asarray(cache_fox_k, f32)[:, 4 * c:4 * c + 4]).reshape(DEPTH, 4, LC, 512),
            cfv=np.ascontiguousarray(np.asarray(cache_fox_v, f32)[:, 4 * c:4 * c + 4]).reshape(DEPTH, 4, LC, 512),
            clf=np.ascontiguousarray(np.asarray(cache_fox_logf, f32)[:, 4 * c:4 * c + 4]),
            w_in=np.asarray(w_in, f32), w_out=np.asarray(w_out, f32), b_forget=np.asarray(b_forget, f32),
            norm_g=np.asarray(norm_g, f32), lq1=np.asarray(lambda_q1, f32), lk1=np.asarray(lambda_k1, f32),
            lq2=np.asarray(lambda_q2, f32), lk2=np.asarray(lambda_k2, f32), subln_g=np.asarray(subln_g, f32),
            fng=np.asarray(final_norm_g, f32).reshape(1, D),
        )
        m.update(_host_tables(cfg, r))
        in_maps.append(m)
    res = run_bass_kernel_spmd(nc, in_maps, core_ids=list(range(8)))
    R = res.results
    y_prompt = np.zeros((B, SEQ, D), f32)
    y_sample = np.zeros((32, 32, D), f32)
    T = NMETA + SEQ
    outs_p = {n: np.zeros((DEPTH, B, T, w), f32) for n, w in (("o_dk", 512), ("o_dv", 512), ("o_fk", 512), ("o_fv", 512), ("o_lf", 8))}
    outs_s = {n: np.zeros((DEPTH, 32, 32, w), f32) for n, w in (("o_dk", 512), ("o_dv", 512), ("o_fk", 512), ("o_fv", 512), ("o_lf", 8))}
    for c in range(8):
        b, r = c // 4, c % 4
        y_prompt.reshape(B, NSLOT, 4, 128, D)[b, :, r] = np.asarray(R[c]["yp"]).reshape(NSLOT, 128, D)
        y_sample[4 * c:4 * c + 4] = np.asarray(R[c]["ys"]).reshape(4, 32, D)
        for n in outs_p:
            a = np.asarray(R[c][n])
            w = a.shape[-1]
            outs_p[n][:, b, NMETA:].reshape(DEPTH, NSLOT, 4, 128, w)[:, :, r] = a[:, 0:NTOKP].reshape(DEPTH, NSLOT, 128, w)
            if r == 0:
                outs_p[n][:, b, 0:NMETA] = a[:, NTOKP:NLOC]
            outs_s[n][:, 4 * c:4 * c + 4] = a[:, NLOC:NTOT].reshape(DEPTH, 4, 32, w)
    return (y_prompt, y_sample,
            outs_p["o_dk"].reshape(DEPTH, B, T, 4, 2, 64), outs_p["o_dv"].reshape(DEPTH, B, T, 4, 128),
            outs_p["o_fk"].reshape(DEPTH, B, T, 8, 64), outs_p["o_fv"].reshape(DEPTH, B, T, 8, 64), outs_p["o_lf"],
            outs_s["o_dk"].reshape(DEPTH, 32, 32, 4, 2, 64), outs_s["o_dv"].reshape(DEPTH, 32, 32, 4, 128),
            outs_s["o_fk"].reshape(DEPTH, 32, 32, 8, 64), outs_s["o_fv"].reshape(DEPTH, 32, 32, 8, 64), outs_s["o_lf"])
```

```python
import math
import numpy as np
import ml_dtypes
import concourse.bass as bass
import concourse.mybir as mybir
from concourse.bass_utils import run_bass_kernel_spmd

F32 = mybir.dt.float32
BF16 = mybir.dt.bfloat16
AF = mybir.ActivationFunctionType
ALU = mybir.AluOpType
AX = mybir.AxisListType

D = 1024
NMETA = 16
HD = 64
NCOLS = 4104
NEGM = -30000.0
SLOPES = [2.0 ** (-8.0 * (h + 1) / 4) for h in range(4)]
ENGS = ("pe", "act", "dve", "pool", "sp")


class Prog:
    def __init__(self):
        self.ops = {e: [] for e in ENGS}
        self.cnt = {}
        self.rs = {}
        self.seen = {e: {} for e in ENGS}
        self.epoch = 0
        self.arena = {}
        self.ast = {}
        self.ncc = 0

    def region(self, name, arena=None, phase=None):
        if arena is not None:
            self.arena[name] = (arena, phase)

    def add(self, eng, fn, R=(), W=(), dma=None, cc=False):
        deps = {}

        def need(d):
            for k, i in d.items():
                if k[0] in ("dma", "cc"):
                    i = self.cnt[k]
                if deps.get(k, 0) < i:
                    deps[k] = i

        for r in R:
            st = self.rs.get(r)
            if st:
                need(st[0])
            ar = self.arena.get(r)
            if ar:
                for (a, p), d in self.ast.items():
                    if a == ar[0] and p != ar[1]:
                        need(d)
        for w in W:
            st = self.rs.get(w)
            if st:
                need(st[0])
                need(st[1])
            ar = self.arena.get(w)
            if ar:
                for (a, p), d in self.ast.items():
                    if a == ar[0] and p != ar[1]:
                        need(d)
        if cc:
            key = ("cc", self.epoch)
        elif dma:
            key = ("dma", dma)
        else:
            key = (eng, self.epoch)
        waits = []
        seen = self.seen[eng]
        for k, i in deps.items():
            if k == key and eng == "pe":
                continue
            if seen.get(k, 0) >= i:
                continue
            seen[k] = i
            waits.append((k, i))
        idx = self.cnt[key] = self.cnt.get(key, 0) + 1
        self.ops[eng].append((fn, waits, key))
        for r in R:
            st = self.rs.setdefault(r, ({}, {}))
            st[1][key] = idx
        for w in W:
            st = self.rs.setdefault(w, ({}, {}))
            st[0].clear()
            st[1].clear()
            st[0][key] = idx
        for x in tuple(R) + tuple(W):
            ar = self.arena.get(x)
            if ar:
                self.ast.setdefault(ar, {})[key] = idx

    def finish(self):
        waits = [(k, i) for k, i in self.cnt.items()]
        self.ops["sp"].append((None, waits, None))


def _cfg_from_shapes(seq, past_len, depth):
    assert seq % 512 == 0 and past_len % 64 == 0
    nblk = seq // 128
    nslot = nblk // 4
    import os
    cfg = dict(SEQ=seq, PAST=past_len, DEPTH=depth, NSLOT=nslot, LC=NMETA + past_len, GS=int(os.environ.get("K_GS", "4")))
    return cfg


def build_nc(cfg):
    NSLOT = cfg["NSLOT"]
    DEPTH = cfg["DEPTH"]
    LC = cfg["LC"]
    NTOKP = NSLOT * 128
    NLOC = NTOKP + NMETA
    NTOT = NLOC + 128
    LCN = LC + 32
    NCT = (LC + 127) // 128
    GS = min(int(cfg.get("GS", 4)), NSLOT)
    NQG = NSLOT // GS
    NB = 4 * NSLOT + 1
    FW = max(NLOC, LCN)
    KW = max(4 * NLOC, 8 * LCN // 2 + 8)
    VU = max(NB, (NCT + 1) * 4 // 2 + 2)

    nc = bass.Bass("TRN2", target_bir_lowering=False)
    P = Prog()

    def din(name, shape, dt=F32):
        return nc.dram_tensor(name, list(shape), dt, kind="ExternalInput").ap()

    def dout(name, shape, dt=F32):
        return nc.dram_tensor(name, list(shape), dt, kind="ExternalOutput").ap()

    def dscr(name, shape, dt):
        return nc.dram_tensor(name, list(shape), dt).ap()

    xp = din("xp", [NTOKP, D]); xm = din("xm", [NMETA, D]); xs = din("xs", [128, D])
    cdk = din("cdk", [DEPTH, 4, LC, 512]); cdv = din("cdv", [DEPTH, 4, LC, 512])
    cfk = din("cfk", [DEPTH, 4, LC, 512]); cfv = din("cfv", [DEPTH, 4, LC, 512])
    clf = din("clf", [DEPTH, 4, LC, 8])
    w_in = din("w_in", [DEPTH, D, NCOLS]); w_out = din("w_out", [DEPTH, D, D])
    b_forget = din("b_forget", [DEPTH, 8]); norm_g = din("norm_g", [DEPTH, D])
    lq1 = din("lq1", [DEPTH, 64]); lk1 = din("lk1", [DEPTH, 64]); lq2 = din("lq2", [DEPTH, 64]); lk2 = din("lk2", [DEPTH, 64])
    subln_g = din("subln_g", [DEPTH, 128]); fng = din("fng", [1, D])
    c_ident = din("c_ident", [128, 128], BF16)
    c_f32 = din("c_f32", [128, 5 * 128 + 8])
    c_mkfox = din("c_mkfox", [128, 4 * 128], BF16)
    c_mkdiff = din("c_mkdiff", [128, 16 * 128], BF16)
    c_mksm = din("c_mksm", [32, 5 * 16 + 5 * 32], BF16)
    c_qauxd = din("c_qauxd", [4, 6, NTOT], BF16)
    c_kauxd = din("c_kauxd", [4, 6, 4 * NLOC], BF16)
    c_kauxds = din("c_kauxds", [4, 6, LCN], BF16)
    onesb = din("c_ones", [8, 3, FW], BF16)

    yp = dout("yp", [NTOKP, D]); ys = dout("ys", [128, D])
    o_dk = dout("o_dk", [DEPTH, NTOT, 512]); o_dv = dout("o_dv", [DEPTH, NTOT, 512])
    o_fk = dout("o_fk", [DEPTH, NTOT, 512]); o_fv = dout("o_fv", [DEPTH, NTOT, 512])
    o_lf = dout("o_lf", [DEPTH, NTOT, 8])

    xscr = dscr("xscr", [NTOT, D], F32)
    qT = dscr("qT", [1024, NTOT], BF16)
    kTs = dscr("kTs", [1024, 128], BF16)
    vsn = dscr("vsn", [128, 8 * 129], BF16)
    agin_k = [dscr(f"agin_k{i}", [128, NLOC], BF16) for i in range(8)]
    agin_v = [dscr(f"agin_v{i}", [NLOC, 129], BF16) for i in range(8)]
    agin_lf = dscr("agin_lf", [8, NLOC], F32)
    agout_k = [[dscr(f"agout_k{i}_{j}", [4 * 128, NLOC], BF16) for j in range(8)] for i in range(2)]
    agout_v = [[dscr(f"agout_v{i}_{j}", [4 * NLOC, 129], BF16) for j in range(8)] for i in range(2)]
    agout_lf = [dscr(f"agout_lf{i}", [32, NLOC], F32) for i in range(2)]
    kauxf = dscr("kauxf", [8, 6, 4 * NLOC], BF16)
    qauxf = dscr("qauxf", [8, 6, NTOT], BF16)
    kauxfs = dscr("kauxfs", [4, 8, 6, LCN], BF16)

    def _sz(shape_free_elems, dt):
        nb = shape_free_elems * (2 if dt == BF16 else 4)
        return ((nb + 3) // 4 + 7) // 8 * 8
    sz_proj = 3 * _sz(8 * 512, BF16) + _sz(8 * 1024, BF16) + 2 * _sz(1024, F32) + 3 * _sz(512, F32) + 2 * _sz(512, BF16) + 2 * _sz(4 * 129, BF16) + 2 * _sz(1024, BF16) + 2 * _sz(4 * 512, F32)
    sz_F = 4 * _sz(FW, F32) + _sz(3 * FW, BF16) + 2 * _sz(NTOT, F32) + _sz(3 * NTOT, BF16) + _sz(NCT * 32, F32) + _sz(4 * NSLOT + 8, F32)
    sz_att = _sz(2 * KW, BF16) + _sz(2 * VU * 129, BF16) + 2 * _sz(NLOC, BF16)
    BIG_F32 = max(sz_proj, sz_F, sz_att) + 8
    from contextlib import ExitStack
    es = ExitStack()

    def sb(name, shape, dt):
        return es.enter_context(nc.sbuf_tensor(name, list(shape), dt))

    def ps(name, shape, dt=F32):
        return es.enter_context(nc.psum_tensor(name, list(shape), dt))

    arH = sb("arH", [128, max(_sz(8 * NTOT, BF16), 12 * _sz(512, F32) + 4 * _sz(512, BF16)) + 8], F32)
    arB = sb("arB", [128, BIG_F32], F32)
    Gt = sb("Gt", [128, 8, NTOT], BF16)
    Pt = [sb(f"Pt{i}", [128, 512], BF16) for i in range(4)]
    ident = sb("ident", [128, 128], BF16)
    cf32 = sb("cf32", [128, 5 * 128 + 8], F32)
    identf = cf32[:, 0:128]; sel63 = cf32[:, 128:256]; onesdiv = cf32[:, 256:384]; Mall = cf32[:, 384:512]; Mlow = cf32[:, 512:640]; selmat = cf32[:, 640:648]
    ones_f = sb("ones_f", [128, 128], F32)
    zer_b = sb("zer_b", [1, 640], BF16)
    mkfox = sb("mkfox", [128, 4, 128], BF16)
    mkdiff = sb("mkdiff", [128, 16, 128], BF16)
    mksm = sb("mksm", [32, 240], BF16)
    gbc = sb("gbc", [128, D], F32)
    bfbc = sb("bfbc", [128, 8], F32)
    nbcol = sb("nbcol", [8, 1], F32)
    lamt = sb("lamt", [128, 4 * 64], F32)
    lamw = sb("lamw", [128, 16], F32)
    gsub = sb("gsub", [128, 1], F32)
    sml = sb("sml", [128, 16], F32)
    lfT = sb("lfT", [8, NTOT], F32)

    PS = [ps(f"psb{i}", [128, 512], F32) for i in range(8)]

    class Carver:
        def __init__(self, arena):
            self.a = arena
            self.off = 0

        def take(self, shape, dt):
            n = 1
            for s in shape[1:]:
                n *= s
            nb = n * (2 if dt == BF16 else 4)
            nf = (nb + 3) // 4
            nf = (nf + 7) // 8 * 8
            v = self.a[0:shape[0], self.off:self.off + nf]
            self.off += nf
            if dt == BF16:
                v = v.bitcast(BF16)[:, 0:n]
            else:
                v = v[:, 0:n]
            assert self.off <= self.a.shape[1], (self.off, self.a.shape)
            return v

    def v3(ap, a, b):
        return ap.rearrange("p (a b) -> p a b", a=a, b=b)

    cH0 = Carver(arH)
    hT = v3(cH0.take([128, 8 * NTOT], BF16), 8, NTOT)
    cH1 = Carver(arH)
    fin = {n: cH1.take([128, 512], F32) for n in ("lrow", "rb", "tmp", "od1", "od2", "od", "sq", "t")}
    cstf = [cH1.take([128, 512], F32) for _ in range(4)]
    cstb = [cH1.take([128, 512], BF16) for _ in range(4)]
    cB0 = Carver(arB)
    wbuf = [v3(cB0.take([128, 8 * 512], BF16), 8, 512) for _ in range(3)]
    wout = v3(cB0.take([128, 8 * 1024], BF16), 8, 1024)
    xt = [cB0.take([128, 1024], F32) for _ in range(2)]
    stf = [cB0.take([128, 512], F32) for _ in range(3)]
    stb = [cB0.take([128, 512], BF16) for _ in range(2)]
    vst = [cB0.take([128, 4 * 129], BF16) for _ in range(2)]
    hrow = cB0.take([128, 1024], BF16)
    wst = [v3(cB0.take([128, 4 * 512], F32), 4, 512) for _ in range(2)]
    junk = cB0.take([128, 1024], BF16)
    cB1 = Carver(arB)
    Fa = cB1.take([128, FW], F32); Fb = cB1.take([128, FW], F32); Fc = cB1.take([128, FW], F32)
    Fp = v3(cB1.take([128, 3 * FW], BF16), 3, FW)
    Fq = cB1.take([8, NTOT], F32); Fqr = cB1.take([8, NTOT], F32)
    Fqp = v3(cB1.take([8, 3 * NTOT], BF16), 3, NTOT)
    clfs = cB1.take([128, NCT * 32], F32)
    Fm = cB1.take([128, FW], F32)
    Tt = cB1.take([128, 4 * NSLOT + 8], F32)
    cB2 = Carver(arB)
    Kall = cB2.take([70, 2 * KW], BF16)
    Kt = [Kall[:, i * KW:(i + 1) * KW] for i in range(2)]
    Vall = cB2.take([128, 2 * VU * 129], BF16)
    Vt = [Vall[:, i * VU * 129:(i + 1) * VU * 129] for i in range(2)]
    Qt = [cB2.take([70, NLOC], BF16) for _ in range(2)]

    for i in range(3):
        P.region(f"wbuf{i}", "B", "proj"); P.region(f"stf{i}", "B", "proj")
    for i in range(2):
        P.region(f"xt{i}", "B", "proj"); P.region(f"stb{i}", "B", "proj"); P.region(f"vst{i}", "B", "proj")
        P.region(f"Kt{i}", "B", "att"); P.region(f"Vt{i}", "B", "att"); P.region(f"Qt{i}", "B", "att")
    for i in range(4):
        P.region(f"cstf{i}", "H", "att"); P.region(f"cstb{i}", "H", "att")
    for i in range(2):
        P.region(f"wst{i}", "B", "proj")
    for n in ("wout", "hrow", "junk"):
        P.region(n, "B", "proj")
    for n in ("Fa", "Fb", "Fc", "Fp", "Fq", "Fqr", "Fqp", "clfs", "Fm", "Tt"):
        P.region(n, "B", "F")
    for n in fin:
        P.region("fin_" + n, "H", "att")
    for tg in range(NQG + 1):
        P.region(f"hT{tg}", "H", "proj")

    ctr = {"st": 0, "stb": 0, "vst": 0, "xt": 0, "w": 0, "ps": 0, "pt": 0, "cst": 0, "wst": 0}

    def nxt(name, n):
        v = ctr[name] % n
        ctr[name] += 1
        return v

    def dma(eng, out, in_, R, W, ch):
        P.add(eng, lambda e, o=out, i=in_: e.dma_start(out=o, in_=i), R=R, W=W, dma=ch)

    def tgs():
        r = []
        for g in range(NQG):
            r.append((g * GS * 128, GS * 128, g))
        r.append((NTOKP, NMETA + 128, NQG))
        return r

    def tiles():
        r = [(j * 128, 128, j // GS) for j in range(NSLOT)]
        r.append((NTOKP, NMETA, NQG))
        r.append((NLOC, 128, NQG))
        return r

    def x_src(l, row0, n):
        if l == 0:
            if row0 < NTOKP:
                return xp[row0:row0 + n, :]
            if row0 < NLOC:
                return xm[0:n, :]
            return xs[0:n, :]
        return xscr[row0:row0 + n, :]

    def norm_tile(xtile, n, xreg, row0, hreg, gtile, greg):
        P.add("dve", lambda e: e.scalar_tensor_tensor(out=junk[0:n, :], in0=xtile[0:n, :], scalar=1.0, in1=xtile[0:n, :],
                                                      op0=ALU.mult, op1=ALU.mult, accum_out=sml[0:n, 0:1]),
              R=[xreg], W=["junk", "sml0"])
        P.add("act", lambda e: e.activation(out=sml[0:n, 1:2], in_=sml[0:n, 0:1], func=AF.Ln, scale=1.0 / D, bias=1e-6),
              R=["sml0"], W=["sml1"])
        P.add("act", lambda e: e.activation(out=sml[0:n, 2:3], in_=sml[0:n, 1:2], func=AF.Exp, scale=-0.5),
              R=["sml1"], W=["sml2"])
        P.add("dve", lambda e: e.scalar_tensor_tensor(out=hrow[0:n, :], in0=xtile[0:n, :], scalar=sml[0:n, 2:3], in1=gtile[0:n, :],
                                                      op0=ALU.mult, op1=ALU.mult),
              R=[xreg, "sml2", greg], W=["hrow"])
        pb = PS[nxt("ps", 8)]
        pbn = f"ps{(ctr['ps'] - 1) % 8}"
        tp = pb[:, :].bitcast(BF16)
        for kc in range(8):
            P.add("pe", lambda e, kc=kc: e.transpose(out=tp[:, kc * 128:kc * 128 + n], in_=hrow[0:n, kc * 128:(kc + 1) * 128],
                                                     identity=ident[0:n, 0:n]),
                  R=["hrow", "ident"], W=[pbn])
        tpv = tp.rearrange("p (a b) -> p a b", a=8, b=128)
        P.add("act", lambda e: e.activation(out=hT[:, :, row0:row0 + n], in_=tpv[:, :, 0:n], func=AF.Copy),
              R=[pbn], W=[hreg])

    dma("sp", ident[:, :], c_ident, [], ["ident"], "c0")
    dma("sp", cf32[:, :], c_f32, [], ["cf32"], "c0")
    dma("sp", mkfox[:, :, :], c_mkfox.rearrange("p (a b) -> p a b", a=4, b=128), [], ["mk"], "c0")
    dma("sp", mkdiff[:, :, :], c_mkdiff.rearrange("p (a b) -> p a b", a=16, b=128), [], ["mk"], "c0")
    dma("sp", mksm[:, :], c_mksm, [], ["mk"], "c0")
    P.add("pool", lambda e: e.memset(ones_f[:, :], 1.0), W=["ones_f"])
    P.add("pool", lambda e: e.memset(zer_b[:, :], 0.0), W=["zer_b"])
    P.add("pool", lambda e: e.memset(lfT[:, :], 0.0), W=["lfT"])
    for i in range(2):
        vv = Vt[i].rearrange("p (u c) -> p u c", u=VU, c=129)
        P.add("pool", lambda e, vv=vv: e.memset(vv[:, :, 64:65], 1.0), W=[f"Vt{i}"])
    for r_ in range(4):
        dma("sp", kauxf[:, 0:3, r_ * NLOC:(r_ + 1) * NLOC], onesb[:, :, 0:NLOC], [], ["kauxf"], "c0")
    dma("sp", qauxf[:, 3:6, 0:NLOC], onesb[:, :, 0:NLOC], [], ["qauxf"], "c0")
    dma("sp", qauxf[:, 3:6, NLOC:NTOT], onesb[:, :, 0:128], [], ["qauxf"], "c0")
    for s in range(4):
        dma("sp", kauxfs[s, :, 0:3, :], onesb[:, :, 0:LCN], [], ["kauxfs"], "c0")

    def load_layer_consts(l):
        dma("sp", gbc[:, :], norm_g[l:l + 1, :].partition_broadcast(128) if l < DEPTH else fng[0:1, :].partition_broadcast(128),
            [], ["gbc"], "c1")

    def vst_memset():
        for i in range(2):
            vv = vst[i].rearrange("p (u c) -> p u c", u=4, c=129)
            P.add("pool", lambda e, vv=vv: e.memset(vv[:, :, 64:65], 1.0), W=[f"vst{i}"])

    load_layer_consts(0)
    for (row0, n, hr) in tiles():
        xi = nxt("xt", 2)
        dma("sp", xt[xi][0:n, :], x_src(0, row0, n), [], [f"xt{xi}"], f"xt{xi}")
        norm_tile(xt[xi], n, f"xt{xi}", row0, f"hT{hr}", gbc, "gbc")

    CH = [("dk", 512, 512), ("dv", 1024, 512), ("fl", 3072, 8), ("fk", 2048, 512), ("fv", 2560, 512),
          ("dq", 0, 512), ("fq", 1536, 512), ("g0", 3080, 512), ("g1", 3592, 512)]

    def evac_copy(eng_name, out, in_, R, W, scale=None):
        if eng_name == "act":
            if scale is None:
                P.add("act", lambda e: e.activation(out=out, in_=in_, func=AF.Copy), R=R, W=W)
            else:
                P.add("act", lambda e: e.activation(out=out, in_=in_, func=AF.Copy, scale=scale), R=R, W=W)
        else:
            if scale is None:
                P.add("dve", lambda e: e.tensor_copy(out=out, in_=in_), R=R, W=W)
            else:
                P.add("dve", lambda e: e.tensor_scalar(out=out, in0=in_, scalar1=scale, scalar2=None, op0=ALU.mult), R=R, W=W)

    def projections(l):
        vst_memset()
        dma("sp", bfbc[:, :], b_forget[l:l + 1, :].partition_broadcast(128), [], ["bfbc"], "c1")
        dma("sp", nbcol[:, :], b_forget[l:l + 1, :].rearrange("a b -> b a"), [], ["nbcol"], "c1")
        P.add("dve", lambda e: e.tensor_scalar(out=nbcol[:, :], in0=nbcol[:, :], scalar1=-1.0, scalar2=None, op0=ALU.mult),
              R=["nbcol"], W=["nbcol"])
        wslots = {}

        def load_w(ci):
            cname, c0, cw = CH[ci]
            wi = nxt("w", 3)
            wsrc = w_in[l].rearrange("(kc p) n -> p kc n", p=128)
            for hh_ in range(2):
                wsi = nxt("wst", 2)
                dma("sp", wst[wsi][:, :, 0:cw], wsrc[:, 4 * hh_:4 * hh_ + 4, c0:c0 + cw], [], [f"wst{wsi}"], f"wst{wsi}")
                P.add("act", lambda e, wsi=wsi, wi=wi, hh_=hh_, cw=cw: e.activation(out=wbuf[wi][:, 4 * hh_:4 * hh_ + 4, 0:cw], in_=wst[wsi][:, :, 0:cw], func=AF.Copy),
                      R=[f"wst{wsi}"], W=[f"wbuf{wi}"])
            wslots[ci] = wi

        load_w(0)
        load_w(1)
        for ci, (cname, c0, cw) in enumerate(CH):
            if ci + 2 < len(CH):
                load_w(ci + 2)
            wi = wslots[ci]
            wreg = f"wbuf{wi}"
            wb = wbuf[wi]
            if cname in ("dq", "fq", "dk", "fk", "g0", "g1"):
                for sc in range(4):
                    for (col0, ncol, hr) in tgs():
                        bi = nxt("ps", 8)
                        pa = PS[bi]; pan = f"ps{bi}"
                        for kc in range(8):
                            P.add("pe", lambda e, kc=kc, pa=pa, sc=sc, col0=col0, ncol=ncol, wb=wb: e.matmul(
                                pa[:, 0:ncol], lhsT=wb[:, kc, sc * 128:(sc + 1) * 128], rhs=hT[:, kc, col0:col0 + ncol],
                                start=(kc == 0), stop=(kc == 7)), R=[wreg, f"hT{hr}"], W=[pan])
                        if cname in ("dq", "fq", "dk", "fk"):
                            si = nxt("stb", 2)
                            sreg = f"stb{si}"
                            evac_copy("act" if (sc % 2 == 0) else "dve", stb[si][:, 0:ncol], pa[:, 0:ncol], [pan], [sreg],
                                      scale=(0.125 if cname in ("dq", "fq") else None))
                            rb0 = (0 if cname[0] == "d" else 512) + sc * 128
                            if cname in ("dq", "fq"):
                                dma("sp", qT[rb0:rb0 + 128, col0:col0 + ncol], stb[si][:, 0:ncol], [sreg], ["qT"], sreg)
                            else:
                                kch = rb0 // 128
                                if col0 < NTOKP:
                                    dma("sp", agin_k[kch][:, col0:col0 + ncol], stb[si][:, 0:ncol], [sreg], [f"agin_k{kch}"], sreg)
                                else:
                                    dma("sp", agin_k[kch][:, NTOKP:NLOC], stb[si][:, 0:NMETA], [sreg], [f"agin_k{kch}"], sreg)
                                    dma("sp", kTs[rb0:rb0 + 128, :], stb[si][:, NMETA:NMETA + 128], [sreg], ["kTs"], sreg)
                        else:
                            chunk = (0 if cname == "g0" else 4) + sc
                            si = nxt("st", 3)
                            sreg = f"stf{si}"
                            greg = f"G{chunk}_{hr}"
                            P.add("act", lambda e, pa=pa, si=si, ncol=ncol: e.activation(out=stf[si][:, 0:ncol], in_=pa[:, 0:ncol],
                                                                                         func=AF.Exp, scale=-1.0), R=[pan], W=[sreg])
                            P.add("dve", lambda e, si=si, ncol=ncol: e.tensor_scalar(out=stf[si][:, 0:ncol], in0=stf[si][:, 0:ncol],
                                                                                     scalar1=1.0, scalar2=None, op0=ALU.add), R=[sreg], W=[sreg])
                            P.add("dve", lambda e, si=si, ncol=ncol: e.reciprocal(out=stf[si][:, 0:ncol], in_=stf[si][:, 0:ncol]),
                                  R=[sreg], W=[sreg])
                            P.add("dve", lambda e, pa=pa, si=si, ncol=ncol, chunk=chunk, col0=col0: e.tensor_tensor(
                                out=Gt[:, chunk, col0:col0 + ncol], in0=pa[:, 0:ncol], in1=stf[si][:, 0:ncol], op=ALU.mult),
                                R=[pan, sreg], W=[greg])
            if cname in ("dk", "dv", "fk", "fv"):
                odst = {"dk": o_dk, "dv": o_dv, "fk": o_fk, "fv": o_fv}[cname]
                for (row0, n, hr) in tiles():
                    bi = nxt("ps", 8)
                    pb = PS[bi]; pbn = f"ps{bi}"
                    for kc in range(8):
                        P.add("pe", lambda e, kc=kc, pb=pb, row0=row0, n=n, wb=wb: e.matmul(
                            pb[0:n, 0:512], lhsT=hT[:, kc, row0:row0 + n], rhs=wb[:, kc, 0:512],
                            start=(kc == 0), stop=(kc == 7)), R=[wreg, f"hT{hr}"], W=[pbn])
                    si = nxt("st", 3)
                    sreg = f"stf{si}"
                    evac_copy("act", stf[si][0:n, :], pb[0:n, 0:512], [pbn], [sreg])
                    dma("sp", odst[l, row0:row0 + n, :], stf[si][0:n, :], [sreg], ["out_" + cname], sreg)
                    if cname in ("dv", "fv"):
                        vi = nxt("vst", 2)
                        vreg = f"vst{vi}"
                        vv = vst[vi].rearrange("p (u c) -> p u c", u=4, c=129)
                        pv = pb[:, 0:512].rearrange("p (u t c) -> p u t c", u=4, t=2, c=64)
                        P.add("dve", lambda e, vv=vv, pv=pv, n=n: e.tensor_copy(out=vv[0:n, :, 0:64], in_=pv[0:n, :, 0, :]), R=[pbn], W=[vreg])
                        P.add("dve", lambda e, vv=vv, pv=pv, n=n: e.tensor_copy(out=vv[0:n, :, 65:129], in_=pv[0:n, :, 1, :]), R=[pbn], W=[vreg])
                        pb0 = 0 if cname == "dv" else 4
                        if row0 < NLOC:
                            for u_ in range(4):
                                dma("sp", agin_v[pb0 + u_][row0:row0 + n, :], vst[vi][0:n, u_ * 129:(u_ + 1) * 129], [vreg], [f"agin_v{pb0 + u_}"], vreg)
                        else:
                            dma("sp", vsn[0:n, pb0 * 129:(pb0 + 4) * 129], vst[vi][0:n, :], [vreg], ["vsn"], vreg)
            if cname == "fl":
                for (row0, n, hr) in tiles():
                    bi = nxt("ps", 8)
                    pb = PS[bi]; pbn = f"ps{bi}"
                    for kc in range(8):
                        P.add("pe", lambda e, kc=kc, pb=pb, row0=row0, n=n, wb=wb: e.matmul(
                            pb[0:n, 0:8], lhsT=hT[:, kc, row0:row0 + n], rhs=wb[:, kc, 0:8],
                            start=(kc == 0), stop=(kc == 7)), R=[wreg, f"hT{hr}"], W=[pbn])
                    si = nxt("st", 3)
                    sreg = f"stf{si}"
                    s8 = stf[si][0:n, 0:8]
                    P.add("dve", lambda e, s8=s8, pb=pb, n=n: e.tensor_tensor(out=s8, in0=pb[0:n, 0:8], in1=bfbc[0:n, :], op=ALU.add),
                          R=[pbn, "bfbc"], W=[sreg])
                    P.add("act", lambda e, s8=s8: e.activation(out=s8, in_=s8, func=AF.Exp, scale=-1.0), R=[sreg], W=[sreg])
                    P.add("act", lambda e, s8=s8: e.activation(out=s8, in_=s8, func=AF.Ln, bias=1.0), R=[sreg], W=[sreg])
                    P.add("dve", lambda e, s8=s8: e.tensor_scalar(out=s8, in0=s8, scalar1=-1.0, scalar2=None, op0=ALU.mult), R=[sreg], W=[sreg])
                    dma("sp", o_lf[l, row0:row0 + n, :], s8, [sreg], ["out_lf"], sreg)
                for (col0, ncol, hr) in tgs():
                    bi = nxt("ps", 8)
                    pa = PS[bi]; pan = f"ps{bi}"
                    for kc in range(8):
                        P.add("pe", lambda e, kc=kc, pa=pa, col0=col0, ncol=ncol, wb=wb: e.matmul(
                            pa[0:8, 0:ncol], lhsT=wb[:, kc, 0:8], rhs=hT[:, kc, col0:col0 + ncol],
                            start=(kc == 0), stop=(kc == 7)), R=[wreg, f"hT{hr}"], W=[pan])
                    lv = lfT[0:8, col0:col0 + ncol]
                    P.add("act", lambda e, lv=lv, pa=pa, ncol=ncol: e.activation(out=lv, in_=pa[0:8, 0:ncol], func=AF.Exp, scale=-1.0, bias=nbcol[:, 0:1]),
                          R=[pan, "nbcol"], W=["lfT"])
                    P.add("act", lambda e, lv=lv: e.activation(out=lv, in_=lv, func=AF.Ln, bias=1.0), R=["lfT"], W=["lfT"])
                    P.add("dve", lambda e, lv=lv: e.tensor_scalar(out=lv, in0=lv, scalar1=-1.0, scalar2=None, op0=ALU.mult), R=["lfT"], W=["lfT"])
                dma("sp", agin_lf[:, :], lfT[0:8, 0:NLOC], ["lfT"], ["agin_lf"], "c1")
            ag_after_chunk(l, cname)

    def allgather(l):
        pass

    def ag_after_chunk(l, cname):
        par = l % 2
        rg = [[0, 1, 2, 3], [4, 5, 6, 7]]
        lst = []
        if cname == "fl":
            lst.append((agin_lf, agout_lf[par], "agin_lf", f"agout_lf{par}"))
        elif cname == "dk":
            lst += [(agin_k[i], agout_k[par][i], f"agin_k{i}", f"agout_k{par}_{i}") for i in range(0, 4)]
        elif cname == "fk":
            lst += [(agin_k[i], agout_k[par][i], f"agin_k{i}", f"agout_k{par}_{i}") for i in range(4, 8)]
        elif cname == "dv":
            lst += [(agin_v[i], agout_v[par][i], f"agin_v{i}", f"agout_v{par}_{i}") for i in range(0, 4)]
        elif cname == "fv":
            lst += [(agin_v[i], agout_v[par][i], f"agin_v{i}", f"agout_v{par}_{i}") for i in range(4, 8)]
        for (src, dst, rn, wn) in lst:
            P.add("pool", lambda e, src=src, dst=dst: e.collective_compute("AllGather", ALU.bypass, replica_groups=rg,
                                                                           ins=[src.opt()], outs=[dst.opt()]),
                  R=[rn], W=[wn], cc=True)

    def lam_consts(l):
        lam_init = 0.8 - 0.6 * math.exp(-0.3 * l)
        for i, t in enumerate((lq1, lk1, lq2, lk2)):
            dma("sp", lamt[:, i * 64:(i + 1) * 64], t[l:l + 1, :].partition_broadcast(128), [], ["lamt"], "c1")
        dma("sp", gsub[:, :], subln_g[l:l + 1, :].rearrange("a b -> b a"), [], ["gsub"], "c1")
        P.add("dve", lambda e: e.scalar_tensor_tensor(out=lamt[:, 0:64], in0=lamt[:, 0:64], scalar=1.0, in1=lamt[:, 64:128],
                                                      op0=ALU.mult, op1=ALU.mult, accum_out=lamw[:, 0:1]), R=["lamt"], W=["lamt", "lamw"])
        P.add("dve", lambda e: e.scalar_tensor_tensor(out=lamt[:, 128:192], in0=lamt[:, 128:192], scalar=1.0, in1=lamt[:, 192:256],
                                                      op0=ALU.mult, op1=ALU.mult, accum_out=lamw[:, 1:2]), R=["lamt", "lamw"], W=["lamt", "lamw"])
        P.add("act", lambda e: e.activation(out=lamw[:, 2:4], in_=lamw[:, 0:2], func=AF.Exp), R=["lamw"], W=["lamw"])
        P.add("dve", lambda e: e.tensor_tensor(out=lamw[:, 4:5], in0=lamw[:, 3:4], in1=lamw[:, 2:3], op=ALU.subtract), R=["lamw"], W=["lamw"])
        P.add("dve", lambda e: e.tensor_scalar(out=lamw[:, 5:6], in0=lamw[:, 4:5], scalar1=-lam_init, scalar2=None, op0=ALU.add), R=["lamw"], W=["lamw"])
        P.add("dve", lambda e: e.tensor_scalar(out=gsub[:, :], in0=gsub[:, :], scalar1=(1.0 - lam_init), scalar2=None, op0=ALU.mult), R=["gsub"], W=["gsub"])

    def pieces(src, n_p, width, dstp, sreg, preg, neg):
        r = Fc[0:n_p, 0:width]
        P.add("dve", lambda e: e.tensor_scalar(out=r, in0=src, scalar1=(-1.0 if neg else 1.0), scalar2=None, op0=ALU.mult), R=[sreg], W=["Fc"])
        for k in range(3):
            P.add("dve", lambda e, k=k: e.tensor_copy(out=dstp[0:n_p, k, 0:width], in_=r), R=["Fc"], W=[preg])
            if k < 2:
                P.add("dve", lambda e, k=k: e.tensor_tensor(out=r, in0=r, in1=dstp[0:n_p, k, 0:width], op=ALU.subtract), R=["Fc", preg], W=["Fc"])

    def fox_F(l):
        par = l % 2
        P.add("pool", lambda e: e.memset(Fa[:, :], 0.0), W=["Fa"])
        P.add("pool", lambda e: e.memset(Fb[:, :], 0.0), W=["Fb"])
        P.add("pool", lambda e: e.memset(Fm[:, :], 1.0), W=["Fm"])
        Fm3 = Fm[:, 0:NTOKP].rearrange("p (j i) -> p j i", j=NSLOT, i=128)
        P.add("pool", lambda e: e.memset(Fm3[:, :, 0:1], 0.0), W=["Fm"])
        P.add("pool", lambda e: e.memset(Fm[:, NTOKP:NTOKP + 1], 0.0), W=["Fm"])
        for r_ in range(4):
            dma("sp", Fa[32 * r_:32 * r_ + 8, 0:NLOC], agout_lf[par][8 * r_:8 * r_ + 8, :], [f"agout_lf{par}"], ["Fa"], "f0")
        P.add("dve", lambda e: e.tensor_tensor_scan(out=Fb[:, 0:NLOC], data0=Fm[:, 0:NLOC], data1=Fa[:, 0:NLOC], initial=0.0,
                                                    op0=ALU.mult, op1=ALU.add), R=["Fa", "Fm"], W=["Fb"])
        Fb3 = Fb[:, 0:NTOKP].rearrange("p (j i) -> p j i", j=NSLOT, i=128)
        P.add("dve", lambda e: e.tensor_copy(out=Tt[:, 0:NSLOT], in_=Fb3[:, :, 127]), R=["Fb"], W=["Tt"])
        P.add("dve", lambda e: e.tensor_copy(out=Tt[:, 4 * NSLOT:4 * NSLOT + 1], in_=Fb[:, NLOC - 1:NLOC]), R=["Fb"], W=["Tt"])
        bi = nxt("ps", 8)
        pm = PS[bi]; pmn = f"ps{bi}"
        P.add("pe", lambda e: e.matmul(pm[:, 0:NSLOT], lhsT=Mall[:, :], rhs=Tt[:, 0:NSLOT], start=True, stop=True), R=["Tt", "cf32"], W=[pmn])
        P.add("pe", lambda e: e.matmul(pm[:, NSLOT:2 * NSLOT], lhsT=Mlow[:, :], rhs=Tt[:, 0:NSLOT], start=False, stop=True, skip_group_check=True),
              R=["Tt", "cf32"], W=[pmn])
        P.add("dve", lambda e: e.tensor_copy(out=Tt[:, NSLOT:3 * NSLOT], in_=pm[:, 0:2 * NSLOT]), R=[pmn], W=["Tt"])
        P.add("dve", lambda e: e.tensor_tensor_scan(out=Tt[:, 3 * NSLOT:4 * NSLOT], data0=ones_f[:, 0:NSLOT], data1=Tt[:, NSLOT:2 * NSLOT], initial=0.0,
                                                    op0=ALU.mult, op1=ALU.add), R=["Tt", "ones_f"], W=["Tt"])
        P.add("dve", lambda e: e.tensor_tensor(out=Tt[:, 3 * NSLOT:4 * NSLOT], in0=Tt[:, 3 * NSLOT:4 * NSLOT], in1=Tt[:, NSLOT:2 * NSLOT], op=ALU.subtract),
              R=["Tt"], W=["Tt"])
        P.add("dve", lambda e: e.tensor_tensor(out=Tt[:, 3 * NSLOT:4 * NSLOT], in0=Tt[:, 3 * NSLOT:4 * NSLOT], in1=Tt[:, 2 * NSLOT:3 * NSLOT], op=ALU.add),
              R=["Tt"], W=["Tt"])
        P.add("dve", lambda e: e.tensor_scalar(out=Tt[:, 3 * NSLOT:4 * NSLOT], in0=Tt[:, 3 * NSLOT:4 * NSLOT], scalar1=Tt[:, 4 * NSLOT:4 * NSLOT + 1], scalar2=None,
                                               op0=ALU.add), R=["Tt"], W=["Tt"])
        for j in range(NSLOT):
            P.add("dve", lambda e, j=j: e.tensor_scalar(out=Fb[:, j * 128:(j + 1) * 128], in0=Fb[:, j * 128:(j + 1) * 128],
                                                        scalar1=Tt[:, 3 * NSLOT + j:3 * NSLOT + j + 1], scalar2=None, op0=ALU.add), R=["Tt", "Fb"], W=["Fb"])
        c = 0
        while c < NLOC:
            w = min(512, NLOC - c)
            bi = nxt("ps", 8)
            pm2 = PS[bi]; pmn2 = f"ps{bi}"
            P.add("pe", lambda e, pm2=pm2, c=c, w=w: e.matmul(pm2[0:8, 0:w], lhsT=selmat[:, 0:8], rhs=Fb[:, c:c + w], start=True, stop=True),
                  R=["Fb", "cf32"], W=[pmn2])
            P.add("dve", lambda e, pm2=pm2, c=c, w=w: e.tensor_copy(out=Fq[0:8, c:c + w], in_=pm2[0:8, 0:w]), R=[pmn2], W=["Fq"])
            c += w
        pieces(Fb[:, 0:NLOC], 128, NLOC, Fp, "Fb", "Fp", True)
        for r_ in range(4):
            dma("sp", kauxf[:, 3:6, r_ * NLOC:(r_ + 1) * NLOC], Fp[32 * r_:32 * r_ + 8, :, 0:NLOC], ["Fp"], ["kauxf"], "f1")
        cl4w = clfs.rearrange("p (t s h) -> p t s h", t=NCT, s=4, h=8)
        nfull = LC // 128
        for s_ in range(4):
            dma("sp", cl4w[:, 0:nfull, s_, :], clf[l, s_, 0:nfull * 128, :].rearrange("(t p) h -> p t h", p=128), [], ["clfs"], "f0")
            if LC % 128:
                n = LC % 128
                dma("sp", cl4w[0:n, nfull, s_, :], clf[l, s_, nfull * 128:LC, :], [], ["clfs"], "f0")
        cl3 = clfs.rearrange("p (t c) -> p t c", t=NCT, c=32)
        t = 0
        while t < NCT:
            nt = min(4, NCT - t)
            bi = nxt("ps", 8)
            pm3 = PS[bi]; pmn3 = f"ps{bi}"
            tot = 0
            for tt in range(t, t + nt):
                n = min(128, LC - tt * 128)
                P.add("pe", lambda e, pm3=pm3, tt=tt, t=t, n=n: e.transpose(out=pm3[0:32, (tt - t) * 128:(tt - t) * 128 + n],
                                                                          in_=cl3[0:n, tt, :], identity=identf[0:n, 0:n]),
                      R=["clfs", "cf32"], W=[pmn3])
                tot = (tt - t) * 128 + n
            P.add("dve", lambda e, pm3=pm3, t=t, tot=tot: e.tensor_copy(out=Fa[0:32, t * 128:t * 128 + tot], in_=pm3[0:32, 0:tot]), R=[pmn3, "Fp"], W=["Fa"])
            t += nt
        for s_ in range(4):
            dma("sp", Fa[8 * s_:8 * s_ + 8, LC:LCN], lfT[0:8, NLOC + 32 * s_:NLOC + 32 * s_ + 32], ["lfT"], ["Fa"], "f0")
        P.add("pool", lambda e: e.memset(Fm[0:32, 0:LCN], 1.0), R=[], W=["Fm"])
        P.add("dve", lambda e: e.tensor_tensor_scan(out=Fb[0:32, 0:LCN], data0=Fm[0:32, 0:LCN], data1=Fa[0:32, 0:LCN], initial=0.0,
                                                    op0=ALU.mult, op1=ALU.add), R=["Fa", "Fm", "Fp"], W=["Fb"])
        for s_ in range(4):
            dma("sp", Fq[0:8, NLOC + 32 * s_:NLOC + 32 * s_ + 32], Fb[8 * s_:8 * s_ + 8, LC:LCN], ["Fb"], ["Fq"], "f0")
        pieces(Fb[0:32, 0:LCN], 32, LCN, Fp, "Fb", "Fp", True)
        dma("sp", kauxfs.rearrange("s h k n -> (s h) k n")[:, 3:6, :], Fp[0:32, :, 0:LCN], ["Fp"], ["kauxfs"], "f1")
        P.add("dve", lambda e: e.tensor_copy(out=Fqr[0:8, :], in_=Fq[0:8, :]), R=["Fq"], W=["Fqr"])
        for k in range(3):
            P.add("dve", lambda e, k=k: e.tensor_copy(out=Fqp[0:8, k, :], in_=Fqr[0:8, :]), R=["Fqr"], W=["Fqp"])
            if k < 2:
                P.add("dve", lambda e, k=k: e.tensor_tensor(out=Fqr[0:8, :], in0=Fqr[0:8, :], in1=Fqp[0:8, k, :], op=ALU.subtract), R=["Fqr", "Fqp"], W=["Fqr"])
        dma("sp", qauxf[:, 0:3, :], Fqp[0:8, :, :], ["Fqp"], ["qauxf"], "f1")

    ACC = [(PS[3], PS[4], "ps3", "ps4"), (PS[5], PS[6], "ps5", "ps6")]
    SB = [(PS[0], "ps0"), (PS[1], "ps1"), (PS[2], "ps2")]
    PM = (PS[7], "ps7")
    actr = {"acc": 0, "s": 0}

    def zero_bank(pbank, pname, width):
        P.add("pe", lambda e: e.matmul(pbank[:, 0:width], lhsT=zer_b[0:1, 0:128], rhs=zer_b[0:1, 128:128 + width], start=True, stop=True,
                                       skip_group_check=True), R=["zer_b"], W=[pname])

    def finalize_fox(X, Xn, co, N, odd, chunk, tc0, greg):
        pm, pmn = PM
        lrow, rb, tmp = fin["lrow"], fin["rb"], fin["tmp"]
        if not odd:
            P.add("dve", lambda e: e.tensor_copy(out=lrow[64:65, 0:N], in_=X[64:65, co:co + N]), R=[Xn], W=["fin_lrow"])
            P.add("pe", lambda e: e.matmul(pm[0:64, 0:N], lhsT=ones_f[64:65, 0:64], rhs=lrow[64:65, 0:N], start=True, stop=True),
                  R=["fin_lrow", "ones_f"], W=[pmn])
            lo, hi = 0, 64
        else:
            P.add("dve", lambda e: e.tensor_copy(out=lrow[32:64, 0:N], in_=X[32:64, co:co + N]), R=[Xn], W=["fin_lrow"])
            P.add("pe", lambda e: e.matmul(pm[0:128, 0:N], lhsT=sel63[32:64, 0:128], rhs=lrow[32:64, 0:N], start=True, stop=True),
                  R=["fin_lrow", "cf32"], W=[pmn])
            lo, hi = 64, 128
        P.add("dve", lambda e: e.reciprocal(out=rb[lo:hi, 0:N], in_=pm[lo:hi, 0:N]), R=[pmn], W=["fin_rb"])
        P.add("dve", lambda e: e.tensor_tensor(out=tmp[lo:hi, 0:N], in0=X[lo:hi, co:co + N], in1=rb[lo:hi, 0:N], op=ALU.mult),
              R=[Xn, "fin_rb"], W=["fin_tmp"])
        P.add("dve", lambda e: e.tensor_tensor(out=Gt[lo:hi, chunk, tc0:tc0 + N], in0=tmp[lo:hi, 0:N], in1=Gt[lo:hi, chunk, tc0:tc0 + N], op=ALU.mult),
              R=["fin_tmp", greg], W=[greg])

    def finalize_diff_comp(X, Y, Xn, Yn, co, N, odn):
        pm, pmn = PM
        lrow, rb = fin["lrow"], fin["rb"]
        od = fin[odn]
        P.add("dve", lambda e: e.tensor_copy(out=lrow[64:65, 0:N], in_=X[64:65, co:co + N]), R=[Xn], W=["fin_lrow"])
        P.add("pe", lambda e: e.matmul(pm[0:128, 0:N], lhsT=ones_f[64:65, 0:128], rhs=lrow[64:65, 0:N], start=True, stop=True),
              R=["fin_lrow", "ones_f"], W=[pmn])
        P.add("dve", lambda e: e.reciprocal(out=rb[:, 0:N], in_=pm[:, 0:N]), R=[pmn], W=["fin_rb"])
        P.add("dve", lambda e: e.tensor_tensor(out=od[0:64, 0:N], in0=X[0:64, co:co + N], in1=rb[0:64, 0:N], op=ALU.mult),
              R=[Xn, "fin_rb"], W=["fin_" + odn])
        P.add("dve", lambda e: e.tensor_tensor(out=od[64:128, 0:N], in0=Y[64:128, co:co + N], in1=rb[64:128, 0:N], op=ALU.mult),
              R=[Yn, "fin_rb"], W=["fin_" + odn])

    def finalize_diff_head(N, chunk, tc0, greg):
        pm, pmn = PM
        od1, od2, od, sq, t = fin["od1"], fin["od2"], fin["od"], fin["sq"], fin["t"]
        P.add("dve", lambda e: e.scalar_tensor_tensor(out=od[:, 0:N], in0=od2[:, 0:N], scalar=lamw[:, 5:6], in1=od1[:, 0:N],
                                                      op0=ALU.mult, op1=ALU.add), R=["fin_od1", "fin_od2", "lamw"], W=["fin_od"])
        P.add("pool", lambda e: e.tensor_tensor(out=sq[:, 0:N], in0=od[:, 0:N], in1=od[:, 0:N], op=ALU.mult), R=["fin_od"], W=["fin_sq"])
        P.add("pe", lambda e: e.matmul(pm[:, 0:N], lhsT=onesdiv[:, 0:128], rhs=sq[:, 0:N], start=True, stop=True), R=["fin_sq", "cf32"], W=[pmn])
        P.add("act", lambda e: e.activation(out=t[:, 0:N], in_=pm[:, 0:N], func=AF.Ln, bias=1e-5), R=[pmn], W=["fin_t"])
        P.add("act", lambda e: e.activation(out=t[:, 0:N], in_=t[:, 0:N], func=AF.Exp, scale=-0.5), R=["fin_t"], W=["fin_t"])
        P.add("dve", lambda e: e.tensor_tensor(out=t[:, 0:N], in0=t[:, 0:N], in1=od[:, 0:N], op=ALU.mult), R=["fin_t", "fin_od"], W=["fin_t"])
        P.add("dve", lambda e: e.scalar_tensor_tensor(out=Gt[:, chunk, tc0:tc0 + N], in0=t[:, 0:N], scalar=gsub[:, 0:1], in1=Gt[:, chunk, tc0:tc0 + N],
                                                      op0=ALU.mult, op1=ALU.mult), R=["fin_t", "gsub", greg], W=[greg])

    def attn_qk(kT, kreg, nk, qlist, masks):
        sbank, sname = SB[actr["s"] % 3]
        actr["s"] += 1
        first = True
        wtot = 0
        for (qa, qreg, sc0, N) in qlist:
            P.add("pe", lambda e, qa=qa, sc0=sc0, N=N, first=first: e.matmul(sbank[0:nk, sc0:sc0 + N], lhsT=kT, rhs=qa, start=first, stop=False,
                                                                             skip_group_check=True), R=[kreg, qreg], W=[sname])
            first = False
            wtot = max(wtot, sc0 + N)
        for (ma, sc0) in masks:
            w = ma.shape[-1]
            P.add("pe", lambda e, ma=ma, sc0=sc0, w=w: e.matmul(sbank[0:nk, sc0:sc0 + w], lhsT=ident[0:nk, 0:nk], rhs=ma, start=False, stop=True,
                                                                skip_group_check=True), R=["mk", "ident"], W=[sname])
        pi = nxt("pt", 4)
        pt = Pt[pi]; ptn = f"Pt{pi}"
        P.add("act", lambda e: e.activation(out=pt[0:nk, 0:wtot], in_=sbank[0:nk, 0:wtot], func=AF.Exp), R=[sname], W=[ptn])
        return (pt, ptn)

    def attn_pv(vlist, st, nk):
        pt, ptn = st
        for (va, vreg, acc, accn, sc0, N) in vlist:
            P.add("pe", lambda e, va=va, acc=acc, sc0=sc0, N=N: e.matmul(acc, lhsT=va, rhs=pt[0:nk, sc0:sc0 + N], start=False, stop=True,
                                                                         skip_group_check=True), R=[vreg, ptn], W=[accn])

    def run_pipeline(items, LA=2):
        pend = []
        for it in items:
            for f in it.get("pre", ()):
                f()
            st = it["qk"]()
            pend.append((it, st))
            if len(pend) > LA:
                it0, st0 = pend.pop(0)
                for f in it0.get("prepv", ()):
                    f()
                it0["pv"](st0)
                for f in it0.get("post", ()):
                    f()
        while pend:
            it0, st0 = pend.pop(0)
            for f in it0.get("prepv", ()):
                f()
            it0["pv"](st0)
            for f in it0.get("post", ()):
                f()

    def prompt_attention(l):
        par = l % 2
        kcount = 0
        items = []

        def mk_loads_v(vp, vi):
            vreg = f"Vt{vi}"
            Vv = Vt[vi].rearrange("p (u c) -> p u c", u=VU, c=129)
            av = agout_v[par][vp]

            def f():
                for r_ in range(4):
                    src = av[r_ * NLOC:r_ * NLOC + NTOKP, :].rearrange("(j p) c -> p j c", p=128)
                    dma("sp", Vv[:, r_ * NSLOT:(r_ + 1) * NSLOT, :], src, [f"agout_v{par}_{vp}"], [vreg], vreg)
                dma("sp", Vv[0:NMETA, 4 * NSLOT, :], av[NTOKP:NLOC, :], [f"agout_v{par}_{vp}"], [vreg], vreg)
            return f

        def mk_loads_kq(vp, comp, ki, isdiff):
            kreg = f"Kt{ki}"; qreg = f"Qt{ki}"
            K3 = Kt[ki][:, 0:4 * NLOC].rearrange("p (r n) -> p r n", r=4, n=NLOC)

            def f():
                akc = agout_k[par][comp // 2].rearrange("(r k) n -> k r n", r=4, k=128)
                dma("sp", K3[0:64, :, :], akc[(comp % 2) * 64:(comp % 2) * 64 + 64, :, :], [f"agout_k{par}_{comp // 2}"], [kreg], kreg)
                if isdiff:
                    dma("sp", Kt[ki][64:70, 0:4 * NLOC], c_kauxd[vp, :, :], [], [kreg], kreg)
                    dma("sp", Qt[ki][64:70, 0:NLOC], c_qauxd[vp, :, 0:NLOC], [], [qreg], qreg)
                else:
                    h = comp - 8
                    dma("sp", Kt[ki][64:70, 0:4 * NLOC], kauxf[h, :, :], ["kauxf"], [kreg], kreg)
                    dma("sp", Qt[ki][64:70, 0:NLOC], qauxf[h, :, 0:NLOC], ["qauxf"], [qreg], qreg)
                dma("sp", Qt[ki][0:64, 0:NLOC], qT[comp * 64:(comp + 1) * 64, 0:NLOC], ["qT"], [qreg], qreg)
            return f

        def mk_block(kT, kreg, nk, qa, qreg, qn, masks, vl):
            return dict(qk=lambda: attn_qk(kT, kreg, nk, [(qa, qreg, 0, qn)], masks),
                        pv=lambda st: attn_pv(vl, st, nk))

        def mk_final(isdiff, ci, odd, X, Y, Xn, Yn, QN, chunk, q0, greg, g):
            def f():
                if isdiff:
                    finalize_diff_comp(X, Y, Xn, Yn, 0, QN, "od1" if ci == 0 else "od2")
                    if ci == 0:
                        P.add("pool", lambda e: e.tensor_copy(out=od1g[g][:, 0:QN], in_=fin["od1"][:, 0:QN]), R=["fin_od1"], W=[f"od1g{g}"])
                    else:
                        P.add("pool", lambda e: e.tensor_copy(out=fin["od1"][:, 0:QN], in_=od1g[g][:, 0:QN]), R=[f"od1g{g}"], W=["fin_od1"])
                        finalize_diff_head(QN, chunk, q0, greg)
                else:
                    finalize_fox(X, Xn, 0, QN, odd, chunk, q0, greg)
            return f

        for vp in range(8):
            vi = vp % 2
            vreg = f"Vt{vi}"
            Vv = Vt[vi].rearrange("p (u c) -> p u c", u=VU, c=129)
            isdiff = vp < 4
            comps = (2 * vp, 2 * vp + 1) if isdiff else (8 + 2 * (vp - 4), 8 + 2 * (vp - 4) + 1)
            first_of_vp = True
            for ci, comp in enumerate(comps):
                ki = kcount % 2
                kcount += 1
                kreg = f"Kt{ki}"; qreg = f"Qt{ki}"
                K3 = Kt[ki][:, 0:4 * NLOC].rearrange("p (r n) -> p r n", r=4, n=NLOC)
                h = vp if isdiff else comp - 8
                odd = (not isdiff) and (ci == 1)
                chunk = vp
                first_of_comp = True
                for g in range(NQG + 1):
                    X, Y, Xn, Yn = ACC[actr["acc"] % 2]
                    actr["acc"] += 1
                    meta_q = (g == NQG)
                    if meta_q:
                        q0, QN = NTOKP, NMETA
                    else:
                        q0, QN = g * GS * 128, GS * 128
                    pre = []
                    if first_of_vp:
                        pre.append(mk_loads_v(vp, vi))
                        first_of_vp = False
                    if first_of_comp:
                        pre.append(mk_loads_kq(vp, comp, ki, isdiff))
                        first_of_comp = False
                    prepv = [lambda X=X, Xn=Xn, QN=QN: zero_bank(X, Xn, QN)]
                    if isdiff:
                        prepv.append(lambda Y=Y, Yn=Yn, QN=QN: zero_bank(Y, Yn, QN))
                    blocks = [("meta", 0, NSLOT)]
                    if not meta_q:
                        for j in range((g + 1) * GS):
                            for r_ in range(4):
                                blocks.append(("blk", r_, j))
                    grp_items = []
                    for (kind, r_, j) in blocks:
                        if kind == "meta":
                            nk = NMETA
                            kT = K3[0:70, 0, NTOKP:NLOC]
                            vb = 4 * NSLOT
                            qs, qn = 0, QN
                            masks = []
                            if meta_q:
                                mo = 0 if not isdiff else 16 * (1 + h)
                                masks = [(mksm[0:NMETA, mo:mo + NMETA], 0)]
                        else:
                            nk = 128
                            kT = K3[0:70, r_, j * 128:(j + 1) * 128]
                            vb = r_ * NSLOT + j
                            i = j - g * GS
                            masks = []
                            if i < 0:
                                qs, qn = 0, QN
                            else:
                                qs, qn = i * 128, QN - i * 128
                                masks = [((mkfox[:, r_, :] if not isdiff else mkdiff[:, h * 4 + r_, :]), 0)]
                        qa = Qt[ki][0:70, q0 + qs:q0 + qs + qn]
                        if isdiff:
                            vl = [(Vv[0:nk, vb, 0:65], vreg, X[0:65, qs:qs + qn], Xn, 0, qn),
                                  (Vv[0:nk, vb, 1:129], vreg, Y[0:128, qs:qs + qn], Yn, 0, qn)]
                        elif not odd:
                            vl = [(Vv[0:nk, vb, 0:65], vreg, X[0:65, qs:qs + qn], Xn, 0, qn)]
                        else:
                            vl = [(Vv[0:nk, vb, 1:129], vreg, X[0:128, qs:qs + qn], Xn, 0, qn)]
                        grp_items.append(mk_block(kT, kreg, nk, qa, qreg, qn, masks, vl))
                    grp_items[0]["pre"] = pre
                    grp_items[0]["prepv"] = prepv
                    hr = NQG if meta_q else g
                    greg = f"G{chunk}_{hr}"
                    grp_items[-1]["post"] = [mk_final(isdiff, ci, odd, X, Y, Xn, Yn, QN, chunk, q0, greg, g)]
                    items.extend(grp_items)
        run_pipeline(items)

    diff_stash = {}
    od1g = [sb(f"od1g{g}", [128, 512 if g < NQG else NMETA], F32) for g in range(NQG + 1)]

    def sample_attention(l):
        K8 = Kall[:, 0:8 * LCN].rearrange("p (c n) -> p c n", c=8, n=LCN)
        NBs = NCT + 1
        Vs = Vall[:, 0:NBs * 4 * 129].rearrange("p (t u c) -> p t u c", t=NBs, u=4, c=129)
        kregs = ["Kt0", "Kt1"]; vregs = ["Vt0", "Vt1"]
        P.add("pool", lambda e: e.memset(Vs[:, :, :, 64:65], 1.0), W=vregs)
        Q8 = Qt[0][:, 0:8 * 32].rearrange("p (c n) -> p c n", c=8, n=32)
        items = []

        def mk_loads(s, hf):
            ck = cdk if hf == 0 else cfk
            cv = cdv if hf == 0 else cfv

            def f():
                for t in range(NCT):
                    n = min(128, LC - t * 128)
                    ci_ = nxt("cst", 4)
                    dma("sp", cstf[ci_][0:n, :], ck[l, s, t * 128:t * 128 + n, :], [], [f"cstf{ci_}"], f"cstf{ci_}")
                    P.add("act", lambda e, ci_=ci_, n=n: e.activation(out=cstb[ci_][0:n, :], in_=cstf[ci_][0:n, :], func=AF.Copy),
                          R=[f"cstf{ci_}"], W=[f"cstb{ci_}"])
                    pm, pmn = PM
                    tp = pm[:, :].bitcast(BF16)
                    for c in range(8):
                        P.add("pe", lambda e, c=c, ci_=ci_, n=n, tp=tp: e.transpose(out=tp[0:64, c * 128:c * 128 + n], in_=cstb[ci_][0:n, c * 64:(c + 1) * 64],
                                                                                    identity=ident[0:n, 0:n]), R=[f"cstb{ci_}", "ident"], W=[pmn])
                    tpv = tp.rearrange("p (c n) -> p c n", c=8, n=128)
                    P.add("dve", lambda e, t=t, n=n, tpv=tpv: e.tensor_copy(out=K8[0:64, :, t * 128:t * 128 + n], in_=tpv[0:64, :, 0:n]), R=[pmn], W=kregs)
                for c in range(8):
                    comp = hf * 8 + c
                    dma("sp", K8[0:64, c, LC:LCN], kTs[comp * 64:(comp + 1) * 64, 32 * s:32 * s + 32], ["kTs"], kregs, "Kt0")
                    if hf == 0:
                        dma("sp", K8[64:70, c, :], c_kauxds[c // 2, :, :], [], kregs, "Kt0")
                        dma("sp", Q8[64:70, c, :], c_qauxd[c // 2, :, NLOC + 32 * s:NLOC + 32 * s + 32], [], ["Qt0"], "Qt0")
                    else:
                        dma("sp", K8[64:70, c, :], kauxfs[s, c, :, :], ["kauxfs"], kregs, "Kt0")
                        dma("sp", Q8[64:70, c, :], qauxf[c, :, NLOC + 32 * s:NLOC + 32 * s + 32], ["qauxf"], ["Qt0"], "Qt0")
                    dma("sp", Q8[0:64, c, :], qT[comp * 64:(comp + 1) * 64, NLOC + 32 * s:NLOC + 32 * s + 32], ["qT"], ["Qt0"], "Qt0")
                for t in range(NCT):
                    n = min(128, LC - t * 128)
                    ci_ = nxt("cst", 4)
                    dma("sp", cstf[ci_][0:n, :], cv[l, s, t * 128:t * 128 + n, :], [], [f"cstf{ci_}"], f"cstf{ci_}")
                    cvw = cstf[ci_].rearrange("p (u t c) -> p u t c", u=4, t=2, c=64)
                    P.add("pool", lambda e, t=t, n=n, cvw=cvw: e.tensor_copy(out=Vs[0:n, t, :, 0:64], in_=cvw[0:n, :, 0, :]), R=[f"cstf{ci_}"], W=vregs)
                    P.add("pool", lambda e, t=t, n=n, cvw=cvw: e.tensor_copy(out=Vs[0:n, t, :, 65:129], in_=cvw[0:n, :, 1, :]), R=[f"cstf{ci_}"], W=vregs)
                dma("sp", Vs[0:32, NCT, :, :], vsn[32 * s:32 * s + 32, hf * 4 * 129:(hf + 1) * 4 * 129].rearrange("p (u c) -> p u c", u=4, c=129),
                    ["vsn"], vregs, "Vt0")
            return f

        def mk_sblock(hf, t, X, Y, Xn, Yn):
            if t < NCT:
                nk = min(128, LC - t * 128)
                k0 = t * 128
            else:
                nk = 32
                k0 = LC

            def qk():
                sbk, sn = SB[actr["s"] % 3]
                actr["s"] += 1
                for c in range(8):
                    P.add("pe", lambda e, c=c: e.matmul(sbk[0:nk, c * 32:(c + 1) * 32], lhsT=K8[0:70, c, k0:k0 + nk], rhs=Q8[0:70, c, :],
                                                       start=(c == 0), stop=False, skip_group_check=True), R=kregs + ["Qt0"], W=[sn])
                if t == NCT:
                    for c in range(8):
                        mo = 80 + (0 if hf == 1 else 32 * (1 + c // 2))
                        P.add("pe", lambda e, c=c, mo=mo: e.matmul(sbk[0:nk, c * 32:(c + 1) * 32], lhsT=ident[0:nk, 0:nk], rhs=mksm[0:32, mo:mo + 32],
                                                                   start=False, stop=True, skip_group_check=True), R=["mk", "ident"], W=[sn])
                pi = nxt("pt", 4)
                pt = Pt[pi]; ptn = f"Pt{pi}"
                P.add("act", lambda e: e.activation(out=pt[0:nk, 0:256], in_=sbk[0:nk, 0:256], func=AF.Exp), R=[sn], W=[ptn])
                return (pt, ptn)

            def pv(st):
                pt, ptn = st
                for c in range(8):
                    u = c // 2
                    cs = slice(c * 32, (c + 1) * 32)
                    if hf == 0:
                        P.add("pe", lambda e, u=u, cs=cs: e.matmul(X[0:65, cs], lhsT=Vs[0:nk, t, u, 0:65], rhs=pt[0:nk, cs], start=False, stop=True,
                                                                   skip_group_check=True), R=vregs + [ptn], W=[Xn])
                        P.add("pe", lambda e, u=u, cs=cs: e.matmul(Y[0:128, cs], lhsT=Vs[0:nk, t, u, 1:129], rhs=pt[0:nk, cs], start=False, stop=True,
                                                                   skip_group_check=True), R=vregs + [ptn], W=[Yn])
                    elif c % 2 == 0:
                        P.add("pe", lambda e, u=u, cs=cs: e.matmul(X[0:65, cs], lhsT=Vs[0:nk, t, u, 0:65], rhs=pt[0:nk, cs], start=False, stop=True,
                                                                   skip_group_check=True), R=vregs + [ptn], W=[Xn])
                    else:
                        P.add("pe", lambda e, u=u, cs=cs: e.matmul(X[0:128, cs], lhsT=Vs[0:nk, t, u, 1:129], rhs=pt[0:nk, cs], start=False, stop=True,
                                                                   skip_group_check=True), R=vregs + [ptn], W=[Xn])
            return dict(qk=qk, pv=pv)

        def mk_sfinal(s, hf, X, Y, Xn, Yn):
            tc0 = NLOC + 32 * s
            pm, pmn = PM
            lrow, rb, tmp = fin["lrow"], fin["rb"], fin["tmp"]
            od1, od, sq, t_ = fin["od1"], fin["od"], fin["sq"], fin["t"]

            def v4(ap, n=4):
                return ap.rearrange("p (c n) -> p c n", c=n, n=32)

            def f():
                if hf == 1:
                    Xv = X[:, 0:256].rearrange("p (c two n) -> p c two n", c=4, two=2, n=32)
                    gregs = [f"G{4 + i}_{NQG}" for i in range(4)]
                    for par_ in range(2):
                        xv = Xv[:, :, par_, :]
                        if par_ == 0:
                            P.add("dve", lambda e, xv=xv: e.tensor_copy(out=v4(lrow[64:65, 0:128]), in_=xv[64:65]), R=[Xn], W=["fin_lrow"])
                            P.add("pe", lambda e: e.matmul(pm[0:64, 0:128], lhsT=ones_f[64:65, 0:64], rhs=lrow[64:65, 0:128], start=True, stop=True),
                                  R=["fin_lrow", "ones_f"], W=[pmn])
                            lo, hi = 0, 64
                        else:
                            P.add("dve", lambda e, xv=xv: e.tensor_copy(out=v4(lrow[32:64, 0:128]), in_=xv[32:64]), R=[Xn], W=["fin_lrow"])
                            P.add("pe", lambda e: e.matmul(pm[0:128, 0:128], lhsT=sel63[32:64, 0:128], rhs=lrow[32:64, 0:128], start=True, stop=True),
                                  R=["fin_lrow", "cf32"], W=[pmn])
                            lo, hi = 64, 128
                        P.add("dve", lambda e, lo=lo, hi=hi: e.reciprocal(out=rb[lo:hi, 0:128], in_=pm[lo:hi, 0:128]), R=[pmn], W=["fin_rb"])
                        P.add("dve", lambda e, lo=lo, hi=hi, xv=xv: e.tensor_tensor(out=v4(tmp[lo:hi, 0:128]), in0=xv[lo:hi], in1=v4(rb[lo:hi, 0:128]), op=ALU.mult),
                              R=[Xn, "fin_rb"], W=["fin_tmp"])
                        P.add("dve", lambda e, lo=lo, hi=hi: e.tensor_tensor(out=Gt[lo:hi, 4:8, tc0:tc0 + 32], in0=v4(tmp[lo:hi, 0:128]),
                                                                             in1=Gt[lo:hi, 4:8, tc0:tc0 + 32], op=ALU.mult),
                              R=["fin_tmp"] + gregs, W=gregs)
                else:
                    gregs = [f"G{i}_{NQG}" for i in range(4)]
                    P.add("dve", lambda e: e.tensor_copy(out=lrow[64:65, 0:256], in_=X[64:65, 0:256]), R=[Xn], W=["fin_lrow"])
                    P.add("pe", lambda e: e.matmul(pm[0:128, 0:256], lhsT=ones_f[64:65, 0:128], rhs=lrow[64:65, 0:256], start=True, stop=True),
                          R=["fin_lrow", "ones_f"], W=[pmn])
                    P.add("dve", lambda e: e.reciprocal(out=rb[:, 0:256], in_=pm[:, 0:256]), R=[pmn], W=["fin_rb"])
                    P.add("dve", lambda e: e.tensor_tensor(out=od1[0:64, 0:256], in0=X[0:64, 0:256], in1=rb[0:64, 0:256], op=ALU.mult),
                          R=[Xn, "fin_rb"], W=["fin_od1"])
                    P.add("dve", lambda e: e.tensor_tensor(out=od1[64:128, 0:256], in0=Y[64:128, 0:256], in1=rb[64:128, 0:256], op=ALU.mult),
                          R=[Yn, "fin_rb"], W=["fin_od1"])
                    o4 = od1[:, 0:256].rearrange("p (c two n) -> p c two n", c=4, two=2, n=32)
                    P.add("dve", lambda e: e.scalar_tensor_tensor(out=v4(od[:, 0:128]), in0=o4[:, :, 1, :], scalar=lamw[:, 5:6], in1=o4[:, :, 0, :],
                                                                  op0=ALU.mult, op1=ALU.add), R=["fin_od1", "lamw"], W=["fin_od"])
                    P.add("pool", lambda e: e.tensor_tensor(out=sq[:, 0:128], in0=od[:, 0:128], in1=od[:, 0:128], op=ALU.mult), R=["fin_od"], W=["fin_sq"])
                    P.add("pe", lambda e: e.matmul(pm[:, 0:128], lhsT=onesdiv[:, 0:128], rhs=sq[:, 0:128], start=True, stop=True), R=["fin_sq", "cf32"], W=[pmn])
                    P.add("act", lambda e: e.activation(out=t_[:, 0:128], in_=pm[:, 0:128], func=AF.Ln, bias=1e-5), R=[pmn], W=["fin_t"])
                    P.add("act", lambda e: e.activation(out=t_[:, 0:128], in_=t_[:, 0:128], func=AF.Exp, scale=-0.5), R=["fin_t"], W=["fin_t"])
                    P.add("dve", lambda e: e.tensor_tensor(out=t_[:, 0:128], in0=t_[:, 0:128], in1=od[:, 0:128], op=ALU.mult), R=["fin_t", "fin_od"], W=["fin_t"])
                    P.add("dve", lambda e: e.scalar_tensor_tensor(out=Gt[:, 0:4, tc0:tc0 + 32], in0=v4(t_[:, 0:128]), scalar=gsub[:, 0:1],
                                                                  in1=Gt[:, 0:4, tc0:tc0 + 32], op0=ALU.mult, op1=ALU.mult),
                          R=["fin_t", "gsub"] + gregs, W=gregs)
            return f

        for s in range(4):
            for hf in range(2):
                X, Y, Xn, Yn = ACC[actr["acc"] % 2]
                actr["acc"] += 1
                grp = [mk_sblock(hf, t, X, Y, Xn, Yn) for t in range(NCT + 1)]
                grp[0]["pre"] = [mk_loads(s, hf)]
                prepv = [lambda X=X, Xn=Xn: zero_bank(X, Xn, 256)]
                if hf == 0:
                    prepv.append(lambda Y=Y, Yn=Yn: zero_bank(Y, Yn, 256))
                grp[0]["prepv"] = prepv
                grp[-1]["post"] = [mk_sfinal(s, hf, X, Y, Xn, Yn)]
                run_pipeline(grp)
        for i in range(2):
            vv = Vt[i].rearrange("p (u c) -> p u c", u=VU, c=129)
            P.add("pool", lambda e, vv=vv: e.memset(vv[:, :, 64:65], 1.0), W=[f"Vt{i}"])

    def out_proj(l):
        dma("pool", wout[:, :, :], w_out[l].rearrange("(kc p) n -> p kc n", p=128), [], ["wout"], "wout")
        last = (l == DEPTH - 1)
        load_layer_consts(l + 1)
        for (row0, n, hr) in tiles():
            xi = nxt("xt", 2)
            xreg = f"xt{xi}"
            dma("sp", xt[xi][0:n, :], x_src(l, row0, n), ["xscr"], [xreg], xreg)
            for half in range(2):
                bi = nxt("ps", 8)
                pb = PS[bi]; pbn = f"ps{bi}"
                for kc in range(8):
                    P.add("pe", lambda e, kc=kc, pb=pb, row0=row0, n=n, half=half: e.matmul(
                        pb[0:n, 0:512], lhsT=Gt[:, kc, row0:row0 + n], rhs=wout[:, kc, half * 512:(half + 1) * 512],
                        start=(kc == 0), stop=(kc == 7)), R=["wout", f"G{kc}_{hr}"], W=[pbn])
                P.add("dve", lambda e, pb=pb, xi=xi, n=n, half=half: e.tensor_tensor(out=xt[xi][0:n, half * 512:(half + 1) * 512],
                                                                                    in0=pb[0:n, 0:512], in1=xt[xi][0:n, half * 512:(half + 1) * 512], op=ALU.add),
                      R=[pbn, xreg], W=[xreg])
            if not last:
                dma("sp", xscr[row0:row0 + n, :], xt[xi][0:n, :], [xreg], ["xscr"], xreg)
                norm_tile(xt[xi], n, xreg, row0, f"hT{hr}", gbc, "gbc")
            else:
                if row0 >= NTOKP and row0 < NLOC:
                    continue
                P.add("dve", lambda e, xi=xi, n=n: e.scalar_tensor_tensor(out=junk[0:n, :], in0=xt[xi][0:n, :], scalar=1.0, in1=xt[xi][0:n, :],
                                                                          op0=ALU.mult, op1=ALU.mult, accum_out=sml[0:n, 0:1]), R=[xreg], W=["junk", "sml0"])
                P.add("act", lambda e, n=n: e.activation(out=sml[0:n, 1:2], in_=sml[0:n, 0:1], func=AF.Ln, scale=1.0 / D, bias=1e-6), R=["sml0"], W=["sml1"])
                P.add("act", lambda e, n=n: e.activation(out=sml[0:n, 2:3], in_=sml[0:n, 1:2], func=AF.Exp, scale=-0.5), R=["sml1"], W=["sml2"])
                P.add("dve", lambda e, xi=xi, n=n: e.scalar_tensor_tensor(out=xt[xi][0:n, :], in0=xt[xi][0:n, :], scalar=sml[0:n, 2:3], in1=gbc[0:n, :],
                                                                          op0=ALU.mult, op1=ALU.mult), R=[xreg, "sml2", "gbc"], W=[xreg])
                dst = yp[row0:row0 + n, :] if row0 < NTOKP else ys[0:n, :]
                dma("sp", dst, xt[xi][0:n, :], [xreg], ["yout"], xreg)

    import os as _os
    _stop = _os.environ.get("K_STOP", "")
    for l in range(DEPTH):
        P.epoch = l
        projections(l)
        if _stop == "proj":
            break
        allgather(l)
        if _stop == "ag":
            break
        lam_consts(l)
        fox_F(l)
        if _stop == "F":
            break
        prompt_attention(l)
        if _stop == "patt":
            break
        sample_attention(l)
        if _stop == "satt":
            break
        out_proj(l)
    P.finish()

    sems = {}
    for k in P.cnt:
        sems[k] = es.enter_context(nc.semaphore("s_" + "_".join(str(x) for x in k)))

    def emit(ename, e):
        for (fn, waits, key) in P.ops[ename]:
            for (k, i) in waits:
                e.wait_ge(sems[k], i * (16 if k[0] == "dma" else 1))
            if fn is None:
                continue
            ins = fn(e)
            if key[0] == "dma":
                ins.then_inc(sems[key], 16)
            elif key[0] == "cc":
                ins.then_inc(sems[key])
            else:
                ins.then_inc(sems[key], 1)

    with nc.Block() as block:
        @block.tensor
        def _(e):
            emit("pe", e)

        @block.scalar
        def _(e):
            emit("act", e)

        @block.vector
        def _(e):
            emit("dve", e)

        @block.gpsimd
        def _(e):
            emit("pool", e)

        @block.sync
        def _(e):
            emit("sp", e)
    es.close()
    return nc


def _bf(a):
    return np.asarray(a, dtype=np.float32).astype(ml_dtypes.bfloat16)


def _host_tables(cfg, r):
    NSLOT = cfg["NSLOT"]; LC = cfg["LC"]; PAST = cfg["PAST"]
    NTOKP = NSLOT * 128; NLOC = NTOKP + NMETA; NTOT = NLOC + 128; LCN = LC + 32

    def pos_local(rank):
        pa = np.zeros(NLOC); pb = np.zeros(NLOC)
        for j in range(NSLOT):
            B = 4 * j + rank
            pa[j * 128:(j + 1) * 128] = 128 * B
            pb[j * 128:(j + 1) * 128] = 16 + np.arange(128)
        pa[NTOKP:] = 0
        pb[NTOKP:] = np.arange(NMETA)
        return pa, pb

    pa_q = np.zeros(NTOT); pb_q = np.zeros(NTOT)
    pa_q[:NLOC], pb_q[:NLOC] = pos_local(r)
    for s in range(4):
        pa_q[NLOC + 32 * s:NLOC + 32 * s + 32] = PAST
        pb_q[NLOC + 32 * s:NLOC + 32 * s + 32] = 16 + np.arange(32)
    pa_k = np.concatenate([pos_local(rr)[0] for rr in range(4)])
    pb_k = np.concatenate([pos_local(rr)[1] for rr in range(4)])
    idx = np.arange(LCN)
    pa_s = np.where(idx < LC, 128 * (idx // 128), PAST).astype(np.float64)
    pb_s = np.where(idx < LC, idx % 128, 16 + (idx - LC)).astype(np.float64)
    qauxd = np.zeros((4, 6, NTOT), np.float32); kauxd = np.zeros((4, 6, 4 * NLOC), np.float32); kauxds = np.zeros((4, 6, LCN), np.float32)
    for h in range(4):
        s_ = SLOPES[h]
        qauxd[h, 0] = -s_ * pa_q; qauxd[h, 1] = -s_ * pb_q; qauxd[h, 2] = 1; qauxd[h, 3] = 1
        kauxd[h, 0] = 1; kauxd[h, 1] = 1; kauxd[h, 2] = s_ * pa_k; kauxd[h, 3] = s_ * pb_k
        kauxds[h, 0] = 1; kauxds[h, 1] = 1; kauxds[h, 2] = s_ * pa_s; kauxds[h, 3] = s_ * pb_s
    k = np.arange(128)[:, None]; q = np.arange(128)[None, :]
    mkfox = np.zeros((128, 4, 128), np.float32); mkdiff = np.zeros((128, 16, 128), np.float32)
    for rr in range(4):
        if rr > r:
            mkfox[:, rr, :] = NEGM
        elif rr == r:
            mkfox[:, rr, :] = np.where(k > q, NEGM, 0.0)
        for h in range(4):
            if rr > r:
                mkdiff[:, h * 4 + rr, :] = NEGM
            elif rr == r:
                m = np.where((k // 64) > (q // 64), NEGM, np.where(k > q, -2.0 * SLOPES[h] * (k - q), 0.0))
                mkdiff[:, h * 4 + rr, :] = m
    mksm = np.zeros((32, 240), np.float32)
    k16 = np.arange(16)[:, None]; q16 = np.arange(16)[None, :]
    mksm[0:16, 0:16] = np.where(k16 > q16, NEGM, 0.0)
    k32 = np.arange(32)[:, None]; q32 = np.arange(32)[None, :]
    mksm[0:32, 80:112] = np.where(k32 > q32, NEGM, 0.0)
    for h in range(4):
        mksm[0:16, 16 * (1 + h):16 * (2 + h)] = np.where(k16 > q16, -2.0 * SLOPES[h] * (k16 - q16), 0.0)
        mksm[0:32, 80 + 32 * (1 + h):80 + 32 * (2 + h)] = np.where(k32 > q32, -2.0 * SLOPES[h] * (k32 - q32), 0.0)
    cf = np.zeros((128, 648), np.float32)
    cf[:, 0:128] = np.eye(128)
    cf[63, 128 + 64:128 + 128] = 1.0
    cf[:, 256:384] = 1.0 / 128.0
    for h in range(8):
        cf[32 * r + h, 640 + h] = 1.0
        for r2 in range(4):
            for r1 in range(4):
                cf[32 * r2 + h, 384 + 32 * r1 + h] = 1.0
                if r2 < r1:
                    cf[32 * r2 + h, 512 + 32 * r1 + h] = 1.0
    return dict(c_ident=_bf(np.eye(128)), c_f32=cf, c_mkfox=_bf(mkfox.reshape(128, -1)), c_mkdiff=_bf(mkdiff.reshape(128, -1)),
                c_mksm=_bf(mksm), c_qauxd=_bf(qauxd), c_kauxd=_bf(kauxd), c_kauxds=_bf(kauxds),
                c_ones=_bf(np.ones((8, 3, max(NLOC, LCN)), np.float32)))


_NC_CACHE = {}


def kernel(x_prompt, x_sample, cache_diff_k, cache_diff_v, cache_fox_k, cache_fox_v, cache_fox_logf,
           meta_tokens, w_in, b_forget, norm_g, w_out, lambda_q1, lambda_k1, lambda_q2, lambda_k2,
           subln_g, final_norm_g):
    f32 = np.float32
    x_prompt = np.asarray(x_prompt, f32); x_sample = np.asarray(x_sample, f32)
    B, SEQ, _ = x_prompt.shape
    DEPTH = w_in.shape[0]
    LC = cache_diff_k.shape[2]
    assert B == 2 and x_sample.shape[0] == 32 and x_sample.shape[1] == 32
    cfg = _cfg_from_shapes(SEQ, LC - NMETA, DEPTH)
    NSLOT = cfg["NSLOT"]
    NTOKP = NSLOT * 128; NLOC = NTOKP + NMETA; NTOT = NLOC + 128
    key = (SEQ, LC, DEPTH)
    if key not in _NC_CACHE:
        _NC_CACHE[key] = build_nc(cfg)
    nc = _NC_CACHE[key]
    in_maps = []
    xpb = x_prompt.reshape(B, NSLOT, 4, 128, D)
    for c in range(8):
        b, r = c // 4, c % 4
        m = dict(
            xp=np.ascontiguousarray(xpb[b, :, r]).reshape(NTOKP, D),
            xm=np.ascontiguousarray(np.asarray(meta_tokens, f32)),
            xs=np.ascontiguousarray(x_sample[4 * c:4 * c + 4]).reshape(128, D),
            cdk=np.ascontiguousarray(np.asarray(cache_diff_k, f32)[:, 4 * c:4 * c + 4]).reshape(DEPTH, 4, LC, 512),
            cdv=np.ascontiguousarray(np.asarray(cache_diff_v, f32)[:, 4 * c:4 * c + 4]).reshape(DEPTH, 4, LC, 512),
            cfk=np.ascontiguousarray(np.asarray(cache_fox_k, f32)[:, 4 * c:4 * c + 4]).reshape(DEPTH, 4, LC, 512),
            cfv=np.ascontiguousarray(np.asarray(cache_fox_v, f32)[:, 4 * c:4 * c + 4]).reshape(DEPTH, 4, LC, 512),
            clf=np.ascontiguousarray(np.asarray(cache_fox_logf, f32)[:, 4 * c:4 * c + 4]),
            w_in=np.asarray(w_in, f32), w_out=np.asarray(w_out, f32), b_forget=np.asarray(b_forget, f32),
            norm_g=np.asarray(norm_g, f32), lq1=np.asarray(lambda_q1, f32), lk1=np.asarray(lambda_k1, f32),
            lq2=np.asarray(lambda_q2, f32), lk2=np.asarray(lambda_k2, f32), subln_g=np.asarray(subln_g, f32),
            fng=np.asarray(final_norm_g, f32).reshape(1, D),
        )
        m.update(_host_tables(cfg, r))
        in_maps.append(m)
    res = run_bass_kernel_spmd(nc, in_maps, core_ids=list(range(8)))
    R = res.results
    y_prompt = np.zeros((B, SEQ, D), f32)
    y_sample = np.zeros((32, 32, D), f32)
    T = NMETA + SEQ
    outs_p = {n: np.zeros((DEPTH, B, T, w), f32) for n, w in (("o_dk", 512), ("o_dv", 512), ("o_fk", 512), ("o_fv", 512), ("o_lf", 8))}
    outs_s = {n: np.zeros((DEPTH, 32, 32, w), f32) for n, w in (("o_dk", 512), ("o_dv", 512), ("o_fk", 512), ("o_fv", 512), ("o_lf", 8))}
    for c in range(8):
        b, r = c // 4, c % 4
        y_prompt.reshape(B, NSLOT, 4, 128, D)[b, :, r] = np.asarray(R[c]["yp"]).reshape(NSLOT, 128, D)
        y_sample[4 * c:4 * c + 4] = np.asarray(R[c]["ys"]).reshape(4, 32, D)
        for n in outs_p:
            a = np.asarray(R[c][n])
            w = a.shape[-1]
            outs_p[n][:, b, NMETA:].reshape(DEPTH, NSLOT, 4, 128, w)[:, :, r] = a[:, 0:NTOKP].reshape(DEPTH, NSLOT, 128, w)
            if r == 0:
                outs_p[n][:, b, 0:NMETA] = a[:, NTOKP:NLOC]
            outs_s[n][:, 4 * c:4 * c + 4] = a[:, NLOC:NTOT].reshape(DEPTH, 4, 32, w)
    return (y_prompt, y_sample,
            outs_p["o_dk"].reshape(DEPTH, B, T, 4, 2, 64), outs_p["o_dv"].reshape(DEPTH, B, T, 4, 128),
            outs_p["o_fk"].reshape(DEPTH, B, T, 8, 64), outs_p["o_fv"].reshape(DEPTH, B, T, 8, 64), outs_p["o_lf"],
            outs_s["o_dk"].reshape(DEPTH, 32, 32, 4, 2, 64), outs_s["o_dv"].reshape(DEPTH, 32, 32, 4, 128),
            outs_s["o_fk"].reshape(DEPTH, 32, 32, 8, 64), outs_s["o_fv"].reshape(DEPTH, 32, 32, 8, 64), outs_s["o_lf"])
```

```python
import math
import numpy as np
import ml_dtypes
import concourse.bass as bass
import concourse.mybir as mybir
from concourse.bass_utils import run_bass_kernel_spmd

F32 = mybir.dt.float32
BF16 = mybir.dt.bfloat16
AF = mybir.ActivationFunctionType
ALU = mybir.AluOpType
AX = mybir.AxisListType

D = 1024
NMETA = 16
HD = 64
NCOLS = 4104
NEGM = -30000.0
SLOPES = [2.0 ** (-8.0 * (h + 1) / 4) for h in range(4)]
ENGS = ("pe", "act", "dve", "pool", "sp")


class Prog:
    def __init__(self):
        self.ops = {e: [] for e in ENGS}
        self.cnt = {}
        self.rs = {}
        self.seen = {e: {} for e in ENGS}
        self.epoch = 0
        self.arena = {}
        self.ast = {}
        self.ncc = 0

    def region(self, name, arena=None, phase=None):
        if arena is not None:
            self.arena[name] = (arena, phase)

    def add(self, eng, fn, R=(), W=(), dma=None, cc=False):
        deps = {}

        def need(d):
            for k, i in d.items():
                if k[0] in ("dma", "cc"):
                    i = self.cnt[k]
                if deps.get(k, 0) < i:
                    deps[k] = i

        for r in R:
            st = self.rs.get(r)
            if st:
                need(st[0])
            ar = self.arena.get(r)
            if ar:
                for (a, p), d in self.ast.items():
                    if a == ar[0] and p != ar[1]:
                        need(d)
        for w in W:
            st = self.rs.get(w)
            if st:
                need(st[0])
                need(st[1])
            ar = self.arena.get(w)
            if ar:
                for (a, p), d in self.ast.items():
                    if a == ar[0] and p != ar[1]:
                        need(d)
        if cc:
            key = ("cc", self.epoch)
        elif dma:
            key = ("dma", dma)
        else:
            key = (eng, self.epoch)
        waits = []
        seen = self.seen[eng]
        for k, i in deps.items():
            if k == key and eng == "pe":
                continue
            if seen.get(k, 0) >= i:
                continue
            seen[k] = i
            waits.append((k, i))
        idx = self.cnt[key] = self.cnt.get(key, 0) + 1
        self.ops[eng].append((fn, waits, key))
        for r in R:
            st = self.rs.setdefault(r, ({}, {}))
            st[1][key] = idx
        for w in W:
            st = self.rs.setdefault(w, ({}, {}))
            st[0].clear()
            st[1].clear()
            st[0][key] = idx
        for x in tuple(R) + tuple(W):
            ar = self.arena.get(x)
            if ar:
                self.ast.setdefault(ar, {})[key] = idx

    def finish(self):
        waits = [(k, i) for k, i in self.cnt.items()]
        self.ops["sp"].append((None, waits, None))


def _cfg_from_shapes(seq, past_len, depth):
    assert seq % 512 == 0 and past_len % 64 == 0
    nblk = seq // 128
    nslot = nblk // 4
    import os
    cfg = dict(SEQ=seq, PAST=past_len, DEPTH=depth, NSLOT=nslot, LC=NMETA + past_len, GS=int(os.environ.get("K_GS", "4")))
    return cfg


def build_nc(cfg):
    NSLOT = cfg["NSLOT"]
    DEPTH = cfg["DEPTH"]
    LC = cfg["LC"]
    NTOKP = NSLOT * 128
    NLOC = NTOKP + NMETA
    NTOT = NLOC + 128
    LCN = LC + 32
    NCT = (LC + 127) // 128
    GS = min(int(cfg.get("GS", 4)), NSLOT)
    NQG = NSLOT // GS
    NB = 4 * NSLOT + 1
    FW = max(NLOC, LCN)
    KW = max(4 * NLOC, 8 * LCN // 2 + 8)
    VU = max(NB, (NCT + 1) * 4 // 2 + 2)

    nc = bass.Bass("TRN2", target_bir_lowering=False)
    P = Prog()

    def din(name, shape, dt=F32):
        return nc.dram_tensor(name, list(shape), dt, kind="ExternalInput").ap()

    def dout(name, shape, dt=F32):
        return nc.dram_tensor(name, list(shape), dt, kind="ExternalOutput").ap()

    def dscr(name, shape, dt):
        return nc.dram_tensor(name, list(shape), dt).ap()

    xp = din("xp", [NTOKP, D]); xm = din("xm", [NMETA, D]); xs = din("xs", [128, D])
    cdk = din("cdk", [DEPTH, 4, LC, 512]); cdv = din("cdv", [DEPTH, 4, LC, 512])
    cfk = din("cfk", [DEPTH, 4, LC, 512]); cfv = din("cfv", [DEPTH, 4, LC, 512])
    clf = din("clf", [DEPTH, 4, LC, 8])
    w_in = din("w_in", [DEPTH, D, NCOLS]); w_out = din("w_out", [DEPTH, D, D])
    b_forget = din("b_forget", [DEPTH, 8]); norm_g = din("norm_g", [DEPTH, D])
    lq1 = din("lq1", [DEPTH, 64]); lk1 = din("lk1", [DEPTH, 64]); lq2 = din("lq2", [DEPTH, 64]); lk2 = din("lk2", [DEPTH, 64])
    subln_g = din("subln_g", [DEPTH, 128]); fng = din("fng", [1, D])
    c_ident = din("c_ident", [128, 128], BF16)
    c_f32 = din("c_f32", [128, 5 * 128 + 8])
    c_mkfox = din("c_mkfox", [128, 4 * 128], BF16)
    c_mkdiff = din("c_mkdiff", [128, 16 * 128], BF16)
    c_mksm = din("c_mksm", [32, 5 * 16 + 5 * 32], BF16)
    c_qauxd = din("c_qauxd", [4, 6, NTOT], BF16)
    c_kauxd = din("c_kauxd", [4, 6, 4 * NLOC], BF16)
    c_kauxds = din("c_kauxds", [4, 6, LCN], BF16)
    onesb = din("c_ones", [8, 3, FW], BF16)

    yp = dout("yp", [NTOKP, D]); ys = dout("ys", [128, D])
    o_dk = dout("o_dk", [DEPTH, NTOT, 512]); o_dv = dout("o_dv", [DEPTH, NTOT, 512])
    o_fk = dout("o_fk", [DEPTH, NTOT, 512]); o_fv = dout("o_fv", [DEPTH, NTOT, 512])
    o_lf = dout("o_lf", [DEPTH, NTOT, 8])

    xscr = dscr("xscr", [NTOT, D], F32)
    qT = dscr("qT", [1024, NTOT], BF16)
    kTs = dscr("kTs", [1024, 128], BF16)
    vsn = dscr("vsn", [128, 8 * 129], BF16)
    agin_k = [dscr(f"agin_k{i}", [128, NLOC], BF16) for i in range(8)]
    agin_v = [dscr(f"agin_v{i}", [NLOC, 129], BF16) for i in range(8)]
    agin_lf = dscr("agin_lf", [8, NLOC], F32)
    agout_k = [[dscr(f"agout_k{i}_{j}", [4 * 128, NLOC], BF16) for j in range(8)] for i in range(2)]
    agout_v = [[dscr(f"agout_v{i}_{j}", [4 * NLOC, 129], BF16) for j in range(8)] for i in range(2)]
    agout_lf = [dscr(f"agout_lf{i}", [32, NLOC], F32) for i in range(2)]
    kauxf = dscr("kauxf", [8, 6, 4 * NLOC], BF16)
    qauxf = dscr("qauxf", [8, 6, NTOT], BF16)
    kauxfs = dscr("kauxfs", [4, 8, 6, LCN], BF16)

    def _sz(shape_free_elems, dt):
        nb = shape_free_elems * (2 if dt == BF16 else 4)
        return ((nb + 3) // 4 + 7) // 8 * 8
    sz_proj = 3 * _sz(8 * 512, BF16) + _sz(8 * 1024, BF16) + 2 * _sz(1024, F32) + 3 * _sz(512, F32) + 2 * _sz(512, BF16) + 2 * _sz(4 * 129, BF16) + 2 * _sz(1024, BF16) + 2 * _sz(4 * 512, F32)
    sz_F = 4 * _sz(FW, F32) + _sz(3 * FW, BF16) + 2 * _sz(NTOT, F32) + _sz(3 * NTOT, BF16) + _sz(NCT * 32, F32) + _sz(4 * NSLOT + 8, F32)
    sz_att = _sz(2 * KW, BF16) + _sz(2 * VU * 129, BF16) + 2 * _sz(NLOC, BF16)
    BIG_F32 = max(sz_proj, sz_F, sz_att) + 8
    from contextlib import ExitStack
    es = ExitStack()

    def sb(name, shape, dt):
        return es.enter_context(nc.sbuf_tensor(name, list(shape), dt))

    def ps(name, shape, dt=F32):
        return es.enter_context(nc.psum_tensor(name, list(shape), dt))

    arH = sb("arH", [128, max(_sz(8 * NTOT, BF16), 12 * _sz(512, F32) + 4 * _sz(512, BF16)) + 8], F32)
    arB = sb("arB", [128, BIG_F32], F32)
    Gt = sb("Gt", [128, 8, NTOT], BF16)
    Pt = [sb(f"Pt{i}", [128, 512], BF16) for i in range(4)]
    ident = sb("ident", [128, 128], BF16)
    cf32 = sb("cf32", [128, 5 * 128 + 8], F32)
    identf = cf32[:, 0:128]; sel63 = cf32[:, 128:256]; onesdiv = cf32[:, 256:384]; Mall = cf32[:, 384:512]; Mlow = cf32[:, 512:640]; selmat = cf32[:, 640:648]
    ones_f = sb("ones_f", [128, 128], F32)
    zer_b = sb("zer_b", [1, 640], BF16)
    mkfox = sb("mkfox", [128, 4, 128], BF16)
    mkdiff = sb("mkdiff", [128, 16, 128], BF16)
    mksm = sb("mksm", [32, 240], BF16)
    gbc = sb("gbc", [128, D], F32)
    bfbc = sb("bfbc", [128, 8], F32)
    nbcol = sb("nbcol", [8, 1], F32)
    lamt = sb("lamt", [128, 4 * 64], F32)
    lamw = sb("lamw", [128, 16], F32)
    gsub = sb("gsub", [128, 1], F32)
    sml = sb("sml", [128, 16], F32)
    lfT = sb("lfT", [8, NTOT], F32)

    PS = [ps(f"psb{i}", [128, 512], F32) for i in range(8)]

    class Carver:
        def __init__(self, arena):
            self.a = arena
            self.off = 0

        def take(self, shape, dt):
            n = 1
            for s in shape[1:]:
                n *= s
            nb = n * (2 if dt == BF16 else 4)
            nf = (nb + 3) // 4
            nf = (nf + 7) // 8 * 8
            v = self.a[0:shape[0], self.off:self.off + nf]
            self.off += nf
            if dt == BF16:
                v = v.bitcast(BF16)[:, 0:n]
            else:
                v = v[:, 0:n]
            assert self.off <= self.a.shape[1], (self.off, self.a.shape)
            return v

    def v3(ap, a, b):
        return ap.rearrange("p (a b) -> p a b", a=a, b=b)

    cH0 = Carver(arH)
    hT = v3(cH0.take([128, 8 * NTOT], BF16), 8, NTOT)
    cH1 = Carver(arH)
    fin = {n: cH1.take([128, 512], F32) for n in ("lrow", "rb", "tmp", "od1", "od2", "od", "sq", "t")}
    cstf = [cH1.take([128, 512], F32) for _ in range(4)]
    cstb = [cH1.take([128, 512], BF16) for _ in range(4)]
    cB0 = Carver(arB)
    wbuf = [v3(cB0.take([128, 8 * 512], BF16), 8, 512) for _ in range(3)]
    wout = v3(cB0.take([128, 8 * 1024], BF16), 8, 1024)
    xt = [cB0.take([128, 1024], F32) for _ in range(2)]
    stf = [cB0.take([128, 512], F32) for _ in range(3)]
    stb = [cB0.take([128, 512], BF16) for _ in range(2)]
    vst = [cB0.take([128, 4 * 129], BF16) for _ in range(2)]
    hrow = cB0.take([128, 1024], BF16)
    wst = [v3(cB0.take([128, 4 * 512], F32), 4, 512) for _ in range(2)]
    junk = cB0.take([128, 1024], BF16)
    cB1 = Carver(arB)
    Fa = cB1.take([128, FW], F32); Fb = cB1.take([128, FW], F32); Fc = cB1.take([128, FW], F32)
    Fp = v3(cB1.take([128, 3 * FW], BF16), 3, FW)
    Fq = cB1.take([8, NTOT], F32); Fqr = cB1.take([8, NTOT], F32)
    Fqp = v3(cB1.take([8, 3 * NTOT], BF16), 3, NTOT)
    clfs = cB1.take([128, NCT * 32], F32)
    Fm = cB1.take([128, FW], F32)
    Tt = cB1.take([128, 4 * NSLOT + 8], F32)
    cB2 = Carver(arB)
    Kall = cB2.take([70, 2 * KW], BF16)
    Kt = [Kall[:, i * KW:(i + 1) * KW] for i in range(2)]
    Vall = cB2.take([128, 2 * VU * 129], BF16)
    Vt = [Vall[:, i * VU * 129:(i + 1) * VU * 129] for i in range(2)]
    Qt = [cB2.take([70, NLOC], BF16) for _ in range(2)]

    for i in range(3):
        P.region(f"wbuf{i}", "B", "proj"); P.region(f"stf{i}", "B", "proj")
    for i in range(2):
        P.region(f"xt{i}", "B", "proj"); P.region(f"stb{i}", "B", "proj"); P.region(f"vst{i}", "B", "proj")
        P.region(f"Kt{i}", "B", "att"); P.region(f"Vt{i}", "B", "att"); P.region(f"Qt{i}", "B", "att")
    for i in range(4):
        P.region(f"cstf{i}", "H", "att"); P.region(f"cstb{i}", "H", "att")
    for i in range(2):
        P.region(f"wst{i}", "B", "proj")
    for n in ("wout", "hrow", "junk"):
        P.region(n, "B", "proj")
    for n in ("Fa", "Fb", "Fc", "Fp", "Fq", "Fqr", "Fqp", "clfs", "Fm", "Tt"):
        P.region(n, "B", "F")
    for n in fin:
        P.region("fin_" + n, "H", "att")
    for tg in range(NQG + 1):
        P.region(f"hT{tg}", "H", "proj")

    ctr = {"st": 0, "stb": 0, "vst": 0, "xt": 0, "w": 0, "ps": 0, "pt": 0, "cst": 0, "wst": 0}

    def nxt(name, n):
        v = ctr[name] % n
        ctr[name] += 1
        return v

    def dma(eng, out, in_, R, W, ch):
        P.add(eng, lambda e, o=out, i=in_: e.dma_start(out=o, in_=i), R=R, W=W, dma=ch)

    def tgs():
        r = []
        for g in range(NQG):
            r.append((g * GS * 128, GS * 128, g))
        r.append((NTOKP, NMETA + 128, NQG))
        return r

    def tiles():
        r = [(j * 128, 128, j // GS) for j in range(NSLOT)]
        r.append((NTOKP, NMETA, NQG))
        r.append((NLOC, 128, NQG))
        return r

    def x_src(l, row0, n):
        if l == 0:
            if row0 < NTOKP:
                return xp[row0:row0 + n, :]
            if row0 < NLOC:
                return xm[0:n, :]
            return xs[0:n, :]
        return xscr[row0:row0 + n, :]

    def norm_tile(xtile, n, xreg, row0, hreg, gtile, greg):
        P.add("dve", lambda e: e.scalar_tensor_tensor(out=junk[0:n, :], in0=xtile[0:n, :], scalar=1.0, in1=xtile[0:n, :],
                                                      op0=ALU.mult, op1=ALU.mult, accum_out=sml[0:n, 0:1]),
              R=[xreg], W=["junk", "sml0"])
        P.add("act", lambda e: e.activation(out=sml[0:n, 1:2], in_=sml[0:n, 0:1], func=AF.Ln, scale=1.0 / D, bias=1e-6),
              R=["sml0"], W=["sml1"])
        P.add("act", lambda e: e.activation(out=sml[0:n, 2:3], in_=sml[0:n, 1:2], func=AF.Exp, scale=-0.5),
              R=["sml1"], W=["sml2"])
        P.add("dve", lambda e: e.scalar_tensor_tensor(out=hrow[0:n, :], in0=xtile[0:n, :], scalar=sml[0:n, 2:3], in1=gtile[0:n, :],
                                                      op0=ALU.mult, op1=ALU.mult),
              R=[xreg, "sml2", greg], W=["hrow"])
        pb = PS[nxt("ps", 8)]
        pbn = f"ps{(ctr['ps'] - 1) % 8}"
        tp = pb[:, :].bitcast(BF16)
        for kc in range(8):
            P.add("pe", lambda e, kc=kc: e.transpose(out=tp[:, kc * 128:kc * 128 + n], in_=hrow[0:n, kc * 128:(kc + 1) * 128],
                                                     identity=ident[0:n, 0:n]),
                  R=["hrow", "ident"], W=[pbn])
        tpv = tp.rearrange("p (a b) -> p a b", a=8, b=128)
        P.add("act", lambda e: e.activation(out=hT[:, :, row0:row0 + n], in_=tpv[:, :, 0:n], func=AF.Copy),
              R=[pbn], W=[hreg])

    dma("sp", ident[:, :], c_ident, [], ["ident"], "c0")
    dma("sp", cf32[:, :], c_f32, [], ["cf32"], "c0")
    dma("sp", mkfox[:, :, :], c_mkfox.rearrange("p (a b) -> p a b", a=4, b=128), [], ["mk"], "c0")
    dma("sp", mkdiff[:, :, :], c_mkdiff.rearrange("p (a b) -> p a b", a=16, b=128), [], ["mk"], "c0")
    dma("sp", mksm[:, :], c_mksm, [], ["mk"], "c0")
    P.add("pool", lambda e: e.memset(ones_f[:, :], 1.0), W=["ones_f"])
    P.add("pool", lambda e: e.memset(zer_b[:, :], 0.0), W=["zer_b"])
    P.add("pool", lambda e: e.memset(lfT[:, :], 0.0), W=["lfT"])
    for i in range(2):
        vv = Vt[i].rearrange("p (u c) -> p u c", u=VU, c=129)
        P.add("pool", lambda e, vv=vv: e.memset(vv[:, :, 64:65], 1.0), W=[f"Vt{i}"])
    for r_ in range(4):
        dma("sp", kauxf[:, 0:3, r_ * NLOC:(r_ + 1) * NLOC], onesb[:, :, 0:NLOC], [], ["kauxf"], "c0")
    dma("sp", qauxf[:, 3:6, 0:NLOC], onesb[:, :, 0:NLOC], [], ["qauxf"], "c0")
    dma("sp", qauxf[:, 3:6, NLOC:NTOT], onesb[:, :, 0:128], [], ["qauxf"], "c0")
    for s in range(4):
        dma("sp", kauxfs[s, :, 0:3, :], onesb[:, :, 0:LCN], [], ["kauxfs"], "c0")

    def load_layer_consts(l):
        dma("sp", gbc[:, :], norm_g[l:l + 1, :].partition_broadcast(128) if l < DEPTH else fng[0:1, :].partition_broadcast(128),
            [], ["gbc"], "c1")

    def vst_memset():
        for i in range(2):
            vv = vst[i].rearrange("p (u c) -> p u c", u=4, c=129)
            P.add("pool", lambda e, vv=vv: e.memset(vv[:, :, 64:65], 1.0), W=[f"vst{i}"])

    load_layer_consts(0)
    for (row0, n, hr) in tiles():
        xi = nxt("xt", 2)
        dma("sp", xt[xi][0:n, :], x_src(0, row0, n), [], [f"xt{xi}"], f"xt{xi}")
        norm_tile(xt[xi], n, f"xt{xi}", row0, f"hT{hr}", gbc, "gbc")

    CH = [("dk", 512, 512), ("dv", 1024, 512), ("fl", 3072, 8), ("fk", 2048, 512), ("fv", 2560, 512),
          ("dq", 0, 512), ("fq", 1536, 512), ("g0", 3080, 512), ("g1", 3592, 512)]

    def evac_copy(eng_name, out, in_, R, W, scale=None):
        if eng_name == "act":
            if scale is None:
                P.add("act", lambda e: e.activation(out=out, in_=in_, func=AF.Copy), R=R, W=W)
            else:
                P.add("act", lambda e: e.activation(out=out, in_=in_, func=AF.Copy, scale=scale), R=R, W=W)
        else:
            if scale is None:
                P.add("dve", lambda e: e.tensor_copy(out=out, in_=in_), R=R, W=W)
            else:
                P.add("dve", lambda e: e.tensor_scalar(out=out, in0=in_, scalar1=scale, scalar2=None, op0=ALU.mult), R=R, W=W)

    def projections(l):
        vst_memset()
        dma("sp", bfbc[:, :], b_forget[l:l + 1, :].partition_broadcast(128), [], ["bfbc"], "c1")
        dma("sp", nbcol[:, :], b_forget[l:l + 1, :].rearrange("a b -> b a"), [], ["nbcol"], "c1")
        P.add("dve", lambda e: e.tensor_scalar(out=nbcol[:, :], in0=nbcol[:, :], scalar1=-1.0, scalar2=None, op0=ALU.mult),
              R=["nbcol"], W=["nbcol"])
        wslots = {}

        def load_w(ci):
            cname, c0, cw = CH[ci]
            wi = nxt("w", 3)
            wsrc = w_in[l].rearrange("(kc p) n -> p kc n", p=128)
            for hh_ in range(2):
                wsi = nxt("wst", 2)
                dma("sp", wst[wsi][:, :, 0:cw], wsrc[:, 4 * hh_:4 * hh_ + 4, c0:c0 + cw], [], [f"wst{wsi}"], f"wst{wsi}")
                P.add("pool", lambda e, wsi=wsi, wi=wi, hh_=hh_, cw=cw: e.tensor_copy(out=wbuf[wi][:, 4 * hh_:4 * hh_ + 4, 0:cw], in_=wst[wsi][:, :, 0:cw]),
                      R=[f"wst{wsi}"], W=[f"wbuf{wi}"])
            wslots[ci] = wi

        load_w(0)
        load_w(1)
        for ci, (cname, c0, cw) in enumerate(CH):
            if ci + 2 < len(CH):
                load_w(ci + 2)
            wi = wslots[ci]
            wreg = f"wbuf{wi}"
            wb = wbuf[wi]
            if cname in ("dq", "fq", "dk", "fk", "g0", "g1"):
                for sc in range(4):
                    for (col0, ncol, hr) in tgs():
                        bi = nxt("ps", 8)
                        pa = PS[bi]; pan = f"ps{bi}"
                        for kc in range(8):
                            P.add("pe", lambda e, kc=kc, pa=pa, sc=sc, col0=col0, ncol=ncol, wb=wb: e.matmul(
                                pa[:, 0:ncol], lhsT=wb[:, kc, sc * 128:(sc + 1) * 128], rhs=hT[:, kc, col0:col0 + ncol],
                                start=(kc == 0), stop=(kc == 7)), R=[wreg, f"hT{hr}"], W=[pan])
                        if cname in ("dq", "fq", "dk", "fk"):
                            si = nxt("stb", 2)
                            sreg = f"stb{si}"
                            evac_copy("act" if (sc % 2 == 0) else "dve", stb[si][:, 0:ncol], pa[:, 0:ncol], [pan], [sreg],
                                      scale=(0.125 if cname in ("dq", "fq") else None))
                            rb0 = (0 if cname[0] == "d" else 512) + sc * 128
                            if cname in ("dq", "fq"):
                                dma("sp", qT[rb0:rb0 + 128, col0:col0 + ncol], stb[si][:, 0:ncol], [sreg], ["qT"], sreg)
                            else:
                                kch = rb0 // 128
                                if col0 < NTOKP:
                                    dma("sp", agin_k[kch][:, col0:col0 + ncol], stb[si][:, 0:ncol], [sreg], [f"agin_k{kch}"], sreg)
                                else:
                                    dma("sp", agin_k[kch][:, NTOKP:NLOC], stb[si][:, 0:NMETA], [sreg], [f"agin_k{kch}"], sreg)
                                    dma("sp", kTs[rb0:rb0 + 128, :], stb[si][:, NMETA:NMETA + 128], [sreg], ["kTs"], sreg)
                        else:
                            chunk = (0 if cname == "g0" else 4) + sc
                            si = nxt("st", 3)
                            sreg = f"stf{si}"
                            greg = f"G{chunk}_{hr}"
                            P.add("act", lambda e, pa=pa, si=si, ncol=ncol: e.activation(out=stf[si][:, 0:ncol], in_=pa[:, 0:ncol],
                                                                                         func=AF.Exp, scale=-1.0), R=[pan], W=[sreg])
                            P.add("dve", lambda e, si=si, ncol=ncol: e.tensor_scalar(out=stf[si][:, 0:ncol], in0=stf[si][:, 0:ncol],
                                                                                     scalar1=1.0, scalar2=None, op0=ALU.add), R=[sreg], W=[sreg])
                            P.add("dve", lambda e, si=si, ncol=ncol: e.reciprocal(out=stf[si][:, 0:ncol], in_=stf[si][:, 0:ncol]),
                                  R=[sreg], W=[sreg])
                            P.add("dve", lambda e, pa=pa, si=si, ncol=ncol, chunk=chunk, col0=col0: e.tensor_tensor(
                                out=Gt[:, chunk, col0:col0 + ncol], in0=pa[:, 0:ncol], in1=stf[si][:, 0:ncol], op=ALU.mult),
                                R=[pan, sreg], W=[greg])
            if cname in ("dk", "dv", "fk", "fv"):
                odst = {"dk": o_dk, "dv": o_dv, "fk": o_fk, "fv": o_fv}[cname]
                for (row0, n, hr) in tiles():
                    bi = nxt("ps", 8)
                    pb = PS[bi]; pbn = f"ps{bi}"
                    for kc in range(8):
                        P.add("pe", lambda e, kc=kc, pb=pb, row0=row0, n=n, wb=wb: e.matmul(
                            pb[0:n, 0:512], lhsT=hT[:, kc, row0:row0 + n], rhs=wb[:, kc, 0:512],
                            start=(kc == 0), stop=(kc == 7)), R=[wreg, f"hT{hr}"], W=[pbn])
                    si = nxt("st", 3)
                    sreg = f"stf{si}"
                    evac_copy("act", stf[si][0:n, :], pb[0:n, 0:512], [pbn], [sreg])
                    dma("sp", odst[l, row0:row0 + n, :], stf[si][0:n, :], [sreg], ["out_" + cname], sreg)
                    if cname in ("dv", "fv"):
                        vi = nxt("vst", 2)
                        vreg = f"vst{vi}"
                        vv = vst[vi].rearrange("p (u c) -> p u c", u=4, c=129)
                        pv = pb[:, 0:512].rearrange("p (u t c) -> p u t c", u=4, t=2, c=64)
                        P.add("dve", lambda e, vv=vv, pv=pv, n=n: e.tensor_copy(out=vv[0:n, :, 0:64], in_=pv[0:n, :, 0, :]), R=[pbn], W=[vreg])
                        P.add("dve", lambda e, vv=vv, pv=pv, n=n: e.tensor_copy(out=vv[0:n, :, 65:129], in_=pv[0:n, :, 1, :]), R=[pbn], W=[vreg])
                        pb0 = 0 if cname == "dv" else 4
                        if row0 < NLOC:
                            for u_ in range(4):
                                dma("sp", agin_v[pb0 + u_][row0:row0 + n, :], vst[vi][0:n, u_ * 129:(u_ + 1) * 129], [vreg], [f"agin_v{pb0 + u_}"], vreg)
                        else:
                            dma("sp", vsn[0:n, pb0 * 129:(pb0 + 4) * 129], vst[vi][0:n, :], [vreg], ["vsn"], vreg)
            if cname == "fl":
                for (row0, n, hr) in tiles():
                    bi = nxt("ps", 8)
                    pb = PS[bi]; pbn = f"ps{bi}"
                    for kc in range(8):
                        P.add("pe", lambda e, kc=kc, pb=pb, row0=row0, n=n, wb=wb: e.matmul(
                            pb[0:n, 0:8], lhsT=hT[:, kc, row0:row0 + n], rhs=wb[:, kc, 0:8],
                            start=(kc == 0), stop=(kc == 7)), R=[wreg, f"hT{hr}"], W=[pbn])
                    si = nxt("st", 3)
                    sreg = f"stf{si}"
                    s8 = stf[si][0:n, 0:8]
                    P.add("dve", lambda e, s8=s8, pb=pb, n=n: e.tensor_tensor(out=s8, in0=pb[0:n, 0:8], in1=bfbc[0:n, :], op=ALU.add),
                          R=[pbn, "bfbc"], W=[sreg])
                    P.add("act", lambda e, s8=s8: e.activation(out=s8, in_=s8, func=AF.Exp, scale=-1.0), R=[sreg], W=[sreg])
                    P.add("act", lambda e, s8=s8: e.activation(out=s8, in_=s8, func=AF.Ln, bias=1.0), R=[sreg], W=[sreg])
                    P.add("dve", lambda e, s8=s8: e.tensor_scalar(out=s8, in0=s8, scalar1=-1.0, scalar2=None, op0=ALU.mult), R=[sreg], W=[sreg])
                    dma("sp", o_lf[l, row0:row0 + n, :], s8, [sreg], ["out_lf"], sreg)
                for (col0, ncol, hr) in tgs():
                    bi = nxt("ps", 8)
                    pa = PS[bi]; pan = f"ps{bi}"
                    for kc in range(8):
                        P.add("pe", lambda e, kc=kc, pa=pa, col0=col0, ncol=ncol, wb=wb: e.matmul(
                            pa[0:8, 0:ncol], lhsT=wb[:, kc, 0:8], rhs=hT[:, kc, col0:col0 + ncol],
                            start=(kc == 0), stop=(kc == 7)), R=[wreg, f"hT{hr}"], W=[pan])
                    lv = lfT[0:8, col0:col0 + ncol]
                    P.add("act", lambda e, lv=lv, pa=pa, ncol=ncol: e.activation(out=lv, in_=pa[0:8, 0:ncol], func=AF.Exp, scale=-1.0, bias=nbcol[:, 0:1]),
                          R=[pan, "nbcol"], W=["lfT"])
                    P.add("act", lambda e, lv=lv: e.activation(out=lv, in_=lv, func=AF.Ln, bias=1.0), R=["lfT"], W=["lfT"])
                    P.add("dve", lambda e, lv=lv: e.tensor_scalar(out=lv, in0=lv, scalar1=-1.0, scalar2=None, op0=ALU.mult), R=["lfT"], W=["lfT"])
                dma("sp", agin_lf[:, :], lfT[0:8, 0:NLOC], ["lfT"], ["agin_lf"], "c1")
            ag_after_chunk(l, cname)

    def allgather(l):
        pass

    def ag_after_chunk(l, cname):
        par = l % 2
        rg = [[0, 1, 2, 3], [4, 5, 6, 7]]
        lst = []
        if cname == "fl":
            lst.append((agin_lf, agout_lf[par], "agin_lf", f"agout_lf{par}"))
        elif cname == "dk":
            lst += [(agin_k[i], agout_k[par][i], f"agin_k{i}", f"agout_k{par}_{i}") for i in range(0, 4)]
        elif cname == "fk":
            lst += [(agin_k[i], agout_k[par][i], f"agin_k{i}", f"agout_k{par}_{i}") for i in range(4, 8)]
        elif cname == "dv":
            lst += [(agin_v[i], agout_v[par][i], f"agin_v{i}", f"agout_v{par}_{i}") for i in range(0, 4)]
        elif cname == "fv":
            lst += [(agin_v[i], agout_v[par][i], f"agin_v{i}", f"agout_v{par}_{i}") for i in range(4, 8)]
        for (src, dst, rn, wn) in lst:
            P.add("pool", lambda e, src=src, dst=dst: e.collective_compute("AllGather", ALU.bypass, replica_groups=rg,
                                                                           ins=[src.opt()], outs=[dst.opt()]),
                  R=[rn], W=[wn], cc=True)

    def lam_consts(l):
        lam_init = 0.8 - 0.6 * math.exp(-0.3 * l)
        for i, t in enumerate((lq1, lk1, lq2, lk2)):
            dma("sp", lamt[:, i * 64:(i + 1) * 64], t[l:l + 1, :].partition_broadcast(128), [], ["lamt"], "c1")
        dma("sp", gsub[:, :], subln_g[l:l + 1, :].rearrange("a b -> b a"), [], ["gsub"], "c1")
        P.add("dve", lambda e: e.scalar_tensor_tensor(out=lamt[:, 0:64], in0=lamt[:, 0:64], scalar=1.0, in1=lamt[:, 64:128],
                                                      op0=ALU.mult, op1=ALU.mult, accum_out=lamw[:, 0:1]), R=["lamt"], W=["lamt", "lamw"])
        P.add("dve", lambda e: e.scalar_tensor_tensor(out=lamt[:, 128:192], in0=lamt[:, 128:192], scalar=1.0, in1=lamt[:, 192:256],
                                                      op0=ALU.mult, op1=ALU.mult, accum_out=lamw[:, 1:2]), R=["lamt", "lamw"], W=["lamt", "lamw"])
        P.add("act", lambda e: e.activation(out=lamw[:, 2:4], in_=lamw[:, 0:2], func=AF.Exp), R=["lamw"], W=["lamw"])
        P.add("dve", lambda e: e.tensor_tensor(out=lamw[:, 4:5], in0=lamw[:, 3:4], in1=lamw[:, 2:3], op=ALU.subtract), R=["lamw"], W=["lamw"])
        P.add("dve", lambda e: e.tensor_scalar(out=lamw[:, 5:6], in0=lamw[:, 4:5], scalar1=-lam_init, scalar2=None, op0=ALU.add), R=["lamw"], W=["lamw"])
        P.add("dve", lambda e: e.tensor_scalar(out=gsub[:, :], in0=gsub[:, :], scalar1=(1.0 - lam_init), scalar2=None, op0=ALU.mult), R=["gsub"], W=["gsub"])

    def pieces(src, n_p, width, dstp, sreg, preg, neg):
        r = Fc[0:n_p, 0:width]
        P.add("dve", lambda e: e.tensor_scalar(out=r, in0=src, scalar1=(-1.0 if neg else 1.0), scalar2=None, op0=ALU.mult), R=[sreg], W=["Fc"])
        for k in range(3):
            P.add("dve", lambda e, k=k: e.tensor_copy(out=dstp[0:n_p, k, 0:width], in_=r), R=["Fc"], W=[preg])
            if k < 2:
                P.add("dve", lambda e, k=k: e.tensor_tensor(out=r, in0=r, in1=dstp[0:n_p, k, 0:width], op=ALU.subtract), R=["Fc", preg], W=["Fc"])

    def fox_F(l):
        par = l % 2
        P.add("pool", lambda e: e.memset(Fa[:, :], 0.0), W=["Fa"])
        P.add("pool", lambda e: e.memset(Fb[:, :], 0.0), W=["Fb"])
        P.add("pool", lambda e: e.memset(Fm[:, :], 1.0), W=["Fm"])
        Fm3 = Fm[:, 0:NTOKP].rearrange("p (j i) -> p j i", j=NSLOT, i=128)
        P.add("pool", lambda e: e.memset(Fm3[:, :, 0:1], 0.0), W=["Fm"])
        P.add("pool", lambda e: e.memset(Fm[:, NTOKP:NTOKP + 1], 0.0), W=["Fm"])
        for r_ in range(4):
            dma("sp", Fa[32 * r_:32 * r_ + 8, 0:NLOC], agout_lf[par][8 * r_:8 * r_ + 8, :], [f"agout_lf{par}"], ["Fa"], "f0")
        P.add("dve", lambda e: e.tensor_tensor_scan(out=Fb[:, 0:NLOC], data0=Fm[:, 0:NLOC], data1=Fa[:, 0:NLOC], initial=0.0,
                                                    op0=ALU.mult, op1=ALU.add), R=["Fa", "Fm"], W=["Fb"])
        Fb3 = Fb[:, 0:NTOKP].rearrange("p (j i) -> p j i", j=NSLOT, i=128)
        P.add("dve", lambda e: e.tensor_copy(out=Tt[:, 0:NSLOT], in_=Fb3[:, :, 127]), R=["Fb"], W=["Tt"])
        P.add("dve", lambda e: e.tensor_copy(out=Tt[:, 4 * NSLOT:4 * NSLOT + 1], in_=Fb[:, NLOC - 1:NLOC]), R=["Fb"], W=["Tt"])
        bi = nxt("ps", 8)
        pm = PS[bi]; pmn = f"ps{bi}"
        P.add("pe", lambda e: e.matmul(pm[:, 0:NSLOT], lhsT=Mall[:, :], rhs=Tt[:, 0:NSLOT], start=True, stop=True), R=["Tt", "cf32"], W=[pmn])
        P.add("pe", lambda e: e.matmul(pm[:, NSLOT:2 * NSLOT], lhsT=Mlow[:, :], rhs=Tt[:, 0:NSLOT], start=False, stop=True, skip_group_check=True),
              R=["Tt", "cf32"], W=[pmn])
        P.add("dve", lambda e: e.tensor_copy(out=Tt[:, NSLOT:3 * NSLOT], in_=pm[:, 0:2 * NSLOT]), R=[pmn], W=["Tt"])
        P.add("dve", lambda e: e.tensor_tensor_scan(out=Tt[:, 3 * NSLOT:4 * NSLOT], data0=ones_f[:, 0:NSLOT], data1=Tt[:, NSLOT:2 * NSLOT], initial=0.0,
                                                    op0=ALU.mult, op1=ALU.add), R=["Tt", "ones_f"], W=["Tt"])
        P.add("dve", lambda e: e.tensor_tensor(out=Tt[:, 3 * NSLOT:4 * NSLOT], in0=Tt[:, 3 * NSLOT:4 * NSLOT], in1=Tt[:, NSLOT:2 * NSLOT], op=ALU.subtract),
              R=["Tt"], W=["Tt"])
        P.add("dve", lambda e: e.tensor_tensor(out=Tt[:, 3 * NSLOT:4 * NSLOT], in0=Tt[:, 3 * NSLOT:4 * NSLOT], in1=Tt[:, 2 * NSLOT:3 * NSLOT], op=ALU.add),
              R=["Tt"], W=["Tt"])
        P.add("dve", lambda e: e.tensor_scalar(out=Tt[:, 3 * NSLOT:4 * NSLOT], in0=Tt[:, 3 * NSLOT:4 * NSLOT], scalar1=Tt[:, 4 * NSLOT:4 * NSLOT + 1], scalar2=None,
                                               op0=ALU.add), R=["Tt"], W=["Tt"])
        for j in range(NSLOT):
            P.add("dve", lambda e, j=j: e.tensor_scalar(out=Fb[:, j * 128:(j + 1) * 128], in0=Fb[:, j * 128:(j + 1) * 128],
                                                        scalar1=Tt[:, 3 * NSLOT + j:3 * NSLOT + j + 1], scalar2=None, op0=ALU.add), R=["Tt", "Fb"], W=["Fb"])
        c = 0
        while c < NLOC:
            w = min(512, NLOC - c)
            bi = nxt("ps", 8)
            pm2 = PS[bi]; pmn2 = f"ps{bi}"
            P.add("pe", lambda e, pm2=pm2, c=c, w=w: e.matmul(pm2[0:8, 0:w], lhsT=selmat[:, 0:8], rhs=Fb[:, c:c + w], start=True, stop=True),
                  R=["Fb", "cf32"], W=[pmn2])
            P.add("dve", lambda e, pm2=pm2, c=c, w=w: e.tensor_copy(out=Fq[0:8, c:c + w], in_=pm2[0:8, 0:w]), R=[pmn2], W=["Fq"])
            c += w
        pieces(Fb[:, 0:NLOC], 128, NLOC, Fp, "Fb", "Fp", True)
        for r_ in range(4):
            dma("sp", kauxf[:, 3:6, r_ * NLOC:(r_ + 1) * NLOC], Fp[32 * r_:32 * r_ + 8, :, 0:NLOC], ["Fp"], ["kauxf"], "f1")
        cl4w = clfs.rearrange("p (t s h) -> p t s h", t=NCT, s=4, h=8)
        nfull = LC // 128
        for s_ in range(4):
            dma("sp", cl4w[:, 0:nfull, s_, :], clf[l, s_, 0:nfull * 128, :].rearrange("(t p) h -> p t h", p=128), [], ["clfs"], "f0")
            if LC % 128:
                n = LC % 128
                dma("sp", cl4w[0:n, nfull, s_, :], clf[l, s_, nfull * 128:LC, :], [], ["clfs"], "f0")
        cl3 = clfs.rearrange("p (t c) -> p t c", t=NCT, c=32)
        t = 0
        while t < NCT:
            nt = min(4, NCT - t)
            bi = nxt("ps", 8)
            pm3 = PS[bi]; pmn3 = f"ps{bi}"
            tot = 0
            for tt in range(t, t + nt):
                n = min(128, LC - tt * 128)
                P.add("pe", lambda e, pm3=pm3, tt=tt, t=t, n=n: e.transpose(out=pm3[0:32, (tt - t) * 128:(tt - t) * 128 + n],
                                                                          in_=cl3[0:n, tt, :], identity=identf[0:n, 0:n]),
                      R=["clfs", "cf32"], W=[pmn3])
                tot = (tt - t) * 128 + n
            P.add("dve", lambda e, pm3=pm3, t=t, tot=tot: e.tensor_copy(out=Fa[0:32, t * 128:t * 128 + tot], in_=pm3[0:32, 0:tot]), R=[pmn3, "Fp"], W=["Fa"])
            t += nt
        for s_ in range(4):
            dma("sp", Fa[8 * s_:8 * s_ + 8, LC:LCN], lfT[0:8, NLOC + 32 * s_:NLOC + 32 * s_ + 32], ["lfT"], ["Fa"], "f0")
        P.add("pool", lambda e: e.memset(Fm[0:32, 0:LCN], 1.0), R=[], W=["Fm"])
        P.add("dve", lambda e: e.tensor_tensor_scan(out=Fb[0:32, 0:LCN], data0=Fm[0:32, 0:LCN], data1=Fa[0:32, 0:LCN], initial=0.0,
                                                    op0=ALU.mult, op1=ALU.add), R=["Fa", "Fm", "Fp"], W=["Fb"])
        for s_ in range(4):
            dma("sp", Fq[0:8, NLOC + 32 * s_:NLOC + 32 * s_ + 32], Fb[8 * s_:8 * s_ + 8, LC:LCN], ["Fb"], ["Fq"], "f0")
        pieces(Fb[0:32, 0:LCN], 32, LCN, Fp, "Fb", "Fp", True)
        dma("sp", kauxfs.rearrange("s h k n -> (s h) k n")[:, 3:6, :], Fp[0:32, :, 0:LCN], ["Fp"], ["kauxfs"], "f1")
        P.add("dve", lambda e: e.tensor_copy(out=Fqr[0:8, :], in_=Fq[0:8, :]), R=["Fq"], W=["Fqr"])
        for k in range(3):
            P.add("dve", lambda e, k=k: e.tensor_copy(out=Fqp[0:8, k, :], in_=Fqr[0:8, :]), R=["Fqr"], W=["Fqp"])
            if k < 2:
                P.add("dve", lambda e, k=k: e.tensor_tensor(out=Fqr[0:8, :], in0=Fqr[0:8, :], in1=Fqp[0:8, k, :], op=ALU.subtract), R=["Fqr", "Fqp"], W=["Fqr"])
        dma("sp", qauxf[:, 0:3, :], Fqp[0:8, :, :], ["Fqp"], ["qauxf"], "f1")

    ACC = [(PS[3], PS[4], "ps3", "ps4"), (PS[5], PS[6], "ps5", "ps6")]
    SB = [(PS[0], "ps0"), (PS[1], "ps1"), (PS[2], "ps2")]
    PM = (PS[7], "ps7")
    actr = {"acc": 0, "s": 0}

    def zero_bank(pbank, pname, width):
        P.add("pe", lambda e: e.matmul(pbank[:, 0:width], lhsT=zer_b[0:1, 0:128], rhs=zer_b[0:1, 128:128 + width], start=True, stop=True,
                                       skip_group_check=True), R=["zer_b"], W=[pname])

    def finalize_fox(X, Xn, co, N, odd, chunk, tc0, greg):
        pm, pmn = PM
        lrow, rb, tmp = fin["lrow"], fin["rb"], fin["tmp"]
        if not odd:
            P.add("dve", lambda e: e.tensor_copy(out=lrow[64:65, 0:N], in_=X[64:65, co:co + N]), R=[Xn], W=["fin_lrow"])
            P.add("pe", lambda e: e.matmul(pm[0:64, 0:N], lhsT=ones_f[64:65, 0:64], rhs=lrow[64:65, 0:N], start=True, stop=True),
                  R=["fin_lrow", "ones_f"], W=[pmn])
            lo, hi = 0, 64
        else:
            P.add("dve", lambda e: e.tensor_copy(out=lrow[32:64, 0:N], in_=X[32:64, co:co + N]), R=[Xn], W=["fin_lrow"])
            P.add("pe", lambda e: e.matmul(pm[0:128, 0:N], lhsT=sel63[32:64, 0:128], rhs=lrow[32:64, 0:N], start=True, stop=True),
                  R=["fin_lrow", "cf32"], W=[pmn])
            lo, hi = 64, 128
        P.add("dve", lambda e: e.reciprocal(out=rb[lo:hi, 0:N], in_=pm[lo:hi, 0:N]), R=[pmn], W=["fin_rb"])
        P.add("dve", lambda e: e.tensor_tensor(out=tmp[lo:hi, 0:N], in0=X[lo:hi, co:co + N], in1=rb[lo:hi, 0:N], op=ALU.mult),
              R=[Xn, "fin_rb"], W=["fin_tmp"])
        P.add("dve", lambda e: e.tensor_tensor(out=Gt[lo:hi, chunk, tc0:tc0 + N], in0=tmp[lo:hi, 0:N], in1=Gt[lo:hi, chunk, tc0:tc0 + N], op=ALU.mult),
              R=["fin_tmp", greg], W=[greg])

    def finalize_diff_comp(X, Y, Xn, Yn, co, N, odn):
        pm, pmn = PM
        lrow, rb = fin["lrow"], fin["rb"]
        od = fin[odn]
        P.add("dve", lambda e: e.tensor_copy(out=lrow[64:65, 0:N], in_=X[64:65, co:co + N]), R=[Xn], W=["fin_lrow"])
        P.add("pe", lambda e: e.matmul(pm[0:128, 0:N], lhsT=ones_f[64:65, 0:128], rhs=lrow[64:65, 0:N], start=True, stop=True),
              R=["fin_lrow", "ones_f"], W=[pmn])
        P.add("dve", lambda e: e.reciprocal(out=rb[:, 0:N], in_=pm[:, 0:N]), R=[pmn], W=["fin_rb"])
        P.add("dve", lambda e: e.tensor_tensor(out=od[0:64, 0:N], in0=X[0:64, co:co + N], in1=rb[0:64, 0:N], op=ALU.mult),
              R=[Xn, "fin_rb"], W=["fin_" + odn])
        P.add("dve", lambda e: e.tensor_tensor(out=od[64:128, 0:N], in0=Y[64:128, co:co + N], in1=rb[64:128, 0:N], op=ALU.mult),
              R=[Yn, "fin_rb"], W=["fin_" + odn])

    def finalize_diff_head(N, chunk, tc0, greg):
        pm, pmn = PM
        od1, od2, od, sq, t = fin["od1"], fin["od2"], fin["od"], fin["sq"], fin["t"]
        P.add("dve", lambda e: e.scalar_tensor_tensor(out=od[:, 0:N], in0=od2[:, 0:N], scalar=lamw[:, 5:6], in1=od1[:, 0:N],
                                                      op0=ALU.mult, op1=ALU.add), R=["fin_od1", "fin_od2", "lamw"], W=["fin_od"])
        P.add("pool", lambda e: e.tensor_tensor(out=sq[:, 0:N], in0=od[:, 0:N], in1=od[:, 0:N], op=ALU.mult), R=["fin_od"], W=["fin_sq"])
        P.add("pe", lambda e: e.matmul(pm[:, 0:N], lhsT=onesdiv[:, 0:128], rhs=sq[:, 0:N], start=True, stop=True), R=["fin_sq", "cf32"], W=[pmn])
        P.add("act", lambda e: e.activation(out=t[:, 0:N], in_=pm[:, 0:N], func=AF.Ln, bias=1e-5), R=[pmn], W=["fin_t"])
        P.add("act", lambda e: e.activation(out=t[:, 0:N], in_=t[:, 0:N], func=AF.Exp, scale=-0.5), R=["fin_t"], W=["fin_t"])
        P.add("dve", lambda e: e.tensor_tensor(out=t[:, 0:N], in0=t[:, 0:N], in1=od[:, 0:N], op=ALU.mult), R=["fin_t", "fin_od"], W=["fin_t"])
        P.add("dve", lambda e: e.scalar_tensor_tensor(out=Gt[:, chunk, tc0:tc0 + N], in0=t[:, 0:N], scalar=gsub[:, 0:1], in1=Gt[:, chunk, tc0:tc0 + N],
                                                      op0=ALU.mult, op1=ALU.mult), R=["fin_t", "gsub", greg], W=[greg])

    def attn_qk(kT, kreg, nk, qlist, masks):
        sbank, sname = SB[actr["s"] % 3]
        actr["s"] += 1
        first = True
        wtot = 0
        for (qa, qreg, sc0, N) in qlist:
            P.add("pe", lambda e, qa=qa, sc0=sc0, N=N, first=first: e.matmul(sbank[0:nk, sc0:sc0 + N], lhsT=kT, rhs=qa, start=first, stop=False,
                                                                             skip_group_check=True), R=[kreg, qreg], W=[sname])
            first = False
            wtot = max(wtot, sc0 + N)
        for (ma, sc0) in masks:
            w = ma.shape[-1]
            P.add("pe", lambda e, ma=ma, sc0=sc0, w=w: e.matmul(sbank[0:nk, sc0:sc0 + w], lhsT=ident[0:nk, 0:nk], rhs=ma, start=False, stop=True,
                                                                skip_group_check=True), R=["mk", "ident"], W=[sname])
        pi = nxt("pt", 4)
        pt = Pt[pi]; ptn = f"Pt{pi}"
        P.add("act", lambda e: e.activation(out=pt[0:nk, 0:wtot], in_=sbank[0:nk, 0:wtot], func=AF.Exp), R=[sname], W=[ptn])
        return (pt, ptn)

    def attn_pv(vlist, st, nk):
        pt, ptn = st
        for (va, vreg, acc, accn, sc0, N) in vlist:
            P.add("pe", lambda e, va=va, acc=acc, sc0=sc0, N=N: e.matmul(acc, lhsT=va, rhs=pt[0:nk, sc0:sc0 + N], start=False, stop=True,
                                                                         skip_group_check=True), R=[vreg, ptn], W=[accn])

    def run_pipeline(items, LA=2):
        pend = []
        for it in items:
            for f in it.get("pre", ()):
                f()
            st = it["qk"]()
            pend.append((it, st))
            if len(pend) > LA:
                it0, st0 = pend.pop(0)
                for f in it0.get("prepv", ()):
                    f()
                it0["pv"](st0)
                for f in it0.get("post", ()):
                    f()
        while pend:
            it0, st0 = pend.pop(0)
            for f in it0.get("prepv", ()):
                f()
            it0["pv"](st0)
            for f in it0.get("post", ()):
                f()

    def prompt_attention(l):
        par = l % 2
        kcount = 0
        items = []

        def mk_loads_v(vp, vi):
            vreg = f"Vt{vi}"
            Vv = Vt[vi].rearrange("p (u c) -> p u c", u=VU, c=129)
            av = agout_v[par][vp]

            def f():
                for r_ in range(4):
                    src = av[r_ * NLOC:r_ * NLOC + NTOKP, :].rearrange("(j p) c -> p j c", p=128)
                    dma("sp", Vv[:, r_ * NSLOT:(r_ + 1) * NSLOT, :], src, [f"agout_v{par}_{vp}"], [vreg], vreg)
                dma("sp", Vv[0:NMETA, 4 * NSLOT, :], av[NTOKP:NLOC, :], [f"agout_v{par}_{vp}"], [vreg], vreg)
            return f

        def mk_loads_kq(vp, comp, ki, isdiff):
            kreg = f"Kt{ki}"; qreg = f"Qt{ki}"
            K3 = Kt[ki][:, 0:4 * NLOC].rearrange("p (r n) -> p r n", r=4, n=NLOC)

            def f():
                akc = agout_k[par][comp // 2].rearrange("(r k) n -> k r n", r=4, k=128)
                dma("sp", K3[0:64, :, :], akc[(comp % 2) * 64:(comp % 2) * 64 + 64, :, :], [f"agout_k{par}_{comp // 2}"], [kreg], kreg)
                if isdiff:
                    dma("sp", Kt[ki][64:70, 0:4 * NLOC], c_kauxd[vp, :, :], [], [kreg], kreg)
                    dma("sp", Qt[ki][64:70, 0:NLOC], c_qauxd[vp, :, 0:NLOC], [], [qreg], qreg)
                else:
                    h = comp - 8
                    dma("sp", Kt[ki][64:70, 0:4 * NLOC], kauxf[h, :, :], ["kauxf"], [kreg], kreg)
                    dma("sp", Qt[ki][64:70, 0:NLOC], qauxf[h, :, 0:NLOC], ["qauxf"], [qreg], qreg)
                dma("sp", Qt[ki][0:64, 0:NLOC], qT[comp * 64:(comp + 1) * 64, 0:NLOC], ["qT"], [qreg], qreg)
            return f

        def mk_block(kT, kreg, nk, qa, qreg, qn, masks, vl):
            return dict(qk=lambda: attn_qk(kT, kreg, nk, [(qa, qreg, 0, qn)], masks),
                        pv=lambda st: attn_pv(vl, st, nk))

        def mk_final(isdiff, ci, odd, X, Y, Xn, Yn, QN, chunk, q0, greg, g):
            def f():
                if isdiff:
                    finalize_diff_comp(X, Y, Xn, Yn, 0, QN, "od1" if ci == 0 else "od2")
                    if ci == 0:
                        P.add("pool", lambda e: e.tensor_copy(out=od1g[g][:, 0:QN], in_=fin["od1"][:, 0:QN]), R=["fin_od1"], W=[f"od1g{g}"])
                    else:
                        P.add("pool", lambda e: e.tensor_copy(out=fin["od1"][:, 0:QN], in_=od1g[g][:, 0:QN]), R=[f"od1g{g}"], W=["fin_od1"])
                        finalize_diff_head(QN, chunk, q0, greg)
                else:
                    finalize_fox(X, Xn, 0, QN, odd, chunk, q0, greg)
            return f

        for vp in range(8):
            vi = vp % 2
            vreg = f"Vt{vi}"
            Vv = Vt[vi].rearrange("p (u c) -> p u c", u=VU, c=129)
            isdiff = vp < 4
            comps = (2 * vp, 2 * vp + 1) if isdiff else (8 + 2 * (vp - 4), 8 + 2 * (vp - 4) + 1)
            first_of_vp = True
            for ci, comp in enumerate(comps):
                ki = kcount % 2
                kcount += 1
                kreg = f"Kt{ki}"; qreg = f"Qt{ki}"
                K3 = Kt[ki][:, 0:4 * NLOC].rearrange("p (r n) -> p r n", r=4, n=NLOC)
                h = vp if isdiff else comp - 8
                odd = (not isdiff) and (ci == 1)
                chunk = vp
                first_of_comp = True
                for g in range(NQG + 1):
                    X, Y, Xn, Yn = ACC[actr["acc"] % 2]
                    actr["acc"] += 1
                    meta_q = (g == NQG)
                    if meta_q:
                        q0, QN = NTOKP, NMETA
                    else:
                        q0, QN = g * GS * 128, GS * 128
                    pre = []
                    if first_of_vp:
                        pre.append(mk_loads_v(vp, vi))
                        first_of_vp = False
                    if first_of_comp:
                        pre.append(mk_loads_kq(vp, comp, ki, isdiff))
                        first_of_comp = False
                    prepv = [lambda X=X, Xn=Xn, QN=QN: zero_bank(X, Xn, QN)]
                    if isdiff:
                        prepv.append(lambda Y=Y, Yn=Yn, QN=QN: zero_bank(Y, Yn, QN))
                    blocks = [("meta", 0, NSLOT)]
                    if not meta_q:
                        for j in range((g + 1) * GS):
                            for r_ in range(4):
                                blocks.append(("blk", r_, j))
                    grp_items = []
                    for (kind, r_, j) in blocks:
                        if kind == "meta":
                            nk = NMETA
                            kT = K3[0:70, 0, NTOKP:NLOC]
                            vb = 4 * NSLOT
                            qs, qn = 0, QN
                            masks = []
                            if meta_q:
                                mo = 0 if not isdiff else 16 * (1 + h)
                                masks = [(mksm[0:NMETA, mo:mo + NMETA], 0)]
                        else:
                            nk = 128
                            kT = K3[0:70, r_, j * 128:(j + 1) * 128]
                            vb = r_ * NSLOT + j
                            i = j - g * GS
                            masks = []
                            if i < 0:
                                qs, qn = 0, QN
                            else:
                                qs, qn = i * 128, QN - i * 128
                                masks = [((mkfox[:, r_, :] if not isdiff else mkdiff[:, h * 4 + r_, :]), 0)]
                        qa = Qt[ki][0:70, q0 + qs:q0 + qs + qn]
                        if isdiff:
                            vl = [(Vv[0:nk, vb, 0:65], vreg, X[0:65, qs:qs + qn], Xn, 0, qn),
                                  (Vv[0:nk, vb, 1:129], vreg, Y[0:128, qs:qs + qn], Yn, 0, qn)]
                        elif not odd:
                            vl = [(Vv[0:nk, vb, 0:65], vreg, X[0:65, qs:qs + qn], Xn, 0, qn)]
                        else:
                            vl = [(Vv[0:nk, vb, 1:129], vreg, X[0:128, qs:qs + qn], Xn, 0, qn)]
                        grp_items.append(mk_block(kT, kreg, nk, qa, qreg, qn, masks, vl))
                    grp_items[0]["pre"] = pre
                    grp_items[0]["prepv"] = prepv
                    hr = NQG if meta_q else g
                    greg = f"G{chunk}_{hr}"
                    grp_items[-1]["post"] = [mk_final(isdiff, ci, odd, X, Y, Xn, Yn, QN, chunk, q0, greg, g)]
                    items.extend(grp_items)
        run_pipeline(items)

    diff_stash = {}
    od1g = [sb(f"od1g{g}", [128, 512 if g < NQG else NMETA], F32) for g in range(NQG + 1)]

    def sample_attention(l):
        K8 = Kall[:, 0:8 * LCN].rearrange("p (c n) -> p c n", c=8, n=LCN)
        NBs = NCT + 1
        Vs = Vall[:, 0:NBs * 4 * 129].rearrange("p (t u c) -> p t u c", t=NBs, u=4, c=129)
        kregs = ["Kt0", "Kt1"]; vregs = ["Vt0", "Vt1"]
        P.add("pool", lambda e: e.memset(Vs[:, :, :, 64:65], 1.0), W=vregs)
        Q8 = Qt[0][:, 0:8 * 32].rearrange("p (c n) -> p c n", c=8, n=32)
        items = []

        def mk_loads(s, hf):
            ck = cdk if hf == 0 else cfk
            cv = cdv if hf == 0 else cfv

            def f():
                for t in range(NCT):
                    n = min(128, LC - t * 128)
                    ci_ = nxt("cst", 4)
                    dma("sp", cstf[ci_][0:n, :], ck[l, s, t * 128:t * 128 + n, :], [], [f"cstf{ci_}"], f"cstf{ci_}")
                    P.add("act", lambda e, ci_=ci_, n=n: e.activation(out=cstb[ci_][0:n, :], in_=cstf[ci_][0:n, :], func=AF.Copy),
                          R=[f"cstf{ci_}"], W=[f"cstb{ci_}"])
                    pm, pmn = ([PM] + SB)[t % 4]
                    tp = pm[:, :].bitcast(BF16)
                    for c in range(8):
                        P.add("pe", lambda e, c=c, ci_=ci_, n=n, tp=tp: e.transpose(out=tp[0:64, c * 128:c * 128 + n], in_=cstb[ci_][0:n, c * 64:(c + 1) * 64],
                                                                                    identity=ident[0:n, 0:n]), R=[f"cstb{ci_}", "ident"], W=[pmn])
                    tpv = tp.rearrange("p (c n) -> p c n", c=8, n=128)
                    P.add("dve", lambda e, t=t, n=n, tpv=tpv: e.tensor_copy(out=K8[0:64, :, t * 128:t * 128 + n], in_=tpv[0:64, :, 0:n]), R=[pmn], W=kregs)
                for c in range(8):
                    comp = hf * 8 + c
                    dma("sp", K8[0:64, c, LC:LCN], kTs[comp * 64:(comp + 1) * 64, 32 * s:32 * s + 32], ["kTs"], kregs, "Kt0")
                    if hf == 0:
                        dma("sp", K8[64:70, c, :], c_kauxds[c // 2, :, :], [], kregs, "Kt0")
                        dma("sp", Q8[64:70, c, :], c_qauxd[c // 2, :, NLOC + 32 * s:NLOC + 32 * s + 32], [], ["Qt0"], "Qt0")
                    else:
                        dma("sp", K8[64:70, c, :], kauxfs[s, c, :, :], ["kauxfs"], kregs, "Kt0")
                        dma("sp", Q8[64:70, c, :], qauxf[c, :, NLOC + 32 * s:NLOC + 32 * s + 32], ["qauxf"], ["Qt0"], "Qt0")
                    dma("sp", Q8[0:64, c, :], qT[comp * 64:(comp + 1) * 64, NLOC + 32 * s:NLOC + 32 * s + 32], ["qT"], ["Qt0"], "Qt0")
                for t in range(NCT):
                    n = min(128, LC - t * 128)
                    ci_ = nxt("cst", 4)
                    dma("sp", cstf[ci_][0:n, :], cv[l, s, t * 128:t * 128 + n, :], [], [f"cstf{ci_}"], f"cstf{ci_}")
                    cvw = cstf[ci_].rearrange("p (u t c) -> p u t c", u=4, t=2, c=64)
                    P.add("pool", lambda e, t=t, n=n, cvw=cvw: e.tensor_copy(out=Vs[0:n, t, :, 0:64], in_=cvw[0:n, :, 0, :]), R=[f"cstf{ci_}"], W=vregs)
                    P.add("dve", lambda e, t=t, n=n, cvw=cvw: e.tensor_copy(out=Vs[0:n, t, :, 65:129], in_=cvw[0:n, :, 1, :]), R=[f"cstf{ci_}"], W=vregs)
                dma("sp", Vs[0:32, NCT, :, :], vsn[32 * s:32 * s + 32, hf * 4 * 129:(hf + 1) * 4 * 129].rearrange("p (u c) -> p u c", u=4, c=129),
                    ["vsn"], vregs, "Vt0")
            return f

        def mk_sblock(hf, t, X, Y, Xn, Yn):
            if t < NCT:
                nk = min(128, LC - t * 128)
                k0 = t * 128
            else:
                nk = 32
                k0 = LC

            def qk():
                sbk, sn = SB[actr["s"] % 3]
                actr["s"] += 1
                for c in range(8):
                    P.add("pe", lambda e, c=c: e.matmul(sbk[0:nk, c * 32:(c + 1) * 32], lhsT=K8[0:70, c, k0:k0 + nk], rhs=Q8[0:70, c, :],
                                                       start=(c == 0), stop=False, skip_group_check=True), R=kregs + ["Qt0"], W=[sn])
                if t == NCT:
                    for c in range(8):
                        mo = 80 + (0 if hf == 1 else 32 * (1 + c // 2))
                        P.add("pe", lambda e, c=c, mo=mo: e.matmul(sbk[0:nk, c * 32:(c + 1) * 32], lhsT=ident[0:nk, 0:nk], rhs=mksm[0:32, mo:mo + 32],
                                                                   start=False, stop=True, skip_group_check=True), R=["mk", "ident"], W=[sn])
                pi = nxt("pt", 4)
                pt = Pt[pi]; ptn = f"Pt{pi}"
                P.add("act", lambda e: e.activation(out=pt[0:nk, 0:256], in_=sbk[0:nk, 0:256], func=AF.Exp), R=[sn], W=[ptn])
                return (pt, ptn)

            def pv(st):
                pt, ptn = st
                for c in range(8):
                    u = c // 2
                    cs = slice(c * 32, (c + 1) * 32)
                    if hf == 0:
                        P.add("pe", lambda e, u=u, cs=cs: e.matmul(X[0:65, cs], lhsT=Vs[0:nk, t, u, 0:65], rhs=pt[0:nk, cs], start=False, stop=True,
                                                                   skip_group_check=True), R=vregs + [ptn], W=[Xn])
                        P.add("pe", lambda e, u=u, cs=cs: e.matmul(Y[0:128, cs], lhsT=Vs[0:nk, t, u, 1:129], rhs=pt[0:nk, cs], start=False, stop=True,
                                                                   skip_group_check=True), R=vregs + [ptn], W=[Yn])
                    elif c % 2 == 0:
                        P.add("pe", lambda e, u=u, cs=cs: e.matmul(X[0:65, cs], lhsT=Vs[0:nk, t, u, 0:65], rhs=pt[0:nk, cs], start=False, stop=True,
                                                                   skip_group_check=True), R=vregs + [ptn], W=[Xn])
                    else:
                        P.add("pe", lambda e, u=u, cs=cs: e.matmul(X[0:128, cs], lhsT=Vs[0:nk, t, u, 1:129], rhs=pt[0:nk, cs], start=False, stop=True,
                                                                   skip_group_check=True), R=vregs + [ptn], W=[Xn])
            return dict(qk=qk, pv=pv)

        def mk_sfinal(s, hf, X, Y, Xn, Yn):
            tc0 = NLOC + 32 * s
            pm, pmn = PM
            lrow, rb, tmp = fin["lrow"], fin["rb"], fin["tmp"]
            od1, od, sq, t_ = fin["od1"], fin["od"], fin["sq"], fin["t"]

            def v4(ap, n=4):
                return ap.rearrange("p (c n) -> p c n", c=n, n=32)

            def f():
                if hf == 1:
                    Xv = X[:, 0:256].rearrange("p (c two n) -> p c two n", c=4, two=2, n=32)
                    gregs = [f"G{4 + i}_{NQG}" for i in range(4)]
                    for par_ in range(2):
                        xv = Xv[:, :, par_, :]
                        if par_ == 0:
                            P.add("dve", lambda e, xv=xv: e.tensor_copy(out=v4(lrow[64:65, 0:128]), in_=xv[64:65]), R=[Xn], W=["fin_lrow"])
                            P.add("pe", lambda e: e.matmul(pm[0:64, 0:128], lhsT=ones_f[64:65, 0:64], rhs=lrow[64:65, 0:128], start=True, stop=True),
                                  R=["fin_lrow", "ones_f"], W=[pmn])
                            lo, hi = 0, 64
                        else:
                            P.add("dve", lambda e, xv=xv: e.tensor_copy(out=v4(lrow[32:64, 0:128]), in_=xv[32:64]), R=[Xn], W=["fin_lrow"])
                            P.add("pe", lambda e: e.matmul(pm[0:128, 0:128], lhsT=sel63[32:64, 0:128], rhs=lrow[32:64, 0:128], start=True, stop=True),
                                  R=["fin_lrow", "cf32"], W=[pmn])
                            lo, hi = 64, 128
                        P.add("dve", lambda e, lo=lo, hi=hi: e.reciprocal(out=rb[lo:hi, 0:128], in_=pm[lo:hi, 0:128]), R=[pmn], W=["fin_rb"])
                        P.add("dve", lambda e, lo=lo, hi=hi, xv=xv: e.tensor_tensor(out=v4(tmp[lo:hi, 0:128]), in0=xv[lo:hi], in1=v4(rb[lo:hi, 0:128]), op=ALU.mult),
                              R=[Xn, "fin_rb"], W=["fin_tmp"])
                        P.add("dve", lambda e, lo=lo, hi=hi: e.tensor_tensor(out=Gt[lo:hi, 4:8, tc0:tc0 + 32], in0=v4(tmp[lo:hi, 0:128]),
                                                                             in1=Gt[lo:hi, 4:8, tc0:tc0 + 32], op=ALU.mult),
                              R=["fin_tmp"] + gregs, W=gregs)
                else:
                    gregs = [f"G{i}_{NQG}" for i in range(4)]
                    P.add("dve", lambda e: e.tensor_copy(out=lrow[64:65, 0:256], in_=X[64:65, 0:256]), R=[Xn], W=["fin_lrow"])
                    P.add("pe", lambda e: e.matmul(pm[0:128, 0:256], lhsT=ones_f[64:65, 0:128], rhs=lrow[64:65, 0:256], start=True, stop=True),
                          R=["fin_lrow", "ones_f"], W=[pmn])
                    P.add("dve", lambda e: e.reciprocal(out=rb[:, 0:256], in_=pm[:, 0:256]), R=[pmn], W=["fin_rb"])
                    P.add("dve", lambda e: e.tensor_tensor(out=od1[0:64, 0:256], in0=X[0:64, 0:256], in1=rb[0:64, 0:256], op=ALU.mult),
                          R=[Xn, "fin_rb"], W=["fin_od1"])
                    P.add("dve", lambda e: e.tensor_tensor(out=od1[64:128, 0:256], in0=Y[64:128, 0:256], in1=rb[64:128, 0:256], op=ALU.mult),
                          R=[Yn, "fin_rb"], W=["fin_od1"])
                    o4 = od1[:, 0:256].rearrange("p (c two n) -> p c two n", c=4, two=2, n=32)
                    P.add("dve", lambda e: e.scalar_tensor_tensor(out=v4(od[:, 0:128]), in0=o4[:, :, 1, :], scalar=lamw[:, 5:6], in1=o4[:, :, 0, :],
                                                                  op0=ALU.mult, op1=ALU.add), R=["fin_od1", "lamw"], W=["fin_od"])
                    P.add("pool", lambda e: e.tensor_tensor(out=sq[:, 0:128], in0=od[:, 0:128], in1=od[:, 0:128], op=ALU.mult), R=["fin_od"], W=["fin_sq"])
                    P.add("pe", lambda e: e.matmul(pm[:, 0:128], lhsT=onesdiv[:, 0:128], rhs=sq[:, 0:128], start=True, stop=True), R=["fin_sq", "cf32"], W=[pmn])
                    P.add("act", lambda e: e.activation(out=t_[:, 0:128], in_=pm[:, 0:128], func=AF.Ln, bias=1e-5), R=[pmn], W=["fin_t"])
                    P.add("act", lambda e: e.activation(out=t_[:, 0:128], in_=t_[:, 0:128], func=AF.Exp, scale=-0.5), R=["fin_t"], W=["fin_t"])
                    P.add("dve", lambda e: e.tensor_tensor(out=t_[:, 0:128], in0=t_[:, 0:128], in1=od[:, 0:128], op=ALU.mult), R=["fin_t", "fin_od"], W=["fin_t"])
                    P.add("dve", lambda e: e.scalar_tensor_tensor(out=Gt[:, 0:4, tc0:tc0 + 32], in0=v4(t_[:, 0:128]), scalar=gsub[:, 0:1],
                                                                  in1=Gt[:, 0:4, tc0:tc0 + 32], op0=ALU.mult, op1=ALU.mult),
                          R=["fin_t", "gsub"] + gregs, W=gregs)
            return f

        for s in range(4):
            for hf in range(2):
                X, Y, Xn, Yn = ACC[actr["acc"] % 2]
                actr["acc"] += 1
                grp = [mk_sblock(hf, t, X, Y, Xn, Yn) for t in range(NCT + 1)]
                grp[0]["pre"] = [mk_loads(s, hf)]
                prepv = [lambda X=X, Xn=Xn: zero_bank(X, Xn, 256)]
                if hf == 0:
                    prepv.append(lambda Y=Y, Yn=Yn: zero_bank(Y, Yn, 256))
                grp[0]["prepv"] = prepv
                grp[-1]["post"] = [mk_sfinal(s, hf, X, Y, Xn, Yn)]
                run_pipeline(grp)
        for i in range(2):
            vv = Vt[i].rearrange("p (u c) -> p u c", u=VU, c=129)
            P.add("pool", lambda e, vv=vv: e.memset(vv[:, :, 64:65], 1.0), W=[f"Vt{i}"])

    def out_proj(l):
        dma("pool", wout[:, :, :], w_out[l].rearrange("(kc p) n -> p kc n", p=128), [], ["wout"], "wout")
        last = (l == DEPTH - 1)
        load_layer_consts(l + 1)
        for (row0, n, hr) in tiles():
            xi = nxt("xt", 2)
            xreg = f"xt{xi}"
            dma("sp", xt[xi][0:n, :], x_src(l, row0, n), ["xscr"], [xreg], xreg)
            for half in range(2):
                bi = nxt("ps", 8)
                pb = PS[bi]; pbn = f"ps{bi}"
                for kc in range(8):
                    P.add("pe", lambda e, kc=kc, pb=pb, row0=row0, n=n, half=half: e.matmul(
                        pb[0:n, 0:512], lhsT=Gt[:, kc, row0:row0 + n], rhs=wout[:, kc, half * 512:(half + 1) * 512],
                        start=(kc == 0), stop=(kc == 7)), R=["wout", f"G{kc}_{hr}"], W=[pbn])
                P.add("dve", lambda e, pb=pb, xi=xi, n=n, half=half: e.tensor_tensor(out=xt[xi][0:n, half * 512:(half + 1) * 512],
                                                                                    in0=pb[0:n, 0:512], in1=xt[xi][0:n, half * 512:(half + 1) * 512], op=ALU.add),
                      R=[pbn, xreg], W=[xreg])
            if not last:
                dma("sp", xscr[row0:row0 + n, :], xt[xi][0:n, :], [xreg], ["xscr"], xreg)
                norm_tile(xt[xi], n, xreg, row0, f"hT{hr}", gbc, "gbc")
            else:
                if row0 >= NTOKP and row0 < NLOC:
                    continue
                P.add("dve", lambda e, xi=xi, n=n: e.scalar_tensor_tensor(out=junk[0:n, :], in0=xt[xi][0:n, :], scalar=1.0, in1=xt[xi][0:n, :],
                                                                          op0=ALU.mult, op1=ALU.mult, accum_out=sml[0:n, 0:1]), R=[xreg], W=["junk", "sml0"])
                P.add("act", lambda e, n=n: e.activation(out=sml[0:n, 1:2], in_=sml[0:n, 0:1], func=AF.Ln, scale=1.0 / D, bias=1e-6), R=["sml0"], W=["sml1"])
                P.add("act", lambda e, n=n: e.activation(out=sml[0:n, 2:3], in_=sml[0:n, 1:2], func=AF.Exp, scale=-0.5), R=["sml1"], W=["sml2"])
                P.add("dve", lambda e, xi=xi, n=n: e.scalar_tensor_tensor(out=xt[xi][0:n, :], in0=xt[xi][0:n, :], scalar=sml[0:n, 2:3], in1=gbc[0:n, :],
                                                                          op0=ALU.mult, op1=ALU.mult), R=[xreg, "sml2", "gbc"], W=[xreg])
                dst = yp[row0:row0 + n, :] if row0 < NTOKP else ys[0:n, :]
                dma("sp", dst, xt[xi][0:n, :], [xreg], ["yout"], xreg)

    import os as _os
    _stop = _os.environ.get("K_STOP", "")
    for l in range(DEPTH):
        P.epoch = l
        projections(l)
        if _stop == "proj":
            break
        allgather(l)
        if _stop == "ag":
            break
        lam_consts(l)
        fox_F(l)
        if _stop == "F":
            break
        prompt_attention(l)
        if _stop == "patt":
            break
        sample_attention(l)
        if _stop == "satt":
            break
        out_proj(l)
    P.finish()

    sems = {}
    for k in P.cnt:
        sems[k] = es.enter_context(nc.semaphore("s_" + "_".join(str(x) for x in k)))

    def emit(ename, e):
        for (fn, waits, key) in P.ops[ename]:
            for (k, i) in waits:
                e.wait_ge(sems[k], i * (16 if k[0] == "dma" else 1))
            if fn is None:
                continue
            ins = fn(e)
            if key[0] == "dma":
                ins.then_inc(sems[key], 16)
            elif key[0] == "cc":
                ins.then_inc(sems[key])
            else:
                ins.then_inc(sems[key], 1)

    with nc.Block() as block:
        @block.tensor
        def _(e):
            emit("pe", e)

        @block.scalar
        def _(e):
            emit("act", e)

        @block.vector
        def _(e):
            emit("dve", e)

        @block.gpsimd
        def _(e):
            emit("pool", e)

        @block.sync
        def _(e):
            emit("sp", e)
    es.close()
    return nc


def _bf(a):
    return np.asarray(a, dtype=np.float32).astype(ml_dtypes.bfloat16)


def _host_tables(cfg, r):
    NSLOT = cfg["NSLOT"]; LC = cfg["LC"]; PAST = cfg["PAST"]
    NTOKP = NSLOT * 128; NLOC = NTOKP + NMETA; NTOT = NLOC + 128; LCN = LC + 32

    def pos_local(rank):
        pa = np.zeros(NLOC); pb = np.zeros(NLOC)
        for j in range(NSLOT):
            B = 4 * j + rank
            pa[j * 128:(j + 1) * 128] = 128 * B
            pb[j * 128:(j + 1) * 128] = 16 + np.arange(128)
        pa[NTOKP:] = 0
        pb[NTOKP:] = np.arange(NMETA)
        return pa, pb

    pa_q = np.zeros(NTOT); pb_q = np.zeros(NTOT)
    pa_q[:NLOC], pb_q[:NLOC] = pos_local(r)
    for s in range(4):
        pa_q[NLOC + 32 * s:NLOC + 32 * s + 32] = PAST
        pb_q[NLOC + 32 * s:NLOC + 32 * s + 32] = 16 + np.arange(32)
    pa_k = np.concatenate([pos_local(rr)[0] for rr in range(4)])
    pb_k = np.concatenate([pos_local(rr)[1] for rr in range(4)])
    idx = np.arange(LCN)
    pa_s = np.where(idx < LC, 128 * (idx // 128), PAST).astype(np.float64)
    pb_s = np.where(idx < LC, idx % 128, 16 + (idx - LC)).astype(np.float64)
    qauxd = np.zeros((4, 6, NTOT), np.float32); kauxd = np.zeros((4, 6, 4 * NLOC), np.float32); kauxds = np.zeros((4, 6, LCN), np.float32)
    for h in range(4):
        s_ = SLOPES[h]
        qauxd[h, 0] = -s_ * pa_q; qauxd[h, 1] = -s_ * pb_q; qauxd[h, 2] = 1; qauxd[h, 3] = 1
        kauxd[h, 0] = 1; kauxd[h, 1] = 1; kauxd[h, 2] = s_ * pa_k; kauxd[h, 3] = s_ * pb_k
        kauxds[h, 0] = 1; kauxds[h, 1] = 1; kauxds[h, 2] = s_ * pa_s; kauxds[h, 3] = s_ * pb_s
    k = np.arange(128)[:, None]; q = np.arange(128)[None, :]
    mkfox = np.zeros((128, 4, 128), np.float32); mkdiff = np.zeros((128, 16, 128), np.float32)
    for rr in range(4):
        if rr > r:
            mkfox[:, rr, :] = NEGM
        elif rr == r:
            mkfox[:, rr, :] = np.where(k > q, NEGM, 0.0)
        for h in range(4):
            if rr > r:
                mkdiff[:, h * 4 + rr, :] = NEGM
            elif rr == r:
                m = np.where((k // 64) > (q // 64), NEGM, np.where(k > q, -2.0 * SLOPES[h] * (k - q), 0.0))
                mkdiff[:, h * 4 + rr, :] = m
    mksm = np.zeros((32, 240), np.float32)
    k16 = np.arange(16)[:, None]; q16 = np.arange(16)[None, :]
    mksm[0:16, 0:16] = np.where(k16 > q16, NEGM, 0.0)
    k32 = np.arange(32)[:, None]; q32 = np.arange(32)[None, :]
    mksm[0:32, 80:112] = np.where(k32 > q32, NEGM, 0.0)
    for h in range(4):
        mksm[0:16, 16 * (1 + h):16 * (2 + h)] = np.where(k16 > q16, -2.0 * SLOPES[h] * (k16 - q16), 0.0)
        mksm[0:32, 80 + 32 * (1 + h):80 + 32 * (2 + h)] = np.where(k32 > q32, -2.0 * SLOPES[h] * (k32 - q32), 0.0)
    cf = np.zeros((128, 648), np.float32)
    cf[:, 0:128] = np.eye(128)
    cf[63, 128 + 64:128 + 128] = 1.0
    cf[:, 256:384] = 1.0 / 128.0
    for h in range(8):
        cf[32 * r + h, 640 + h] = 1.0
        for r2 in range(4):
            for r1 in range(4):
                cf[32 * r2 + h, 384 + 32 * r1 + h] = 1.0
                if r2 < r1:
                    cf[32 * r2 + h, 512 + 32 * r1 + h] = 1.0
    return dict(c_ident=_bf(np.eye(128)), c_f32=cf, c_mkfox=_bf(mkfox.reshape(128, -1)), c_mkdiff=_bf(mkdiff.reshape(128, -1)),
                c_mksm=_bf(mksm), c_qauxd=_bf(qauxd), c_kauxd=_bf(kauxd), c_kauxds=_bf(kauxds),
                c_ones=_bf(np.ones((8, 3, max(NLOC, LCN)), np.float32)))


_NC_CACHE = {}


def kernel(x_prompt, x_sample, cache_diff_k, cache_diff_v, cache_fox_k, cache_fox_v, cache_fox_logf,
           meta_tokens, w_in, b_forget, norm_g, w_out, lambda_q1, lambda_k1, lambda_q2, lambda_k2,
           subln_g, final_norm_g):
    f32 = np.float32
    x_prompt = np.asarray(x_prompt, f32); x_sample = np.asarray(x_sample, f32)
    B, SEQ, _ = x_prompt.shape
    DEPTH = w_in.shape[0]
    LC = cache_diff_k.shape[2]
    assert B == 2 and x_sample.shape[0] == 32 and x_sample.shape[1] == 32
    cfg = _cfg_from_shapes(SEQ, LC - NMETA, DEPTH)
    NSLOT = cfg["NSLOT"]
    NTOKP = NSLOT * 128; NLOC = NTOKP + NMETA; NTOT = NLOC + 128
    key = (SEQ, LC, DEPTH)
    if key not in _NC_CACHE:
        _NC_CACHE[key] = build_nc(cfg)
    nc = _NC_CACHE[key]
    in_maps = []
    xpb = x_prompt.reshape(B, NSLOT, 4, 128, D)
    for c in range(8):
        b, r = c // 4, c % 4
        m = dict(
            xp=np.ascontiguousarray(xpb[b, :, r]).reshape(NTOKP, D),
            xm=np.ascontiguousarray(np.asarray(meta_tokens, f32)),
            xs=np.ascontiguousarray(x_sample[4 * c:4 * c + 4]).reshape(128, D),
            cdk=np.ascontiguousarray(np.asarray(cache_diff_k, f32)[:, 4 * c:4 * c + 4]).reshape(DEPTH, 4, LC, 512),
            cdv=np.ascontiguousarray(np.asarray(cache_diff_v, f32)[:, 4 * c:4 * c + 4]).reshape(DEPTH, 4, LC, 512),
            cfk=np.ascontiguousarray(np.asarray(cache_fox_k, f32)[:, 4 * c:4 * c + 4]).reshape(DEPTH, 4, LC, 512),
            cfv=np.ascontiguousarray(np.asarray(cache_fox_v, f32)[:, 4 * c:4 * c + 4]).reshape(DEPTH, 4, LC, 512),
            clf=np.ascontiguousarray(np.asarray(cache_fox_logf, f32)[:, 4 * c:4 * c + 4]),
            w_in=np.asarray(w_in, f32), w_out=np.asarray(w_out, f32), b_forget=np.asarray(b_forget, f32),
            norm_g=np.asarray(norm_g, f32), lq1=np.asarray(lambda_q1, f32), lk1=np.asarray(lambda_k1, f32),
            lq2=np.asarray(lambda_q2, f32), lk2=np.asarray(lambda_k2, f32), subln_g=np.asarray(subln_g, f32),
            fng=np.asarray(final_norm_g, f32).reshape(1, D),
        )
        m.update(_host_tables(cfg, r))
        in_maps.append(m)
    res = run_bass_kernel_spmd(nc, in_maps, core_ids=list(range(8)))
    R = res.results
    y_prompt = np.zeros((B, SEQ, D), f32)
    y_sample = np.zeros((32, 32, D), f32)
    T = NMETA + SEQ
    outs_p = {n: np.zeros((DEPTH, B, T, w), f32) for n, w in (("o_dk", 512), ("o_dv", 512), ("o_fk", 512), ("o_fv", 512), ("o_lf", 8))}
    outs_s = {n: np.zeros((DEPTH, 32, 32, w), f32) for n, w in (("o_dk", 512), ("o_dv", 512), ("o_fk", 512), ("o_fv", 512), ("o_lf", 8))}
    for c in range(8):
        b, r = c // 4, c % 4
        y_prompt.reshape(B, NSLOT, 4, 128, D)[b, :, r] = np.asarray(R[c]["yp"]).reshape(NSLOT, 128, D)
        y_sample[4 * c:4 * c + 4] = np.asarray(R[c]["ys"]).reshape(4, 32, D)
        for n in outs_p:
            a = np.asarray(R[c][n])
            w = a.shape[-1]
            outs_p[n][:, b, NMETA:].reshape(DEPTH, NSLOT, 4, 128, w)[:, :, r] = a[:, 0:NTOKP].reshape(DEPTH, NSLOT, 128, w)
            if r == 0:
                outs_p[n][:, b, 0:NMETA] = a[:, NTOKP:NLOC]
            outs_s[n][:, 4 * c:4 * c + 4] = a[:, NLOC:NTOT].reshape(DEPTH, 4, 32, w)
    return (y_prompt, y_sample,
            outs_p["o_dk"].reshape(DEPTH, B, T, 4, 2, 64), outs_p["o_dv"].reshape(DEPTH, B, T, 4, 128),
            outs_p["o_fk"].reshape(DEPTH, B, T, 8, 64), outs_p["o_fv"].reshape(DEPTH, B, T, 8, 64), outs_p["o_lf"],
            outs_s["o_dk"].reshape(DEPTH, 32, 32, 4, 2, 64), outs_s["o_dv"].reshape(DEPTH, 32, 32, 4, 128),
            outs_s["o_fk"].reshape(DEPTH, 32, 32, 8, 64), outs_s["o_fv"].reshape(DEPTH, 32, 32, 8, 64), outs_s["o_lf"])
```
